# Optimizing a Trainium2 kernel written in Bass

```python
import math
import jax, jax.numpy as jnp
from jax import lax
import numpy as np

D_MODEL = 1024
BATCH = 2
SEQ = 8192
DEPTH = 1
DEC_BATCH = 128
DEC_SEQ = 1
PAST_LEN = 2048
PAGE_SIZE = 128

H_A = 4
DK_A = 64
DV_A = 2 * DK_A
H_R = 4
DK_R = 64
DV_R = 128
RET_CHUNK = 128
N_MEM = 256
H_M = 4
DH_M = D_MODEL // H_M
D_FF = 4 * D_MODEL
NUM_BUCKETS = 32
MAX_DISTANCE = 128
Q_BLOCK = 128
EPS = 1e-6
ROPE_BASE = 10000.0
WA_QK = H_A * 2 * DK_A
WA_V = H_A * DV_A
WR_QK = H_R * DK_R
WR_V = H_R * DV_R
D_IN = 2 * WA_QK + WA_V + 2 * WR_QK + 2 * WR_V
D_MIX = H_A * DV_A + H_R * DV_R

kernel_name = 'hybrid_diffattn_retention_decoder_step'


def rmsnorm(x, g):
    xf = x.astype(jnp.float32)
    y = xf * lax.rsqrt(jnp.mean(xf * xf, axis=-1, keepdims=True) + EPS)
    return (y * g.astype(jnp.float32)).astype(x.dtype)


def head_rms(x):
    return x * lax.rsqrt(jnp.mean(x * x, axis=-1, keepdims=True) + EPS)


def t5_bucket(rel):
    n = jnp.maximum(rel, 0)
    max_exact = NUM_BUCKETS // 2
    nf = jnp.maximum(n, 1).astype(jnp.float32)
    large = max_exact + (jnp.log(nf / max_exact) / math.log(MAX_DISTANCE / max_exact)
                         * (NUM_BUCKETS - max_exact)).astype(jnp.int32)
    large = jnp.minimum(large, NUM_BUCKETS - 1)
    return jnp.where(n < max_exact, n, large)


def project(h, w_in):
    B, T = h.shape[0], h.shape[1]
    z = h @ w_in
    sizes = [WA_QK, WA_QK, WA_V, WR_QK, WR_QK, WR_V, WR_V]
    offs = np.cumsum(sizes)[:-1].tolist()
    qa, ka, va, qr, kr, vr, gr = jnp.split(z, offs, axis=-1)
    return (qa.reshape(B, T, H_A, 2 * DK_A), ka.reshape(B, T, H_A, 2 * DK_A),
            va.reshape(B, T, H_A, DV_A), qr.reshape(B, T, H_R, DK_R),
            kr.reshape(B, T, H_R, DK_R), vr.reshape(B, T, H_R, DV_R), gr)


def diff_attention(q, k, v, q_pos, k_pos, rel_bias, lam):
    f32 = jnp.float32
    scale = DK_A ** -0.5
    q1, q2 = q[..., :DK_A], q[..., DK_A:]
    k1, k2 = k[..., :DK_A], k[..., DK_A:]
    rel = q_pos[:, None] - k_pos[None, :]
    bias = jnp.transpose(rel_bias[t5_bucket(rel)], (2, 0, 1))[None].astype(f32)
    mask = (rel >= 0)[None, None]
    neg = jnp.finfo(f32).min

    def probs(qi, ki):
        s = jnp.einsum('bqhd,bkhd->bhqk', qi, ki).astype(f32) * scale + bias
        return jax.nn.softmax(jnp.where(mask, s, neg), axis=-1)

    attn = probs(q1, k1) - lam * probs(q2, k2)
    return jnp.einsum('bhqk,bkhd->bqhd', attn, v.astype(f32))


def diff_attention_prompt(qa, ka, va, rel_bias, lam):
    B, T = qa.shape[0], qa.shape[1]
    nb = T // Q_BLOCK
    k_pos = jnp.arange(T)

    def block(i):
        qb = lax.dynamic_slice_in_dim(qa, i * Q_BLOCK, Q_BLOCK, axis=1)
        q_pos = i * Q_BLOCK + jnp.arange(Q_BLOCK)
        return diff_attention(qb, ka, va, q_pos, k_pos, rel_bias, lam)

    ob = lax.map(block, jnp.arange(nb))
    return jnp.moveaxis(ob, 0, 1).reshape(B, T, H_A, DV_A)


def rotary(x, pos):
    half = x.shape[-1] // 2
    inv_freq = ROPE_BASE ** (-jnp.linspace(0.0, 1.0, half, dtype=jnp.float32))
    ang = pos.astype(jnp.float32)[:, None] * inv_freq[None, :]
    cos = jnp.cos(ang)[:, None, :]
    sin = jnp.sin(ang)[:, None, :]
    xf = x.astype(jnp.float32)
    x1, x2 = xf[..., :half], xf[..., half:]
    return jnp.concatenate([x1 * cos - x2 * sin, x2 * cos + x1 * sin], axis=-1)


def retention_log_decay():
    return jnp.log(1.0 - 2.0 ** (-5.0 - jnp.arange(H_R, dtype=jnp.float32)))


def retention_chunk(S, q, k, v, lg):
    C = q.shape[1]
    i = jnp.arange(C, dtype=jnp.float32)
    d = i[:, None] - i[None, :]
    decay = jnp.where(d >= 0, jnp.exp(d[None] * lg[:, None, None]), 0.0)
    scores = jnp.einsum('bihd,bjhd->bhij', q, k) * decay[None]
    o = jnp.einsum('bhij,bjhe->bihe', scores, v)
    o = o + jnp.einsum('bihd,bhde->bihe', q, S) * jnp.exp((i + 1.0)[:, None] * lg[None, :])[None, :, :, None]
    k_dec = k * jnp.exp((C - 1.0 - i)[:, None] * lg[None, :])[None, :, :, None]
    S_new = jnp.exp(C * lg)[None, :, None, None] * S + jnp.einsum('bjhd,bjhe->bhde', k_dec, v)
    return S_new, o


def retention_prompt(q, k, v, lg):
    B, T = q.shape[0], q.shape[1]
    nc = T // RET_CHUNK

    def chunks(a):
        return jnp.moveaxis(a.reshape(B, nc, RET_CHUNK, a.shape[2], a.shape[3]), 1, 0)

    S0 = jnp.zeros((B, H_R, DK_R, DV_R), jnp.float32)
    S, oc = lax.scan(lambda s, xs: retention_chunk(s, xs[0], xs[1], xs[2], lg),
                     S0, (chunks(q), chunks(k), chunks(v)))
    return S, jnp.moveaxis(oc, 0, 1).reshape(B, T, H_R, DV_R)


def merge_groups(oa, orr, gr, subln, lam_init):
    B, T = oa.shape[0], oa.shape[1]
    a = head_rms(oa) * subln.astype(jnp.float32) * (1.0 - lam_init)
    r = head_rms(orr).reshape(B, T, H_R * DV_R) * jax.nn.silu(gr.astype(jnp.float32))
    return jnp.concatenate([a.reshape(B, T, H_A * DV_A), r], axis=-1)


def mem_kv(mem, g, w_mk, w_mv):
    B, M = mem.shape[0], mem.shape[1]
    h = rmsnorm(mem, g)
    return (h @ w_mk).reshape(B, M, H_M, DH_M), (h @ w_mv).reshape(B, M, H_M, DH_M)


def mem_attend(h, mk, mv, w_mq, w_mo):
    B, T = h.shape[0], h.shape[1]
    q = (h @ w_mq).reshape(B, T, H_M, DH_M)
    s = jnp.einsum('bthd,bmhd->bhtm', q, mk.astype(q.dtype)).astype(jnp.float32) * DH_M ** -0.5
    p = jax.nn.softmax(s, axis=-1)
    o = jnp.einsum('bhtm,bmhd->bthd', p, mv.astype(jnp.float32)).reshape(B, T, D_MODEL)
    return o.astype(h.dtype) @ w_mo


def sq_relu_mlp(h, w_up, w_down):
    return jnp.square(jax.nn.relu(h @ w_up)) @ w_down


def setup_inputs(seed: int = 0) -> dict:
    key = jax.random.key(seed)
    ks = jax.random.split(key, 32)
    f32 = jnp.float32
    n_pages = PAST_LEN // PAGE_SIZE
    n_used = DEC_BATCH * n_pages
    n_phys = n_used + max(n_used // 4, 1)

    def nrm(k, shape, s=1.0):
        return jax.random.normal(k, shape, f32) * s

    def gain(k, shape):
        return 1.0 + 0.05 * jax.random.normal(k, shape, f32)

    page_table = jax.random.permutation(ks[5], n_phys)[:n_used].reshape(DEC_BATCH, n_pages).astype(jnp.int32)
    return {
        'x_prompt': nrm(ks[0], (BATCH, SEQ, D_MODEL)),
        'x_sample': nrm(ks[1], (DEC_BATCH, DEC_SEQ, D_MODEL)),
        'mem_prompt': nrm(ks[2], (BATCH, N_MEM, D_MODEL)),
        'cache_k': nrm(ks[3], (DEPTH, n_phys, PAGE_SIZE, H_A, 2 * DK_A)),
        'cache_v': nrm(ks[4], (DEPTH, n_phys, PAGE_SIZE, H_A, DV_A)),
        'page_table': page_table,
        'state_ret': nrm(ks[6], (DEPTH, DEC_BATCH, H_R, DK_R, DV_R), 0.5),
        'cache_mem_k': nrm(ks[7], (DEPTH, DEC_BATCH, N_MEM, H_M, DH_M)),
        'cache_mem_v': nrm(ks[8], (DEPTH, DEC_BATCH, N_MEM, H_M, DH_M)),
        'rel_bias': nrm(ks[9], (NUM_BUCKETS, H_A), 0.5),
        'norm_mix': gain(ks[10], (DEPTH, D_MODEL)),
        'w_in': nrm(ks[11], (DEPTH, D_MODEL, D_IN), D_MODEL ** -0.5),
        'lambda_q1': nrm(ks[12], (DEPTH, DK_A), 0.1),
        'lambda_k1': nrm(ks[13], (DEPTH, DK_A), 0.1),
        'lambda_q2': nrm(ks[14], (DEPTH, DK_A), 0.1),
        'lambda_k2': nrm(ks[15], (DEPTH, DK_A), 0.1),
        'subln_a': gain(ks[16], (DEPTH, DV_A)),
        'w_out': nrm(ks[17], (DEPTH, D_MIX, D_MODEL), D_MIX ** -0.5),
        'norm_mem_q': gain(ks[18], (DEPTH, D_MODEL)),
        'norm_mem_kv': gain(ks[19], (DEPTH, D_MODEL)),
        'w_mq': nrm(ks[20], (DEPTH, D_MODEL, D_MODEL), D_MODEL ** -0.5),
        'w_mk': nrm(ks[21], (DEPTH, D_MODEL, D_MODEL), D_MODEL ** -0.5),
        'w_mv': nrm(ks[22], (DEPTH, D_MODEL, D_MODEL), D_MODEL ** -0.5),
        'w_mo': nrm(ks[23], (DEPTH, D_MODEL, D_MODEL), D_MODEL ** -0.5),
        'norm_mlp': gain(ks[24], (DEPTH, D_MODEL)),
        'w_up': nrm(ks[25], (DEPTH, D_MODEL, D_FF), D_MODEL ** -0.5),
        'w_down': nrm(ks[26], (DEPTH, D_FF, D_MODEL), D_FF ** -0.5),
        'norm_final': gain(ks[27], (D_MODEL,)),
    }


def reference(x_prompt, x_sample, mem_prompt, cache_k, cache_v, page_table, state_ret,
              cache_mem_k, cache_mem_v, rel_bias, norm_mix, w_in, lambda_q1, lambda_k1,
              lambda_q2, lambda_k2, subln_a, w_out, norm_mem_q, norm_mem_kv, w_mq, w_mk,
              w_mv, w_mo, norm_mlp, w_up, w_down, norm_final):
    f32 = jnp.float32
    B, T = x_prompt.shape[0], x_prompt.shape[1]
    Bd, Td = x_sample.shape[0], x_sample.shape[1]
    past = page_table.shape[1] * PAGE_SIZE
    pos_p = jnp.arange(T)
    pos_d = past + jnp.arange(Td)
    kpos_d = jnp.arange(past + Td)
    lg = retention_log_decay()
    kr_scale = DK_R ** -0.5

    xp, xd = x_prompt, x_sample
    kp_l, vp_l, sp_l, mkp_l, mvp_l, kd_l, vd_l, sd_l = [], [], [], [], [], [], [], []
    for l in range(DEPTH):
        lam_init = 0.8 - 0.6 * math.exp(-0.3 * l)
        lam = (jnp.exp(jnp.sum(lambda_q1[l].astype(f32) * lambda_k1[l].astype(f32)))
               - jnp.exp(jnp.sum(lambda_q2[l].astype(f32) * lambda_k2[l].astype(f32))) + lam_init)

        hp = rmsnorm(xp, norm_mix[l])
        qa, ka, va, qr, kr, vr, gr = project(hp, w_in[l])
        oa = diff_attention_prompt(qa, ka, va, rel_bias, lam)
        s_p, o_r = retention_prompt(rotary(qr, pos_p), rotary(kr, pos_p) * kr_scale, vr.astype(f32), lg)
        xp = xp + merge_groups(oa, o_r, gr, subln_a[l], lam_init).astype(xp.dtype) @ w_out[l]
        mk_p, mv_p = mem_kv(mem_prompt, norm_mem_kv[l], w_mk[l], w_mv[l])
        xp = xp + mem_attend(rmsnorm(xp, norm_mem_q[l]), mk_p, mv_p, w_mq[l], w_mo[l])
        xp = xp + sq_relu_mlp(rmsnorm(xp, norm_mlp[l]), w_up[l], w_down[l])
        kp_l.append(ka.reshape(B, T // PAGE_SIZE, PAGE_SIZE, H_A, 2 * DK_A))
        vp_l.append(va.reshape(B, T // PAGE_SIZE, PAGE_SIZE, H_A, DV_A))
        sp_l.append(s_p)
        mkp_l.append(mk_p)
        mvp_l.append(mv_p)

        hd = rmsnorm(xd, norm_mix[l])
        qa_d, ka_d, va_d, qr_d, kr_d, vr_d, gr_d = project(hd, w_in[l])
        k_past = cache_k[l][page_table].reshape(Bd, past, H_A, 2 * DK_A)
        v_past = cache_v[l][page_table].reshape(Bd, past, H_A, DV_A)
        k_all = jnp.concatenate([k_past.astype(ka_d.dtype), ka_d], axis=1)
        v_all = jnp.concatenate([v_past.astype(va_d.dtype), va_d], axis=1)
        oa_d = diff_attention(qa_d, k_all, v_all, pos_d, kpos_d, rel_bias, lam)
        s_d, o_rd = retention_chunk(state_ret[l].astype(f32), rotary(qr_d, pos_d),
                                    rotary(kr_d, pos_d) * kr_scale, vr_d.astype(f32), lg)
        xd = xd + merge_groups(oa_d, o_rd, gr_d, subln_a[l], lam_init).astype(xd.dtype) @ w_out[l]
        xd = xd + mem_attend(rmsnorm(xd, norm_mem_q[l]), cache_mem_k[l], cache_mem_v[l], w_mq[l], w_mo[l])
        xd = xd + sq_relu_mlp(rmsnorm(xd, norm_mlp[l]), w_up[l], w_down[l])
        kd_l.append(ka_d)
        vd_l.append(va_d)
        sd_l.append(s_d)

    y_prompt = rmsnorm(xp, norm_final)
    y_sample = rmsnorm(xd, norm_final)
    k_prompt = jnp.stack(kp_l)
    v_prompt = jnp.stack(vp_l)
    state_ret_prompt = jnp.stack(sp_l)
    mem_k_prompt = jnp.stack(mkp_l)
    mem_v_prompt = jnp.stack(mvp_l)
    k_sample = jnp.stack(kd_l)
    v_sample = jnp.stack(vd_l)
    state_ret_sample = jnp.stack(sd_l)
    return (y_prompt, y_sample, k_prompt, v_prompt, state_ret_prompt, mem_k_prompt, mem_v_prompt, k_sample, v_sample, state_ret_sample)
```

```python
import math
from contextlib import ExitStack

import numpy as np
import ml_dtypes

import concourse.bass as bass
import concourse.mybir as mybir
from concourse.bass_utils import run_bass_kernel_spmd

F32 = mybir.dt.float32
BF16 = mybir.dt.bfloat16
I32 = mybir.dt.int32
AF = mybir.ActivationFunctionType
ALU = mybir.AluOpType
AX = mybir.AxisListType

NCORES = 8
D = 1024
T = 8192
NB = T // 128
NOWN = 16
EPS = 1e-6
NEG = -30000.0
DEC = 128
PAST = 2048
NPG = PAST // 128
ENGS = ["sp", "act", "dve", "pe", "pool"]


class Prog:
    def __init__(self, nc):
        self.nc = nc
        self.ops = []
        self.last_writer = {}
        self.readers = {}

    def add(self, eng, fn, reads=(), writes=(), slot=None, inc=None):
        def _ps(r):
            return isinstance(r, str) and len(r) >= 3 and r.startswith("ps") and r[2].isdigit()
        writes = list(writes) + [r[:3] for r in reads if _ps(r)]
        writes = [r[:3] if _ps(r) else r for r in writes]
        reads = [r for r in reads if not _ps(r)]
        oid = len(self.ops)
        deps = set()
        for r in reads:
            w = self.last_writer.get(r)
            if w is not None:
                deps.add(w)
        for r in writes:
            w = self.last_writer.get(r)
            if w is not None:
                deps.add(w)
            for rd in self.readers.get(r, {}).values():
                deps.add(rd)
        dma = slot is not None
        rkey = ("dma", slot) if dma else eng
        for r in reads:
            self.readers.setdefault(r, {})[rkey] = oid
        for r in writes:
            self.last_writer[r] = oid
            self.readers[r] = {}
        deps.discard(oid)
        self.ops.append(dict(eng=eng, fn=fn, deps=deps, dma=dma, slot=slot, rw=(list(reads), list(writes)),
                             inc=(inc if inc is not None else (16 if dma else 1))))
        return oid

    def barrier(self):
        allres = list(self.last_writer.keys() | self.readers.keys())
        marks = []
        for e in ("act", "dve", "pool"):
            marks.append(self.add(e, self._bar_fn[e], writes=allres + ["bar_" + e]))
        for e in ENGS:
            self.add(e, None, reads=["bar_act", "bar_dve", "bar_pool"], writes=allres + ["barx_" + e])

    def emit(self, es, limit=None):
        nc = self.nc
        if limit is not None:
            self.ops = self.ops[:limit]
            self.ops.append(dict(eng="sp", fn=None, deps=set(i for i, o in enumerate(self.ops) if o["dma"] and o["fn"] is not None),
                                 dma=False, slot=None, inc=1))
        ops = self.ops
        sig = [False] * len(ops)

        def pruned(D_, o):
            return (D_["eng"] == "pe" and o["eng"] == "pe" and not D_["dma"] and not o["dma"])

        eff = []
        for o in ops:
            s_ = set()
            for d in o["deps"]:
                if ops[d]["fn"] is None:
                    s_ |= eff[d]
                else:
                    s_.add(d)
            eff.append(s_)
        for i, o in enumerate(ops):
            if o["dma"] and o["fn"] is not None:
                sig[i] = True
            for d in eff[i]:
                if pruned(ops[d], o):
                    continue
                sig[d] = True
        cnt = {}
        for i, o in enumerate(ops):
            if not sig[i]:
                o["sv"] = None
                continue
            key = ("dma", o["slot"]) if o["dma"] else o["eng"]
            cnt[key] = cnt.get(key, 0) + o["inc"]
            o["key"] = key
            o["sv"] = cnt[key]
        known = {e: {} for e in ENGS}
        nwaits = 0
        for i, o in enumerate(ops):
            kn = known[o["eng"]]
            need = {}
            for d in eff[i]:
                D_ = ops[d]
                if pruned(D_, o):
                    continue
                k, v = D_["key"], D_["sv"]
                if kn.get(k, 0) >= v:
                    continue
                if need.get(k, (0, None))[0] < v:
                    need[k] = (v, d)
            waits = []
            for k, (v, d) in need.items():
                if kn.get(k, 0) >= v:
                    continue
                waits.append((k, v))
                for kk, vv in ops[d]["vc"].items():
                    if kn.get(kk, 0) < vv:
                        kn[kk] = vv
                if kn.get(k, 0) < v:
                    kn[k] = v
            o["waits"] = waits
            nwaits += len(waits)
            vc = dict(kn)
            if sig[i]:
                vc[o["key"]] = max(vc.get(o["key"], 0), o["sv"])
            o["vc"] = vc
        sems = {}
        for n, k in enumerate(sorted(cnt.keys(), key=str)):
            sems[k] = es.enter_context(nc.semaphore("s%d" % n))
        self.stats = dict(nops=len(ops), nwaits=nwaits, nsems=len(sems))
        block = es.enter_context(nc.Block())

        def run(engname):
            def body(e):
                for o in ops:
                    if o["eng"] != engname:
                        continue
                    for (k, v) in o["waits"]:
                        e.wait_ge(sems[k], v)
                    if o["fn"] is not None:
                        ins = o["fn"](e)
                        if o["sv"] is not None:
                            ins.then_inc(sems[o["key"]], o["inc"])
            return body

        block.sync(run("sp"))
        block.scalar(run("act"))
        block.vector(run("dve"))
        block.tensor(run("pe"))
        block.gpsimd(run("pool"))


class Arena:
    def __init__(self, t, size):
        self.t = t
        self.size = size
        self.off = 0
        self.peak = 0

    def alloc(self, cols, dtype=BF16):
        n = cols * (2 if dtype in (F32, I32) else 1)
        self.off = (self.off + 15) // 16 * 16
        assert self.off + n <= self.size, ("SBUF arena overflow", self.off, n, self.size)
        ap = self.t[:, self.off:self.off + n]
        self.off += n
        self.peak = max(self.peak, self.off)
        if dtype != BF16:
            ap = ap.bitcast(dtype)
        return ap


def build_program(nbA=NB, do_cc=True, do_C=True, nblkC=NOWN, do_attn=True, do_front=True, limit=None, n_phys=2560, do_B=True):
    nc = bass.Bass("TRN2", target_bir_lowering=False)

    def din(name, shape, dt=F32):
        return nc.dram_tensor(name, list(shape), dt, kind="ExternalInput").ap()

    def dout(name, shape, dt=F32):
        return nc.dram_tensor(name, list(shape), dt, kind="ExternalOutput").ap()

    xb = din("xb", [T, D])
    wA = din("wA", [D, 768])
    gmix = din("gmix", [128, 8])
    rope = din("rope", [NB, 128, 64])
    hconst = din("hconst", [128, 8])
    btin = din("bt", [128, 256])
    maskT_d = din("maskT", [128, 128])
    ident_d = din("ident", [128, 128])
    lamv_d = din("lamv", [1, 256])
    subln_d = din("subln", [1, 128])
    xown = din("xown", [NOWN * 128, D])
    idx_d = din("idx", [128, NOWN * 4], I32)
    wout_d = din("wout", [D, D])
    wmq_d = din("wmq", [D, D])
    wmk_d = din("wmk", [D, D])
    wmv_d = din("wmv", [D, D])
    wmo_d = din("wmo", [D, D])
    wup_d = din("wup", [D, 4 * D])
    wdn_d = din("wdn", [4 * D, D])
    gains_d = din("gains", [128, 24])
    nfin_d = din("nfin", [1, D])
    memp_d = din("memp", [256, D])
    xd_d = din("xd", [16, D])
    win_d = din("win", [D, 3072])
    ck_d = din("ck", [n_phys * 128, 512])
    cv_d = din("cv", [n_phys * 128, 512])
    pt_d = din("pt", [1, 256], I32)
    sst_d = din("sst", [16, 4, 64, 128])
    cmk_d = din("cmk", [16, 256, D])
    cmv_d = din("cmv", [16, 256, D])
    bconst_d = din("bconst", [128, 72])
    e16_d = din("e16", [64, 256])
    r84_d = din("r84", [8, 8])
    roped_d = din("roped", [16, 64])
    ys_o = dout("ys", [16, D])
    ks_o = dout("ks", [16, 512])
    vs_o = dout("vs", [16, 512])
    ss_o = dout("ss", [16, 4, 64, 128])
    yown = dout("yown", [NOWN * 128, D])
    kout = dout("kout", [T, 128])
    vout = dout("vout", [T, 128])
    sret = dout("sret", [64, 128])
    memk_o = dout("memk", [256, D])
    memv_o = dout("memv", [256, D])
    Ex = nc.dram_tensor("Ex", [T, 256], BF16)
    Gx = nc.dram_tensor("Gx", [4 * T, 256], BF16)

    es = ExitStack()
    ARENA_COLS = 106000
    arena_t = es.enter_context(nc.sbuf_tensor("arena", [128, ARENA_COLS], BF16))
    PS = es.enter_context(nc.psum_tensor("ps", [128, 8, 512], F32))
    A = Arena(arena_t, ARENA_COLS)
    P = Prog(nc)

    def psf(b, lo=0, hi=512, p0=0, p1=128):
        return PS[p0:p1, b, lo:hi]

    def psb(b):
        return PS[:, b, :].bitcast(BF16)

    ident = A.alloc(128, BF16)
    hc = A.alloc(8, F32)
    epsb = A.alloc(1, F32)
    barscr = A.alloc(8, F32)
    P._bar_fn = {
        "act": lambda e: e.activation(out=barscr[:, 0:1], in_=barscr[:, 1:2], func=AF.Copy),
        "dve": lambda e: e.memset(barscr[:, 2:3], 0.0),
        "pool": lambda e: e.memset(barscr[:, 4:5], 0.0),
    }
    identf = A.alloc(128, F32)
    P.add("sp", lambda e: e.dma_start(out=identf, in_=ident_d), writes=["identf"], slot="ident")
    P.add("dve", lambda e: e.tensor_copy(out=ident, in_=identf), reads=["identf"], writes=["ident"])
    P.add("sp", lambda e: e.dma_start(out=hc, in_=hconst), writes=["hc"], slot="hc")
    P.add("dve", lambda e: e.memset(epsb, EPS), writes=["epsb"])
    P.add("dve", lambda e: e.memset(barscr, 0.0), writes=["barscr"])
    XS = A.alloc(D, F32)
    MD = A.alloc(D, BF16)
    E16 = A.alloc(256, F32)
    r84 = A.alloc(8, F32)
    onesf = A.alloc(2, F32)
    selq = A.alloc(128, F32)
    hid_s = A.alloc(16, F32)
    OD4 = A.alloc(D, F32)
    Wsel4 = A.alloc(16, F32)
    P.add("sp", lambda e: e.dma_start(out=E16[0:64], in_=e16_d), writes=["E16"], slot="E16")
    P.add("sp", lambda e: e.dma_start(out=r84[0:8], in_=r84_d), writes=["r84"], slot="r84")
    P.add("pool", lambda e: e.memset(onesf, 1.0), writes=["onesf"])
    P.add("pool", lambda e: e.memset(XS, 0.0), writes=["XS"])
    P.add("pool", lambda e: e.memset(MD, 0.0), writes=["MD"])
    P.add("sp", lambda e: e.dma_start(out=XS[0:16], in_=xd_d), writes=["XS"], slot="XS")
    phase_base = A.off

    def rstd_ops(src_ap, src_res, n, junk, junk_res, ssq, lnv, rstd, tag):
        P.add("act", lambda e: e.activation(out=junk, in_=src_ap, func=AF.Square, accum_out=ssq),
              reads=[src_res], writes=[junk_res, tag + "ssq"])
        P.add("act", lambda e: e.activation(out=lnv, in_=ssq, func=AF.Ln, bias=epsb, scale=1.0 / n),
              reads=[tag + "ssq", "epsb"], writes=[tag + "lnv"])
        P.add("act", lambda e: e.activation(out=rstd, in_=lnv, func=AF.Exp, scale=-0.5),
              reads=[tag + "lnv"], writes=[tag + "rstd"])

    def load_weight(dst, dst_res, src, K, N, stage, gain=None, gain_res=None, eng="pool"):
        srcv = src.rearrange("(k p) n -> p k n", p=128)
        for k0 in range(0, K, 8):
            for n0 in range(0, N, 256):
                nn = min(256, N - n0)
                kk = min(8, K - k0)
                stv = stage[:, 0:kk * nn].rearrange("p (k n) -> p k n", k=kk)
                P.add("sp", lambda e, stv=stv, k0=k0, kk=kk, n0=n0, nn=nn: e.dma_start(
                    out=stv, in_=srcv[:, k0:k0 + kk, n0:n0 + nn]), writes=["wstage"], slot="wstage")
                if gain is None:
                    P.add(eng, lambda e, stv=stv, k0=k0, kk=kk, n0=n0, nn=nn: e.tensor_copy(
                        out=dst[:, k0:k0 + kk, n0:n0 + nn], in_=stv), reads=["wstage"], writes=[dst_res])
                else:
                    for k in range(kk):
                        P.add(eng, lambda e, stv=stv, k=k, k0=k0, n0=n0, nn=nn: e.tensor_scalar(
                            out=dst[:, k0 + k, n0:n0 + nn], in0=stv[:, k, :],
                            scalar1=gain[:, k0 + k:k0 + k + 1], scalar2=1.0, op0=ALU.mult, op1=ALU.mult),
                            reads=["wstage", gain_res], writes=[dst_res])

    wstage = A.alloc(8 * 256, F32)
    WA = A.alloc(8 * 768, BF16).rearrange("p (k n) -> p k n", k=8)
    gm = A.alloc(8, F32)
    QK = A.alloc(2 * T, BF16).rearrange("p (m t) -> p m t", m=2)
    VX = A.alloc(NB * 130, BF16).rearrange("p (t c) -> p t c", t=NB)
    XB = [A.alloc(D, F32) for _ in range(2)]
    RP = [A.alloc(64, F32) for _ in range(2)]
    sqj = A.alloc(D, BF16)
    xn = A.alloc(D, BF16)
    hpT = A.alloc(D, BF16).rearrange("p (k t) -> p k t", k=8)
    KV = [A.alloc(256, F32) for _ in range(2)]
    qkbf = A.alloc(256, BF16)
    vrbf = A.alloc(128, BF16)
    sg1 = A.alloc(128, F32)
    sg = A.alloc(128, F32)
    qkr = A.alloc(128, F32).rearrange("p (m d) -> p m d", m=2)
    rt = [A.alloc(64, F32).rearrange("p (m d) -> p m d", m=2) for _ in range(4)]
    rot = A.alloc(128, F32).rearrange("p (m d) -> p m d", m=2)
    qkp = A.alloc(128, BF16).rearrange("p (m d) -> p m d", m=2)
    qkT = A.alloc(256, BF16).rearrange("p (m t) -> p m t", m=2)
    ptr = A.alloc(128, BF16)
    Sst = A.alloc(128, F32)
    Stt = A.alloc(128, F32)
    Sbf = A.alloc(128, BF16)
    sm = A.alloc(16, F32)
    bt = A.alloc(256, F32).rearrange("p (m t) -> p m t", m=2)
    maskT = A.alloc(128, F32)
    lamb = A.alloc(256, F32)
    lamt = A.alloc(128, F32)
    lam8 = A.alloc(8, F32)
    sub8 = A.alloc(128, F32)
    PT = [[A.alloc(512, BF16) for _ in range(2)] for _ in range(2)]
    tmpn = [[A.alloc(128, F32) for _ in range(2)] for _ in range(2)]
    oa = A.alloc(128, F32)
    oa2 = A.alloc(128, F32)
    MRG = [A.alloc(256, BF16) for _ in range(2)]

    P.add("sp", lambda e: e.dma_start(out=gm, in_=gmix), writes=["gm"], slot="gm")
    load_weight(WA, "WA", wA, 8, 768, wstage, gain=gm, gain_res="gm", eng="dve")
    P.add("sp", lambda e: e.dma_start(out=bt.rearrange("p m t -> p (m t)"), in_=btin), writes=["bt"], slot="bt")
    P.add("sp", lambda e: e.dma_start(out=maskT, in_=maskT_d), writes=["maskT"], slot="maskT")
    P.add("sp", lambda e: e.dma_start(out=lamb, in_=lamv_d.partition_broadcast(128)), writes=["lamb"], slot="lamb")
    P.add("sp", lambda e: e.dma_start(out=sub8, in_=subln_d.partition_broadcast(128)), writes=["sub8"], slot="sub8")
    P.add("dve", lambda e: e.tensor_scalar(out=sub8, in0=sub8, scalar1=0.8, scalar2=None, op0=ALU.mult),
          reads=["sub8"], writes=["sub8"])
    P.add("dve", lambda e: e.tensor_tensor(out=lamt[:, 0:64], in0=lamb[:, 0:64], in1=lamb[:, 64:128], op=ALU.mult),
          reads=["lamb"], writes=["lamt"])
    P.add("dve", lambda e: e.tensor_tensor(out=lamt[:, 64:128], in0=lamb[:, 128:192], in1=lamb[:, 192:256], op=ALU.mult),
          reads=["lamb"], writes=["lamt"])
    P.add("dve", lambda e: e.tensor_reduce(out=lam8[:, 0:2], in_=lamt.rearrange("p (a b) -> p a b", a=2),
                                           axis=AX.X, op=ALU.add), reads=["lamt"], writes=["lam8"])
    P.add("act", lambda e: e.activation(out=lam8[:, 2:4], in_=lam8[:, 0:2], func=AF.Exp), reads=["lam8"], writes=["lam8"])
    P.add("dve", lambda e: e.tensor_tensor(out=lam8[:, 4:5], in0=lam8[:, 3:4], in1=lam8[:, 2:3], op=ALU.subtract),
          reads=["lam8"], writes=["lam8"])
    P.add("dve", lambda e: e.tensor_scalar(out=lam8[:, 5:6], in0=lam8[:, 4:5], scalar1=-0.2, scalar2=None, op0=ALU.add),
          reads=["lam8"], writes=["lam8"])
    neglam = lam8[:, 5:6]
    P.add("pool", lambda e: e.memset(VX[:, :, 128:130], 1.0), writes=["VXones"])
    P.add("dve", lambda e: e.memset(Sst, 0.0), writes=["S"])
    P.add("dve", lambda e: e.memset(Sbf, 0.0), writes=["Sbf"])

    qsc, ksc, gC, cfar = hc[:, 0:1], hc[:, 1:2], hc[:, 2:3], hc[:, 3:4]

    def attention(t, mi):
        nk = t + 1
        groups = [list(range(g0, min(g0 + 4, nk))) for g0 in range(0, nk, 4)]
        o1 = psf(7, 0, 129)
        o2 = psf(4, 256, 385)
        sbanks = [(5, 6), (0, 1)]

        def qk(gi):
            b1, b2 = sbanks[gi % 2]
            for j, kb in enumerate(groups[gi]):
                P.add("pe", lambda e, j=j, kb=kb, b1=b1: e.matmul(
                    psf(b1, j * 128, (j + 1) * 128), lhsT=QK[0:64, 1, kb * 128:(kb + 1) * 128],
                    rhs=QK[0:64, 0, t * 128:(t + 1) * 128], start=True, stop=True),
                    reads=[("QK", kb), ("QK", t)], writes=["ps%d" % b1])
                P.add("pe", lambda e, j=j, kb=kb, b2=b2: e.matmul(
                    psf(b2, j * 128, (j + 1) * 128), lhsT=QK[64:128, 1, kb * 128:(kb + 1) * 128],
                    rhs=QK[64:128, 0, t * 128:(t + 1) * 128], start=True, stop=True),
                    reads=[("QK", kb), ("QK", t)], writes=["ps%d" % b2])

        def soft(gi):
            bb = sbanks[gi % 2]
            buf = gi % 2
            kbs = groups[gi]
            nf = sum(1 for kb in kbs if kb <= t - 2)
            for m in range(2):
                b = bb[m]
                pt = PT[m][buf]
                ptres = "pt%d%d" % (m, buf)
                if nf > 0:
                    P.add("act", lambda e, b=b, pt=pt, nf=nf: e.activation(
                        out=pt[:, 0:nf * 128], in_=psf(b, 0, nf * 128), func=AF.Exp, bias=cfar, scale=0.125),
                        reads=["ps%d" % b, "hc"], writes=[ptres])
                for j, kb in enumerate(kbs):
                    if kb <= t - 2:
                        continue
                    w = kb - (t - 1)
                    tm = tmpn[m][w]
                    tres = "tmpn%d%d" % (m, w)
                    P.add("dve", lambda e, b=b, j=j, w=w, tm=tm: e.scalar_tensor_tensor(
                        out=tm, in0=psf(b, j * 128, (j + 1) * 128), scalar=0.125, in1=bt[:, w, :],
                        op0=ALU.mult, op1=ALU.add), reads=["ps%d" % b, "bt"], writes=[tres])
                    P.add("act", lambda e, j=j, tm=tm, pt=pt: e.activation(
                        out=pt[:, j * 128:(j + 1) * 128], in_=tm, func=AF.Exp), reads=[tres], writes=[ptres])

        def pv(gi):
            buf = gi % 2
            for j, kb in enumerate(groups[gi]):
                P.add("pe", lambda e, j=j, kb=kb, buf=buf: e.matmul(
                    o1, lhsT=PT[0][buf][:, j * 128:(j + 1) * 128], rhs=VX[:, kb, 0:129],
                    start=(kb == 0), stop=(kb == t)),
                    reads=["pt0%d" % buf, ("V", kb), "VXones"], writes=["ps7"])
                P.add("pe", lambda e, j=j, kb=kb, buf=buf: e.matmul(
                    o2, lhsT=PT[1][buf][:, j * 128:(j + 1) * 128], rhs=VX[:, kb, 0:129],
                    start=(kb == 0), stop=(kb == t)),
                    reads=["pt1%d" % buf, ("V", kb), "VXones"], writes=["ps4b"])

        ng = len(groups)
        for gi in range(ng):
            qk(gi)
            soft(gi)
            if gi > 0:
                pv(gi - 1)
        pv(ng - 1)
        r1, r2 = sm[:, 4:5], sm[:, 5:6]
        P.add("dve", lambda e: e.reciprocal(out=r1, in_=psf(7, 128, 129)), reads=["ps7"], writes=["r1"])
        P.add("dve", lambda e: e.reciprocal(out=r2, in_=psf(4, 384, 385)), reads=["ps4b"], writes=["r2"])
        P.add("dve", lambda e: e.tensor_tensor(out=r2, in0=r2, in1=neglam, op=ALU.mult), reads=["r2", "lam8"], writes=["r2"])
        P.add("dve", lambda e: e.tensor_scalar(out=oa, in0=psf(7, 0, 128), scalar1=r1, scalar2=None, op0=ALU.mult),
              reads=["ps7", "r1"], writes=["oa"])
        P.add("dve", lambda e: e.scalar_tensor_tensor(out=oa2, in0=psf(4, 256, 384), scalar=r2, in1=oa,
                                                      op0=ALU.mult, op1=ALU.add),
              reads=["ps4b", "r2", "oa"], writes=["oa2"])
        rstd_ops(oa2, "oa2", 128, sqj[:, 0:128], "sqj", sm[:, 6:7], sm[:, 7:8], sm[:, 8:9], "a_")
        P.add("dve", lambda e: e.scalar_tensor_tensor(out=MRG[mi][:, 0:128], in0=oa2, scalar=sm[:, 8:9], in1=sub8,
                                                      op0=ALU.mult, op1=ALU.mult),
              reads=["oa2", "a_rstd", "sub8"], writes=["mrg%d" % mi])

    def block_front(t):
        i = t % 2
        x = XB[i]
        xr = "x%d" % i
        P.add("sp", lambda e: e.dma_start(out=x, in_=xb[t * 128:(t + 1) * 128, :]), writes=[xr], slot=xr)
        P.add("sp", lambda e: e.dma_start(out=RP[i], in_=rope[t]), writes=["rp%d" % i], slot="rp%d" % i)
        rstd_ops(x, xr, D, sqj, "sqj", sm[:, 0:1], sm[:, 1:2], sm[:, 2:3], "x_")
        P.add("dve", lambda e: e.tensor_scalar(out=xn, in0=x, scalar1=sm[:, 2:3], scalar2=None, op0=ALU.mult),
              reads=[xr, "x_rstd"], writes=["xn"])
        for k in range(8):
            P.add("pe", lambda e, k=k: e.transpose(out=psb(0)[:, k * 128:(k + 1) * 128], in_=xn[:, k * 128:(k + 1) * 128],
                                                   identity=ident), reads=["xn", "ident"], writes=["ps0"])
        P.add("act", lambda e: e.activation(out=hpT.rearrange("p k t -> p (k t)"), in_=psb(0), func=AF.Copy),
              reads=["ps0"], writes=["hpT"])
        for k in range(8):
            P.add("pe", lambda e, k=k: e.matmul(psf(1), lhsT=hpT[:, k, :], rhs=WA[:, k, 0:512], start=(k == 0), stop=(k == 7)),
                  reads=["hpT", "WA"], writes=["ps1"])
        for k in range(8):
            P.add("pe", lambda e, k=k: e.matmul(psf(2, 0, 256), lhsT=hpT[:, k, :], rhs=WA[:, k, 512:768], start=(k == 0), stop=(k == 7)),
                  reads=["hpT", "WA"], writes=["ps2"])
        kv = KV[i]
        P.add("dve", lambda e: e.tensor_copy(out=kv, in_=psf(1, 128, 384)), reads=["ps1"], writes=["kv%d" % i])
        P.add("sp", lambda e: e.dma_start(out=kout[t * 128:(t + 1) * 128, :], in_=kv[:, 0:128]),
              reads=["kv%d" % i], writes=[("kout", t)], slot="kv%d" % i)
        P.add("sp", lambda e: e.dma_start(out=vout[t * 128:(t + 1) * 128, :], in_=kv[:, 128:256]),
              reads=["kv%d" % i], writes=[("vout", t)], slot="kv%d" % i)
        P.add("dve", lambda e: e.tensor_copy(out=qkbf, in_=psf(1, 0, 256)), reads=["ps1"], writes=["qkbf"])
        P.add("act", lambda e: e.activation(out=vrbf, in_=psf(1, 384, 512), func=AF.Copy), reads=["ps1"], writes=["vrbf"])
        P.add("act", lambda e: e.activation(out=VX[:, t, 0:128], in_=psf(1, 256, 384), func=AF.Copy),
              reads=["ps1"], writes=[("V", t)])
        for m in range(2):
            P.add("pe", lambda e, m=m: e.transpose(out=psb(3)[:, m * 128:(m + 1) * 128], in_=qkbf[:, m * 128:(m + 1) * 128],
                                                   identity=ident), reads=["qkbf", "ident"], writes=["ps3A"])
        P.add("dve", lambda e: e.tensor_copy(out=QK[:, :, t * 128:(t + 1) * 128],
                                             in_=psb(3)[:, 0:256].rearrange("p (m t) -> p m t", m=2)),
              reads=["ps3A"], writes=[("QK", t)])
        P.add("act", lambda e: e.activation(out=sg1, in_=psf(2, 0, 128), func=AF.Exp, scale=-1.0), reads=["ps2"], writes=["sg1"])
        P.add("dve", lambda e: e.tensor_scalar(out=sg1, in0=sg1, scalar1=1.0, scalar2=None, op0=ALU.add), reads=["sg1"], writes=["sg1"])
        P.add("dve", lambda e: e.reciprocal(out=sg1, in_=sg1), reads=["sg1"], writes=["sg1"])
        P.add("dve", lambda e: e.tensor_tensor(out=sg, in0=psf(2, 0, 128), in1=sg1, op=ALU.mult), reads=["ps2", "sg1"], writes=["sg"])
        P.add("dve", lambda e: e.tensor_copy(out=qkr.rearrange("p m d -> p (m d)"), in_=psf(2, 128, 256)), reads=["ps2"], writes=["qkr"])
        cosb = RP[i][:, 0:32].unsqueeze(1).to_broadcast([128, 2, 32])
        sinb = RP[i][:, 32:64].unsqueeze(1).to_broadcast([128, 2, 32])
        x1, x2 = qkr[:, :, 0:32], qkr[:, :, 32:64]
        rpr = "rp%d" % i
        P.add("pool", lambda e: e.tensor_tensor(out=rt[0], in0=x1, in1=cosb, op=ALU.mult), reads=["qkr", rpr], writes=["rt0"])
        P.add("pool", lambda e: e.tensor_tensor(out=rt[1], in0=x2, in1=sinb, op=ALU.mult), reads=["qkr", rpr], writes=["rt1"])
        P.add("pool", lambda e: e.tensor_tensor(out=rt[2], in0=x2, in1=cosb, op=ALU.mult), reads=["qkr", rpr], writes=["rt2"])
        P.add("pool", lambda e: e.tensor_tensor(out=rt[3], in0=x1, in1=sinb, op=ALU.mult), reads=["qkr", rpr], writes=["rt3"])
        P.add("pool", lambda e: e.tensor_tensor(out=rot[:, :, 0:32], in0=rt[0], in1=rt[1], op=ALU.subtract),
              reads=["rt0", "rt1"], writes=["rot"])
        P.add("pool", lambda e: e.tensor_tensor(out=rot[:, :, 32:64], in0=rt[2], in1=rt[3], op=ALU.add),
              reads=["rt2", "rt3"], writes=["rot"])
        P.add("dve", lambda e: e.tensor_scalar(out=qkp[:, 0, :], in0=rot[:, 0, :], scalar1=qsc, scalar2=None, op0=ALU.mult),
              reads=["rot", "hc"], writes=["qkp"])
        P.add("dve", lambda e: e.tensor_scalar(out=qkp[:, 1, :], in0=rot[:, 1, :], scalar1=ksc, scalar2=None, op0=ALU.mult),
              reads=["rot", "hc"], writes=["qkp"])
        for m in range(2):
            P.add("pe", lambda e, m=m: e.transpose(out=psb(3)[0:64, 256 + m * 128:256 + (m + 1) * 128], in_=qkp[:, m, :],
                                                   identity=ident), reads=["qkp", "ident"], writes=["ps3B"])
        P.add("act", lambda e: e.activation(out=qkT[0:64].rearrange("p m t -> p (m t)"), in_=psb(3)[0:64, 256:512], func=AF.Copy),
              reads=["ps3B"], writes=["qkT"])
        P.add("pe", lambda e: e.matmul(psf(3, 256, 384), lhsT=qkT[0:64, 1, :], rhs=qkT[0:64, 0, :], start=True, stop=True),
              reads=["qkT"], writes=["ps3C"])
        P.add("dve", lambda e: e.tensor_tensor(out=ptr, in0=psf(3, 256, 384), in1=maskT, op=ALU.mult),
              reads=["ps3C", "maskT"], writes=["ptr"])
        P.add("pe", lambda e: e.matmul(psf(4, 0, 128), lhsT=ptr, rhs=vrbf, start=True, stop=False),
              reads=["ptr", "vrbf"], writes=["ps4a"])
        P.add("pe", lambda e: e.matmul(psf(4, 0, 128), lhsT=qkT[0:64, 0, :], rhs=Sbf[0:64, :], start=False, stop=True),
              reads=["qkT", "Sbf"], writes=["ps4a"])
        P.add("pe", lambda e: e.matmul(psf(3, 384, 512, 0, 64), lhsT=qkp[:, 1, :], rhs=vrbf, start=True, stop=True),
              reads=["qkp", "vrbf"], writes=["ps3D"])
        P.add("dve", lambda e: e.tensor_tensor(out=Stt[0:64], in0=Sst[0:64], in1=psf(3, 384, 512, 0, 64), op=ALU.add),
              reads=["S", "ps3D"], writes=["Stt"])
        P.add("dve", lambda e: e.tensor_scalar(out=Sst[0:64], in0=Stt[0:64], scalar1=gC[0:64], scalar2=None, op0=ALU.mult),
              reads=["Stt", "hc"], writes=["S"])
        P.add("act", lambda e: e.activation(out=Sbf[0:64], in_=Sst[0:64], func=AF.Copy), reads=["S"], writes=["Sbf"])
        rstd_ops(psf(4, 0, 128), "ps4a", 128, sqj[:, 128:256], "sqj", sm[:, 9:10], sm[:, 10:11], sm[:, 11:12], "r_")
        P.add("dve", lambda e: e.scalar_tensor_tensor(out=MRG[i][:, 128:256], in0=psf(4, 0, 128), scalar=sm[:, 11:12], in1=sg,
                                                      op0=ALU.mult, op1=ALU.mult),
              reads=["ps4a", "r_rstd", "sg"], writes=["mrg%d" % i])

    for t in range(nbA):
        if do_front:
            block_front(t)
        if do_attn:
            attention(t, t % 2)
        mi = t % 2
        P.add("sp", lambda e, t=t, mi=mi: e.dma_start(out=Ex.ap()[t * 128:(t + 1) * 128, :], in_=MRG[mi]),
              reads=["mrg%d" % mi], writes=[("Ex", t)], slot="mrg%d" % mi)
        if do_cc and t % 16 == 15:
            q = t // 16
            P.add("pool", lambda e, q=q: e.collective_compute(
                "AllGather", ALU.bypass, replica_groups=[[0, 1, 2, 3], [4, 5, 6, 7]],
                ins=[Ex.ap()[q * 2048:(q + 1) * 2048, :].opt()], outs=[Gx.ap()[q * 8192:(q + 1) * 8192, :].opt()]),
                reads=[("Ex", tt) for tt in range(q * 16, q * 16 + 16)], writes=[("Gx", q)], slot="cc", inc=1)
    P.add("sp", lambda e: e.dma_start(out=sret, in_=Sst[0:64]), reads=["S"], writes=["sret"], slot="sret")
    peakA = A.peak

    if do_B:
        zd = A.alloc(3072, F32)
        wtB = A.alloc(8 * 512, BF16).rearrange("p (k n) -> p k n", k=8)
        xdb = A.alloc(D, BF16)
        hdT = A.alloc(D, BF16).rearrange("p (k t) -> p k t", k=8)
        qb = A.alloc(512, F32)
        Kt = [A.alloc(512, F32) for _ in range(2)]
        Vt = [A.alloc(512, F32) for _ in range(2)]
        KN = A.alloc(512, F32)
        VN = A.alloc(512, F32)
        prod = A.alloc(512, F32)
        Sal = A.alloc(17 * 8, F32)
        Pal = A.alloc(17 * 8, F32)
        bcs = A.alloc(72, F32)
        ptb = A.alloc(256, I32)
        idxp = A.alloc(256, I32)
        selt = A.alloc(128, F32)
        OD = A.alloc(512, F32)
        Wsel = A.alloc(16, F32)
        smb = A.alloc(32, F32)
        sq16 = A.alloc(512, F32)
        a16 = A.alloc(512, F32)
        roped = A.alloc(64, F32)
        rtd = [A.alloc(8 * 32, F32).rearrange("p (a d) -> p a d", a=8) for _ in range(4)]
        rotd = A.alloc(512, F32).rearrange("p (a d) -> p a d", a=8)
        prd = A.alloc(256, F32)
        ord16 = A.alloc(512, F32).rearrange("p (h d) -> p h d", h=4)
        qTd = A.alloc(64, F32)
        QZ = A.alloc(4 * 16 * 16, F32).rearrange("p (h n c) -> p h n c", h=4, n=16)
        Sn = [A.alloc(512, F32).rearrange("p (h d) -> p h d", h=4) for _ in range(2)]
        Snw = [A.alloc(512, F32).rearrange("p (h d) -> p h d", h=4) for _ in range(2)]
        VZn = A.alloc(512, F32)
        sgd = A.alloc(512, F32)
        peakB = A.off

        P.add("sp", lambda e: e.dma_start(out=bcs, in_=bconst_d), writes=["bcs"], slot="bcs")
        P.add("sp", lambda e: e.dma_start(out=ptb, in_=pt_d.partition_broadcast(128)), writes=["ptb"], slot="ptb")
        P.add("sp", lambda e: e.dma_start(out=roped[0:16], in_=roped_d), writes=["roped"], slot="roped")
        P.add("dve", lambda e: e.tensor_scalar(out=idxp, in0=ptb, scalar1=128.0, scalar2=bcs[:, 0:1], op0=ALU.mult, op1=ALU.add),
              reads=["ptb", "bcs"], writes=["idxp"])
        P.add("pool", lambda e: e.memset(KN, 0.0), writes=["KN"])
        P.add("pool", lambda e: e.memset(VN, 0.0), writes=["VN"])
        rstd_ops(XS, "XS", D, sqj, "sqj", smb[:, 0:1], smb[:, 1:2], smb[:, 2:3], "d_")
        P.add("dve", lambda e: e.tensor_scalar(out=xdb, in0=XS, scalar1=smb[:, 2:3], scalar2=None, op0=ALU.mult),
              reads=["XS", "d_rstd"], writes=["xdb"])
        for k in range(8):
            P.add("pe", lambda e, k=k: e.transpose(out=psb(7)[:, k * 128:(k + 1) * 128], in_=xdb[:, k * 128:(k + 1) * 128], identity=ident),
                  reads=["xdb", "ident"], writes=["ps7"])
        P.add("act", lambda e: e.activation(out=hdT.rearrange("p k t -> p (k t)"), in_=psb(7), func=AF.Copy), reads=["ps7"], writes=["hdT"])
        for ci in range(6):
            load_weight(wtB, "wtB", win_d[:, ci * 512:(ci + 1) * 512], 8, 512, wstage, gain=gm, gain_res="gm", eng="pool")
            for k in range(8):
                P.add("pe", lambda e, k=k: e.matmul(psf(4, 0, 512, 0, 16), lhsT=hdT[:, k, 0:16], rhs=wtB[:, k, :], start=(k == 0), stop=(k == 7)),
                      reads=["hdT", "wtB"], writes=["ps4"])
            P.add("dve", lambda e, ci=ci: e.tensor_copy(out=zd[0:16, ci * 512:(ci + 1) * 512], in_=psf(4, 0, 512, 0, 16)),
                  reads=["ps4"], writes=["zd"])
        P.add("sp", lambda e: e.dma_start(out=ks_o, in_=zd[0:16, 512:1024]), reads=["zd"], writes=["ks_o"], slot="zdo")
        P.add("sp", lambda e: e.dma_start(out=vs_o, in_=zd[0:16, 1024:1536]), reads=["zd"], writes=["vs_o"], slot="zdo")
        coef8 = smb[:, 3:4]
        P.add("dve", lambda e: e.scalar_tensor_tensor(out=coef8[0:8], in0=r84[0:8, 5:6], scalar=neglam[0:8], in1=r84[0:8, 4:5],
                                                      op0=ALU.mult, op1=ALU.add), reads=["r84", "lam8"], writes=["coef8"])
        dm8 = r84[0:8, 0:4].unsqueeze(2).to_broadcast([8, 4, 128])
        biasd = bcs[:, 1:69].rearrange("p (g h) -> p g h", g=17).unsqueeze(3).to_broadcast([128, 17, 4, 2])
        for n in range(16):
            P.add("dve", lambda e, n=n: e.tensor_copy(out=selt[0:16], in_=identf[0:16, n:n + 1].to_broadcast([16, 128])),
                  reads=["identf"], writes=["selt"])
            P.add("pe", lambda e: e.matmul(psf(0), lhsT=selt[0:16], rhs=zd[0:16, 0:512], start=True, stop=True),
                  reads=["selt", "zd"], writes=["ps0"])
            P.add("act", lambda e: e.activation(out=qb, in_=psf(0), func=AF.Copy, scale=0.125), reads=["ps0"], writes=["qb"])
            P.add("sp", lambda e, n=n: e.dma_start(out=KN[0:1, :], in_=zd[n:n + 1, 512:1024]), reads=["zd"], writes=["KN"], slot="KN")
            P.add("sp", lambda e, n=n: e.dma_start(out=VN[0:1, :], in_=zd[n:n + 1, 1024:1536]), reads=["zd"], writes=["VN"], slot="VN")
            for g in range(17):
                if g < 16:
                    kb, kres = Kt[g % 2], "kt%d" % (g % 2)
                    P.add("pool", lambda e, kb=kb, n=n, g=g: e.indirect_dma_start(
                        out=kb, out_offset=None, in_=ck_d,
                        in_offset=bass.IndirectOffsetOnAxis(ap=idxp[:, n * 16 + g:n * 16 + g + 1], axis=0)),
                        reads=["idxp"], writes=[kres], slot=kres)
                else:
                    kb, kres = KN, "KN"
                P.add("pool", lambda e, kb=kb: e.tensor_tensor(out=prod, in0=kb, in1=qb, op=ALU.mult), reads=[kres, "qb"], writes=["prod"])
                P.add("dve", lambda e, g=g: e.tensor_reduce(out=Sal[:, g * 8:(g + 1) * 8], in_=prod.rearrange("p (a d) -> p a d", a=8),
                                                            axis=AX.X, op=ALU.add), reads=["prod"], writes=["Sal"])
            Sal4 = Sal.rearrange("p (g h m) -> p g h m", g=17, h=4)
            P.add("dve", lambda e: e.tensor_tensor(out=Sal4, in0=Sal4, in1=biasd, op=ALU.add), reads=["Sal", "bcs"], writes=["Sal"])
            P.add("act", lambda e: e.activation(out=Pal, in_=Sal, func=AF.Exp), reads=["Sal"], writes=["Pal"])
            for g in range(17):
                if g < 16:
                    vb, vres = Vt[g % 2], "vt%d" % (g % 2)
                    P.add("pool", lambda e, vb=vb, n=n, g=g: e.indirect_dma_start(
                        out=vb, out_offset=None, in_=cv_d,
                        in_offset=bass.IndirectOffsetOnAxis(ap=idxp[:, n * 16 + g:n * 16 + g + 1], axis=0)),
                        reads=["idxp"], writes=[vres], slot=vres)
                else:
                    vb, vres = VN, "VN"
                P.add("pe", lambda e, g=g, vb=vb: e.matmul(psf(1, 0, 512, 0, 8), lhsT=Pal[:, g * 8:(g + 1) * 8], rhs=vb, start=(g == 0), stop=(g == 16)),
                      reads=["Pal", vres], writes=["ps1"])
                P.add("pe", lambda e, g=g: e.matmul(psf(2, 0, 1, 0, 8), lhsT=Pal[:, g * 8:(g + 1) * 8], rhs=onesf[:, 0:1], start=(g == 0), stop=(g == 16)),
                      reads=["Pal", "onesf"], writes=["ps2"])
            w8 = smb[:, 4:5]
            P.add("dve", lambda e: e.reciprocal(out=w8[0:8], in_=psf(2, 0, 1, 0, 8)), reads=["ps2"], writes=["w8"])
            P.add("dve", lambda e: e.tensor_tensor(out=w8[0:8], in0=w8[0:8], in1=coef8[0:8], op=ALU.mult), reads=["w8", "coef8"], writes=["w8"])
            P.add("dve", lambda e: e.tensor_tensor(out=OD[0:8].rearrange("p (h d) -> p h d", h=4),
                                                   in0=psf(1, 0, 512, 0, 8).rearrange("p (h d) -> p h d", h=4), in1=dm8, op=ALU.mult),
                  reads=["ps1", "r84"], writes=["OD"])
            P.add("dve", lambda e, n=n: e.tensor_scalar(out=Wsel[0:8], in0=E16[0:8, n * 16:(n + 1) * 16], scalar1=w8[0:8], scalar2=None, op0=ALU.mult),
                  reads=["E16", "w8"], writes=["Wsel"])
            P.add("pe", lambda e, n=n: e.matmul(psf(3, 0, 512, 0, 16), lhsT=Wsel[0:8], rhs=OD[0:8], start=(n == 0), stop=(n == 15)),
                  reads=["Wsel", "OD"], writes=["ps3"])
        MD4 = MD[0:16].rearrange("p (h c) -> p h c", h=4)
        P.add("act", lambda e: e.activation(out=sq16[0:16], in_=psf(3, 0, 512, 0, 16), func=AF.Square), reads=["ps3"], writes=["sq16"])
        P.add("dve", lambda e: e.tensor_reduce(out=smb[0:16, 8:12], in_=sq16[0:16].rearrange("p (h d) -> p h d", h=4), axis=AX.X, op=ALU.add),
              reads=["sq16"], writes=["ms4"])
        P.add("act", lambda e: e.activation(out=smb[0:16, 12:16], in_=smb[0:16, 8:12], func=AF.Ln, bias=epsb[0:16], scale=1.0 / 128),
              reads=["ms4", "epsb"], writes=["l4"])
        P.add("act", lambda e: e.activation(out=smb[0:16, 16:20], in_=smb[0:16, 12:16], func=AF.Exp, scale=-0.5), reads=["l4"], writes=["r4"])
        P.add("dve", lambda e: e.tensor_tensor(out=a16[0:16].rearrange("p (h d) -> p h d", h=4),
                                               in0=psf(3, 0, 512, 0, 16).rearrange("p (h d) -> p h d", h=4),
                                               in1=smb[0:16, 16:20].unsqueeze(2).to_broadcast([16, 4, 128]), op=ALU.mult),
              reads=["ps3", "r4"], writes=["a16"])
        P.add("dve", lambda e: e.tensor_tensor(out=MD4[:, :, 0:128], in0=a16[0:16].rearrange("p (h d) -> p h d", h=4),
                                               in1=sub8[0:16].unsqueeze(1).to_broadcast([16, 4, 128]), op=ALU.mult),
              reads=["a16", "sub8"], writes=["MD"])
        qkd = zd[0:16, 1536:2048].rearrange("p (a d) -> p a d", a=8)
        cosd = roped[0:16, 0:32].unsqueeze(1).to_broadcast([16, 8, 32])
        sind = roped[0:16, 32:64].unsqueeze(1).to_broadcast([16, 8, 32])
        d1, d2 = qkd[:, :, 0:32], qkd[:, :, 32:64]
        P.add("dve", lambda e: e.tensor_tensor(out=rtd[0][0:16], in0=d1, in1=cosd, op=ALU.mult), reads=["zd", "roped"], writes=["rtd0"])
        P.add("dve", lambda e: e.tensor_tensor(out=rtd[1][0:16], in0=d2, in1=sind, op=ALU.mult), reads=["zd", "roped"], writes=["rtd1"])
        P.add("dve", lambda e: e.tensor_tensor(out=rtd[2][0:16], in0=d2, in1=cosd, op=ALU.mult), reads=["zd", "roped"], writes=["rtd2"])
        P.add("dve", lambda e: e.tensor_tensor(out=rtd[3][0:16], in0=d1, in1=sind, op=ALU.mult), reads=["zd", "roped"], writes=["rtd3"])
        P.add("dve", lambda e: e.tensor_tensor(out=rotd[0:16, :, 0:32], in0=rtd[0][0:16], in1=rtd[1][0:16], op=ALU.subtract),
              reads=["rtd0", "rtd1"], writes=["rotd"])
        P.add("dve", lambda e: e.tensor_tensor(out=rotd[0:16, :, 32:64], in0=rtd[2][0:16], in1=rtd[3][0:16], op=ALU.add),
              reads=["rtd2", "rtd3"], writes=["rotd"])
        P.add("dve", lambda e: e.tensor_scalar(out=rotd[0:16, 4:8, :], in0=rotd[0:16, 4:8, :], scalar1=0.125, scalar2=None, op0=ALU.mult),
              reads=["rotd"], writes=["rotd"])
        P.add("dve", lambda e: e.tensor_tensor(out=prd[0:16].rearrange("p (h d) -> p h d", h=4), in0=rotd[0:16, 0:4, :], in1=rotd[0:16, 4:8, :], op=ALU.mult),
              reads=["rotd"], writes=["prd"])
        P.add("dve", lambda e: e.tensor_reduce(out=smb[0:16, 20:24], in_=prd[0:16].rearrange("p (h d) -> p h d", h=4), axis=AX.X, op=ALU.add),
              reads=["prd"], writes=["qk4"])
        vr4 = zd[0:16, 2048:2560].rearrange("p (h d) -> p h d", h=4)
        P.add("dve", lambda e: e.tensor_tensor(out=ord16[0:16], in0=vr4, in1=smb[0:16, 20:24].unsqueeze(2).to_broadcast([16, 4, 128]), op=ALU.mult),
              reads=["zd", "qk4"], writes=["ord16"])
        for h in range(4):
            P.add("pe", lambda e, h=h: e.transpose(out=psf(0, h * 16, (h + 1) * 16, 0, 64), in_=rotd[0:16, h, :], identity=identf[0:16, 0:16]),
                  reads=["rotd", "identf"], writes=["ps0"])
        P.add("dve", lambda e: e.tensor_copy(out=qTd[0:64], in_=psf(0, 0, 64, 0, 64)), reads=["ps0"], writes=["qTd"])
        qT3 = qTd[0:64].rearrange("p (h n) -> p h n", h=4)
        P.add("dve", lambda e: e.tensor_tensor(out=QZ[0:64], in0=qT3.unsqueeze(3).to_broadcast([64, 4, 16, 16]),
                                               in1=E16[0:64].rearrange("p (n c) -> p n c", n=16).unsqueeze(1).to_broadcast([64, 4, 16, 16]), op=ALU.mult),
              reads=["qTd", "E16"], writes=["QZ"])
        gams = [1.0 - 2.0 ** (-5.0 - h) for h in range(4)]
        for n in range(16):
            i = n % 2
            P.add("sp", lambda e, n=n, i=i: e.dma_start(out=Sn[i][0:64], in_=sst_d[n].rearrange("h d e -> d h e")), writes=["sn%d" % i], slot="sn%d" % i)
            for h in range(4):
                P.add("pe", lambda e, n=n, h=h, i=i: e.matmul(psf(4 + h, 0, 128, 0, 16), lhsT=QZ[0:64, h, n, :], rhs=Sn[i][0:64, h, :],
                                                              start=(n == 0), stop=(n == 15)),
                      reads=["QZ", "sn%d" % i], writes=["ps%d" % (4 + h)])
            P.add("dve", lambda e, n=n: e.tensor_scalar(out=VZn[0:16], in0=zd[0:16, 2048:2560], scalar1=identf[0:16, n:n + 1], scalar2=None, op0=ALU.mult),
                  reads=["zd", "identf"], writes=["VZn"])
            for h in range(4):
                P.add("pe", lambda e, h=h: e.matmul(psf(1, h * 128, (h + 1) * 128, 0, 64), lhsT=rotd[0:16, 4 + h, :], rhs=VZn[0:16, h * 128:(h + 1) * 128],
                                                    start=True, stop=True), reads=["rotd", "VZn"], writes=["ps1"])
            for h in range(4):
                P.add("dve", lambda e, h=h, i=i: e.scalar_tensor_tensor(out=Snw[i][0:64, h, :], in0=Sn[i][0:64, h, :], scalar=gams[h],
                                                                        in1=psf(1, h * 128, (h + 1) * 128, 0, 64), op0=ALU.mult, op1=ALU.add),
                      reads=["sn%d" % i, "ps1"], writes=["snw%d" % i])
            P.add("sp", lambda e, n=n, i=i: e.dma_start(out=ss_o[n].rearrange("h d e -> d h e"), in_=Snw[i][0:64]),
                  reads=["snw%d" % i], writes=[("ss_o", n)], slot="snw%d" % i)
        for h in range(4):
            P.add("dve", lambda e, h=h: e.scalar_tensor_tensor(out=ord16[0:16, h, :], in0=psf(4 + h, 0, 128, 0, 16), scalar=gams[h], in1=ord16[0:16, h, :],
                                                               op0=ALU.mult, op1=ALU.add), reads=["ps%d" % (4 + h), "ord16"], writes=["ord16"])
        o2d = ord16[0:16].rearrange("p h d -> p (h d)")
        P.add("act", lambda e: e.activation(out=sq16[0:16], in_=o2d, func=AF.Square), reads=["ord16"], writes=["sq16"])
        P.add("dve", lambda e: e.tensor_reduce(out=smb[0:16, 8:12], in_=sq16[0:16].rearrange("p (h d) -> p h d", h=4), axis=AX.X, op=ALU.add),
              reads=["sq16"], writes=["ms4"])
        P.add("act", lambda e: e.activation(out=smb[0:16, 12:16], in_=smb[0:16, 8:12], func=AF.Ln, bias=epsb[0:16], scale=1.0 / 128),
              reads=["ms4", "epsb"], writes=["l4"])
        P.add("act", lambda e: e.activation(out=smb[0:16, 16:20], in_=smb[0:16, 12:16], func=AF.Exp, scale=-0.5), reads=["l4"], writes=["r4"])
        grd = zd[0:16, 2560:3072]
        P.add("act", lambda e: e.activation(out=sgd[0:16], in_=grd, func=AF.Exp, scale=-1.0), reads=["zd"], writes=["sgd"])
        P.add("dve", lambda e: e.tensor_scalar(out=sgd[0:16], in0=sgd[0:16], scalar1=1.0, scalar2=None, op0=ALU.add), reads=["sgd"], writes=["sgd"])
        P.add("dve", lambda e: e.reciprocal(out=sgd[0:16], in_=sgd[0:16]), reads=["sgd"], writes=["sgd"])
        P.add("dve", lambda e: e.tensor_tensor(out=sgd[0:16], in0=sgd[0:16], in1=grd, op=ALU.mult), reads=["sgd", "zd"], writes=["sgd"])
        P.add("dve", lambda e: e.tensor_tensor(out=a16[0:16].rearrange("p (h d) -> p h d", h=4), in0=ord16[0:16],
                                               in1=smb[0:16, 16:20].unsqueeze(2).to_broadcast([16, 4, 128]), op=ALU.mult),
              reads=["ord16", "r4"], writes=["a16"])
        P.add("dve", lambda e: e.tensor_tensor(out=MD4[:, :, 128:256], in0=a16[0:16].rearrange("p (h d) -> p h d", h=4),
                                               in1=sgd[0:16].rearrange("p (h d) -> p h d", h=4), op=ALU.mult),
              reads=["a16", "sgd"], writes=["MD"])

    if not do_C:
        P.add("sp", None, reads=[("kout", t) for t in range(nbA)] + [("vout", t) for t in range(nbA)] + ["sret"] + [("Ex", t) for t in range(nbA)] + (["ks_o", "vs_o"] + [("ss_o", n) for n in range(16)] if do_B else []))
        P.emit(es, limit)
        es.close()
        return nc, dict(P.stats)
    P.barrier()
    A.off = phase_base
    NBLK = nblkC
    wstage = A.alloc(8 * 256, F32)
    gains = A.alloc(24, F32)
    idxt = A.alloc(NOWN * 4, I32)
    X2 = A.alloc(NBLK * D, F32).rearrange("p (b n) -> p b n", b=NBLK)
    H2T = A.alloc(8 * (NBLK + 1) * 128, BF16).rearrange("p (k t) -> p k t", k=8)
    cT = A.alloc(D, BF16).rearrange("p (k t) -> p k t", k=8)
    hbf = A.alloc(D, BF16)
    smc = A.alloc(16, F32)
    sqc = hbf
    qmb = A.alloc(D, F32)
    sweep_base = A.off
    Wout = A.alloc(8 * D, BF16).rearrange("p (k n) -> p k n", k=8)
    Wmq = A.alloc(8 * D, BF16).rearrange("p (k n) -> p k n", k=8)
    Wmo = A.alloc(8 * D, BF16).rearrange("p (k n) -> p k n", k=8)
    mkT = A.alloc(8 * 256, BF16).rearrange("p (c m) -> p c m", c=8)
    mvx = A.alloc(2 * 4 * 258, BF16).rearrange("p (a h c) -> p a h c", a=2, h=4)
    memT = A.alloc(8 * 256, BF16).rearrange("p (k t) -> p k t", k=8)
    regR = A.off
    memx = A.alloc(2 * D, F32).rearrange("p (a n) -> p a n", a=2)
    memo = A.alloc(2 * 512, F32).rearrange("p (a n) -> p a n", a=2)
    wtmp = A.alloc(8 * 512, BF16).rearrange("p (k n) -> p k n", k=8)
    peakC1 = A.off

    P.add("sp", lambda e: e.dma_start(out=gains, in_=gains_d), writes=["gains"], slot="gains")
    P.add("sp", lambda e: e.dma_start(out=idxt, in_=idx_d), writes=["idxt"], slot="idxt")
    gmq, gkv, gmlp = gains[:, 0:8], gains[:, 8:16], gains[:, 16:24]
    load_weight(Wout, "Wout", wout_d, 8, D, wstage)
    load_weight(Wmq, "Wmq", wmq_d, 8, D, wstage, gain=gmq, gain_res="gains")
    load_weight(Wmo, "Wmo", wmo_d, 8, D, wstage)
    P.add("pool", lambda e: e.memset(mvx[:, :, :, 256:258], 1.0), writes=["mvxones"])

    def transposes_to(dst, dst_res, src, src_res, bank, nk=8, evac="act"):
        for k in range(nk):
            P.add("pe", lambda e, k=k: e.transpose(out=psb(bank)[:, k * 128:(k + 1) * 128], in_=src[:, k * 128:(k + 1) * 128],
                                                   identity=ident), reads=(src_res if isinstance(src_res, list) else [src_res]) + ["ident"], writes=["ps%d" % bank])
        if evac == "act":
            P.add("act", lambda e: e.activation(out=dst, in_=psb(bank)[:, 0:nk * 128].rearrange("p (k t) -> p k t", k=nk), func=AF.Copy),
                  reads=["ps%d" % bank], writes=[dst_res])
        else:
            P.add("dve", lambda e: e.tensor_copy(out=dst, in_=psb(bank)[:, 0:nk * 128].rearrange("p (k t) -> p k t", k=nk)),
                  reads=["ps%d" % bank], writes=[dst_res])

    P.add("sp", lambda e: e.dma_start(out=memx, in_=memp_d.rearrange("(a p) n -> p a n", p=128)), writes=["memx"], slot="memx")
    for a in range(2):
        rstd_ops(memx[:, a, :], "memx", D, sqc, "hbf", smc[:, 0:1], smc[:, 1:2], smc[:, 2:3], "m_")
        P.add("dve", lambda e, a=a: e.tensor_scalar(out=hbf, in0=memx[:, a, :], scalar1=smc[:, 2:3], scalar2=None, op0=ALU.mult),
              reads=["memx", "m_rstd"], writes=["hbf"])
        transposes_to(memT[:, :, a * 128:(a + 1) * 128], "memT", hbf, "hbf", 0)
    for which, wsrc, outd in (("k", wmk_d, memk_o), ("v", wmv_d, memv_o)):
        for half in range(2):
            load_weight(wtmp, "wtmp", wsrc[:, half * 512:(half + 1) * 512], 8, 512, wstage, gain=gkv, gain_res="gains")
            for a in range(2):
                for k in range(8):
                    P.add("pe", lambda e, a=a, k=k: e.matmul(psf(1 + a), lhsT=memT[:, k, a * 128:(a + 1) * 128], rhs=wtmp[:, k, :],
                                                             start=(k == 0), stop=(k == 7)),
                          reads=["memT", "wtmp"], writes=["ps%d" % (1 + a)])
                P.add("dve", lambda e, a=a: e.tensor_copy(out=memo[:, a, :], in_=psf(1 + a)),
                      reads=["ps%d" % (1 + a)], writes=["memo"])
                if which == "v":
                    P.add("act", lambda e, a=a, half=half: e.activation(
                        out=mvx[:, a, half * 2:half * 2 + 2, 0:256], in_=psf(1 + a).rearrange("p (h c) -> p h c", h=2), func=AF.Copy),
                        reads=["ps%d" % (1 + a)], writes=["mvx"])
            if which == "k":
                for ct in range(4):
                    for k in range(8):
                        P.add("pe", lambda e, ct=ct, k=k: e.matmul(psf(3, 0, 256), lhsT=wtmp[:, k, ct * 128:(ct + 1) * 128], rhs=memT[:, k, :],
                                                                   start=(k == 0), stop=(k == 7)),
                              reads=["memT", "wtmp"], writes=["ps3"])
                    P.add("act", lambda e, ct=ct, half=half: e.activation(out=mkT[:, half * 4 + ct, :], in_=psf(3, 0, 256), func=AF.Copy),
                          reads=["ps3"], writes=["mkT"])
            P.add("sp", lambda e, outd=outd, half=half: e.dma_start(
                out=outd.rearrange("(a p) n -> p a n", p=128)[:, :, half * 512:(half + 1) * 512], in_=memo),
                reads=["memo"], writes=[("memout", which, half)], slot="memo")

    def rms_to_T(x_ap, x_res, dst, dst_res, tag):
        rstd_ops(x_ap, x_res, D, sqc, "hbf", smc[:, 3:4], smc[:, 4:5], smc[:, 5:6], tag)
        P.add("dve", lambda e: e.tensor_scalar(out=hbf, in0=x_ap, scalar1=smc[:, 5:6], scalar2=None, op0=ALU.mult),
              reads=[x_res, tag + "rstd"], writes=["hbf"])
        transposes_to(dst, dst_res, hbf, "hbf", 0)

    def proj_add(x_ap, x_res, lT, lT_res, W, W_res):
        for nt in range(2):
            for k in range(8):
                P.add("pe", lambda e, nt=nt, k=k: e.matmul(psf(1 + nt), lhsT=lT[:, k, :], rhs=W[:, k, nt * 512:(nt + 1) * 512],
                                                           start=(k == 0), stop=(k == 7)),
                      reads=[lT_res, W_res], writes=["ps%d" % (1 + nt)])
            P.add("dve", lambda e, nt=nt: e.tensor_tensor(out=x_ap[:, nt * 512:(nt + 1) * 512], in0=x_ap[:, nt * 512:(nt + 1) * 512],
                                                          in1=psf(1 + nt), op=ALU.add),
                  reads=[x_res, "ps%d" % (1 + nt)], writes=[x_res])

    P.barrier()
    A.off = regR
    MGall = A.alloc(8 * 256, BF16)
    MG = [MGall[:, 0:1024], MGall[:, 1024:2048]]
    qmd = MGall.bitcast(F32)
    qmT = A.alloc(D, BF16).rearrange("p (k t) -> p k t", k=8)
    PTm = A.alloc(256, BF16)
    om = A.alloc(D, BF16)
    peakC1 = max(peakC1, A.off)
    mkt = wstage[:, 0:1024]
    mvt = wstage[:, 1024:2048]

    def mem_attend_prompt():
        for ct in range(8):
            bank = 5 + ct // 4
            for k in range(8):
                P.add("pe", lambda e, ct=ct, k=k, bank=bank: e.matmul(psf(bank, (ct % 4) * 128, (ct % 4 + 1) * 128),
                                                                     lhsT=Wmq[:, k, ct * 128:(ct + 1) * 128], rhs=cT[:, k, :],
                                                                     start=(k == 0), stop=(k == 7)),
                      reads=["cT", "Wmq"], writes=["ps%d" % bank])
        for hb in range(2):
            P.add("act", lambda e, hb=hb: e.activation(out=qmT[:, hb * 4:hb * 4 + 4, :], in_=psf(5 + hb).rearrange("p (c t) -> p c t", c=4), func=AF.Copy),
                  reads=["ps%d" % (5 + hb)], writes=["qmT"])
        for hm in range(4):
            for mc in range(2):
                for c in range(2):
                    P.add("pe", lambda e, hm=hm, mc=mc, c=c: e.matmul(psf(3, mc * 128, (mc + 1) * 128),
                                                                     lhsT=mkT[:, hm * 2 + c, mc * 128:(mc + 1) * 128], rhs=qmT[:, hm * 2 + c, :],
                                                                     start=(c == 0), stop=(c == 1)),
                          reads=["mkT", "qmT"], writes=["ps3"])
            P.add("act", lambda e: e.activation(out=PTm, in_=psf(3, 0, 256), func=AF.Exp, scale=1.0 / 16.0), reads=["ps3"], writes=["PTm"])
            for mc in range(2):
                P.add("pe", lambda e, hm=hm, mc=mc: e.matmul(psf(4, 0, 257), lhsT=PTm[:, mc * 128:(mc + 1) * 128], rhs=mvx[:, mc, hm, 0:257],
                                                             start=(mc == 0), stop=(mc == 1)),
                      reads=["PTm", "mvx", "mvxones"], writes=["ps4"])
            P.add("dve", lambda e: e.reciprocal(out=smc[:, 6:7], in_=psf(4, 256, 257)), reads=["ps4"], writes=["rm"])
            P.add("dve", lambda e, hm=hm: e.tensor_scalar(out=om[:, hm * 256:(hm + 1) * 256], in0=psf(4, 0, 256), scalar1=smc[:, 6:7],
                                                          scalar2=None, op0=ALU.mult), reads=["ps4", "rm"], writes=["om"])

    def mem_attend_sample():
        mgres = [("mg%d" % i_, r_) for i_ in range(2) for r_ in range(4)]
        for nt in range(2):
            for k in range(8):
                P.add("pe", lambda e, nt=nt, k=k: e.matmul(psf(4 + nt, 0, 512, 0, 16), lhsT=cT[:, k, 0:16], rhs=Wmq[:, k, nt * 512:(nt + 1) * 512],
                                                           start=(k == 0), stop=(k == 7)), reads=["cT", "Wmq"], writes=["ps%d" % (4 + nt)])
            P.add("act", lambda e, nt=nt: e.activation(out=qmd[0:16, nt * 512:(nt + 1) * 512], in_=psf(4 + nt, 0, 512, 0, 16), func=AF.Copy, scale=1.0 / 16.0),
                  reads=["ps%d" % (4 + nt)], writes=mgres)
        P.add("pool", lambda e: e.memset(om, 0.0), writes=["om"])
        selc = smc[:, 12:13]
        Sm = hid_s
        for n in range(16):
            P.add("dve", lambda e, n=n: e.tensor_copy(out=selq[0:16], in_=identf[0:16, n:n + 1].to_broadcast([16, 128])),
                  reads=["identf"], writes=["selq"])
            for nt in range(2):
                P.add("pe", lambda e, nt=nt: e.matmul(psf(4 + nt), lhsT=selq[0:16], rhs=qmd[0:16, nt * 512:(nt + 1) * 512], start=True, stop=True),
                      reads=["selq"] + mgres, writes=["ps%d" % (4 + nt)])
                P.add("act", lambda e, nt=nt: e.activation(out=qmb[:, nt * 512:(nt + 1) * 512], in_=psf(4 + nt), func=AF.Copy),
                      reads=["ps%d" % (4 + nt)], writes=["qmb"])
            for mc in range(2):
                P.add("sp", lambda e, n=n, mc=mc: e.dma_start(out=mkt, in_=cmk_d[n, mc * 128:(mc + 1) * 128, :]), writes=["wstage"], slot="wstage")
                P.add("pool", lambda e: e.tensor_tensor(out=mkt, in0=mkt, in1=qmb, op=ALU.mult), reads=["qmb"], writes=["wstage"])
                P.add("dve", lambda e, mc=mc: e.tensor_reduce(out=Sm[:, mc * 4:(mc + 1) * 4], in_=mkt.rearrange("p (h d) -> p h d", h=4), axis=AX.X, op=ALU.add),
                      reads=["wstage"], writes=["Sm"])
            P.add("act", lambda e: e.activation(out=Sm[:, 8:16], in_=Sm[:, 0:8], func=AF.Exp), reads=["Sm"], writes=["Pm"])
            for mc in range(2):
                P.add("sp", lambda e, n=n, mc=mc: e.dma_start(out=mvt, in_=cmv_d[n, mc * 128:(mc + 1) * 128, :]), writes=["wstage2"], slot="wstage2")
                for nt in range(2):
                    P.add("pe", lambda e, mc=mc, nt=nt: e.matmul(psf(1 + nt, 0, 512, 0, 4), lhsT=Sm[:, 8 + mc * 4:8 + (mc + 1) * 4], rhs=mvt[:, nt * 512:(nt + 1) * 512],
                                                                start=(mc == 0), stop=(mc == 1)), reads=["Pm", "wstage2"], writes=["ps%d" % (1 + nt)])
                P.add("pe", lambda e, mc=mc: e.matmul(psf(3, 0, 1, 0, 4), lhsT=Sm[:, 8 + mc * 4:8 + (mc + 1) * 4], rhs=onesf[:, 0:1], start=(mc == 0), stop=(mc == 1)),
                      reads=["Pm", "onesf"], writes=["ps3"])
            w4 = smc[:, 13:14]
            P.add("dve", lambda e: e.reciprocal(out=w4[0:4], in_=psf(3, 0, 1, 0, 4)), reads=["ps3"], writes=["w4"])
            dm4 = identf[0:4, 0:4].unsqueeze(2).to_broadcast([4, 4, 256])
            for nt in range(2):
                P.add("dve", lambda e, nt=nt: e.tensor_tensor(out=OD4[0:4, nt * 512:(nt + 1) * 512].rearrange("p (h d) -> p h d", h=2),
                                                              in0=psf(1 + nt, 0, 512, 0, 4).rearrange("p (h d) -> p h d", h=2),
                                                              in1=identf[0:4, 2 * nt:2 * nt + 2].unsqueeze(2).to_broadcast([4, 2, 256]), op=ALU.mult),
                      reads=["ps%d" % (1 + nt), "identf"], writes=["OD4"])
            P.add("dve", lambda e, n=n: e.tensor_scalar(out=Wsel4[0:4], in0=E16[0:4, n * 16:(n + 1) * 16], scalar1=w4[0:4], scalar2=None, op0=ALU.mult),
                  reads=["E16", "w4"], writes=["Wsel4"])
            for nt in range(2):
                P.add("pe", lambda e, n=n, nt=nt: e.matmul(psf(6 + nt, 0, 512, 0, 16), lhsT=Wsel4[0:4], rhs=OD4[0:4, nt * 512:(nt + 1) * 512],
                                                           start=(n == 0), stop=(n == 15)), reads=["Wsel4", "OD4"], writes=["ps%d" % (6 + nt)])
        for nt in range(2):
            P.add("dve", lambda e, nt=nt: e.tensor_copy(out=om[0:16, nt * 512:(nt + 1) * 512], in_=psf(6 + nt, 0, 512, 0, 16)),
                  reads=["ps%d" % (6 + nt)], writes=["om"])

    def sweep1(blk, x, xr, mg, mgres, sample):
        transposes_to(cT, "cT", mg, mgres, 0)
        proj_add(x, xr, cT, "cT", Wout, "Wout")
        rms_to_T(x, xr, cT, "cT", "q_")
        if sample:
            mem_attend_sample()
        else:
            mem_attend_prompt()
        transposes_to(cT, "cT", om, "om", 0)
        proj_add(x, xr, cT, "cT", Wmo, "Wmo")
        rms_to_T(x, xr, H2T[:, :, blk * 128:(blk + 1) * 128], ("H2T", blk), "h_")

    for blk in range(NBLK):
        i = blk % 2
        x = X2[:, blk, :]
        xr = ("X2", blk)
        mg = MG[i]
        mgr = "mg%d" % i
        P.add("sp", lambda e, blk=blk, x=x: e.dma_start(out=x, in_=xown[blk * 128:(blk + 1) * 128, :]), writes=[xr], slot="x2ld%d" % i)
        for r in range(4):
            P.add("pool", lambda e, r=r, blk=blk, mg=mg: e.indirect_dma_start(
                out=mg[:, r * 256:(r + 1) * 256], out_offset=None, in_=Gx.ap(),
                in_offset=bass.IndirectOffsetOnAxis(ap=idxt[:, blk * 4 + r:blk * 4 + r + 1], axis=0)),
                reads=[("Gx", q_) for q_ in range(4)] + ["idxt"], writes=[(mgr, r)], slot="%s_%d" % (mgr, r))
        sweep1(blk, x, xr, mg, [(mgr, r) for r in range(4)], False)
    if do_B:
        sweep1(NBLK, XS, "XS", MD, ["MD"], True)

    P.barrier()
    A.off = sweep_base
    Wq = [A.alloc(8 * 1024, BF16).rearrange("p (k n) -> p k n", k=8) for _ in range(2)]
    hid = A.alloc(8 * 512, BF16).rearrange("p (f t) -> p f t", f=8)
    sqh = [A.alloc(512, F32) for _ in range(2)]
    yo = [A.alloc(D, F32) for _ in range(2)]
    nfin = A.alloc(D, F32)
    P.add("sp", lambda e: e.dma_start(out=nfin, in_=nfin_d.partition_broadcast(128)), writes=["nfin"], slot="nfin")
    NT4 = NBLK // 4
    for q in range(4):
        load_weight(Wq[0], "Wup", wup_d[:, q * 1024:(q + 1) * 1024], 8, 1024, wstage, gain=gmlp, gain_res="gains")
        load_weight(Wq[1], "Wdn", wdn_d[q * 1024:(q + 1) * 1024, :], 8, 1024, wstage)
        for tt in range(NT4):
            h2res = [("H2T", tt * 4 + bb) for bb in range(4)]
            for fc in range(8):
                bank = 5 + fc % 2
                for k in range(8):
                    P.add("pe", lambda e, fc=fc, k=k, bank=bank, tt=tt: e.matmul(psf(bank), lhsT=Wq[0][:, k, fc * 128:(fc + 1) * 128],
                                                                             rhs=H2T[:, k, tt * 512:(tt + 1) * 512], start=(k == 0), stop=(k == 7)),
                          reads=h2res + ["Wup"], writes=["ps%d" % bank])
                sq = sqh[fc % 2]
                sqr = "sqh%d" % (fc % 2)
                P.add("act", lambda e, bank=bank, sq=sq: e.activation(out=sq, in_=psf(bank), func=AF.Square), reads=["ps%d" % bank], writes=[sqr])
                P.add("dve", lambda e, bank=bank, sq=sq, fc=fc: e.scalar_tensor_tensor(out=hid[:, fc, :], in0=psf(bank), scalar=0.0, in1=sq,
                                                                                   op0=ALU.is_gt, op1=ALU.mult),
                      reads=["ps%d" % bank, sqr], writes=["hid"])
            for bb in range(4):
                blk = tt * 4 + bb
                x = X2[:, blk, :]
                xr = ("X2", blk)
                for nt in range(2):
                    for fc in range(8):
                        P.add("pe", lambda e, nt=nt, fc=fc, bb=bb: e.matmul(psf(1 + nt), lhsT=hid[:, fc, bb * 128:(bb + 1) * 128],
                                                                         rhs=Wq[1][:, fc, nt * 512:(nt + 1) * 512], start=(fc == 0), stop=(fc == 7)),
                              reads=["hid", "Wdn"], writes=["ps%d" % (1 + nt)])
                    P.add("dve", lambda e, nt=nt, x=x: e.tensor_tensor(out=x[:, nt * 512:(nt + 1) * 512], in0=x[:, nt * 512:(nt + 1) * 512],
                                                                      in1=psf(1 + nt), op=ALU.add),
                          reads=[xr, "ps%d" % (1 + nt)], writes=[xr])
        if do_B:
            for fc in range(8):
                bank = 5 + fc % 2
                for k in range(8):
                    P.add("pe", lambda e, fc=fc, k=k, bank=bank: e.matmul(psf(bank, 0, 128), lhsT=Wq[0][:, k, fc * 128:(fc + 1) * 128],
                                                                      rhs=H2T[:, k, NBLK * 128:(NBLK + 1) * 128], start=(k == 0), stop=(k == 7)),
                          reads=[("H2T", NBLK), "Wup"], writes=["ps%d" % bank])
                sq = sqh[fc % 2]
                sqr = "sqh%d" % (fc % 2)
                P.add("act", lambda e, bank=bank, sq=sq: e.activation(out=sq[:, 0:128], in_=psf(bank, 0, 128), func=AF.Square), reads=["ps%d" % bank], writes=[sqr])
                P.add("dve", lambda e, bank=bank, sq=sq, fc=fc: e.scalar_tensor_tensor(out=hid[:, fc, 0:128], in0=psf(bank, 0, 128), scalar=0.0, in1=sq[:, 0:128],
                                                                                   op0=ALU.is_gt, op1=ALU.mult),
                      reads=["ps%d" % bank, sqr], writes=["hid"])
            for nt in range(2):
                for fc in range(8):
                    P.add("pe", lambda e, nt=nt, fc=fc: e.matmul(psf(1 + nt), lhsT=hid[:, fc, 0:128], rhs=Wq[1][:, fc, nt * 512:(nt + 1) * 512],
                                                                start=(fc == 0), stop=(fc == 7)), reads=["hid", "Wdn"], writes=["ps%d" % (1 + nt)])
                P.add("dve", lambda e, nt=nt: e.tensor_tensor(out=XS[:, nt * 512:(nt + 1) * 512], in0=XS[:, nt * 512:(nt + 1) * 512],
                                                              in1=psf(1 + nt), op=ALU.add), reads=["XS", "ps%d" % (1 + nt)], writes=["XS"])
    for blk in range(NBLK + (1 if do_B else 0)):
        i = blk % 2
        x = X2[:, blk, :] if blk < NBLK else XS
        xr = ("X2", blk) if blk < NBLK else "XS"
        rstd_ops(x, xr, D, sqc, "hbf", smc[:, 7:8], smc[:, 8:9], smc[:, 9:10], "f_")
        P.add("dve", lambda e, x=x, i=i: e.scalar_tensor_tensor(out=yo[i], in0=x, scalar=smc[:, 9:10], in1=nfin, op0=ALU.mult, op1=ALU.mult),
              reads=[xr, "f_rstd", "nfin"], writes=["yo%d" % i])
        if blk < NBLK:
            P.add("sp", lambda e, blk=blk, i=i: e.dma_start(out=yown[blk * 128:(blk + 1) * 128, :], in_=yo[i]),
                  reads=["yo%d" % i], writes=[("yown", blk)], slot="yo%d" % i)
        else:
            P.add("sp", lambda e, i=i: e.dma_start(out=ys_o, in_=yo[i][0:16, :]), reads=["yo%d" % i], writes=["ys_o"], slot="yo%d" % i)

    P.add("sp", None, reads=[("yown", blk) for blk in range(NBLK)] + [("kout", t) for t in range(NB)]
          + [("vout", t) for t in range(NB)] + ["sret"] + [("memout", w, hf) for w in "kv" for hf in range(2)]
          + (["ys_o", "ks_o", "vs_o"] + [("ss_o", n) for n in range(16)] if do_B else []))
    P.emit(es, limit)
    es.close()
    return nc, dict(P.stats, peakA=peakA, peakC1=peakC1, peakC2=A.off)


def _t5_bucket_np(n):
    n = np.maximum(n, 0)
    nf = np.maximum(n, 1).astype(np.float32)
    large = 16 + (np.log(nf / np.float32(16)) / np.float32(math.log(128 / 16)) * np.float32(16)).astype(np.int32)
    large = np.minimum(large, 31)
    return np.where(n < 16, n, large)


def _rope_table():
    inv = (np.float32(10000.0) ** (-np.linspace(0.0, 1.0, 32, dtype=np.float32))).astype(np.float32)
    pos = np.arange(T, dtype=np.float32)
    ang = (pos[:, None] * inv[None, :]).astype(np.float32)
    tab = np.concatenate([np.cos(ang), np.sin(ang)], axis=1).astype(np.float32)
    return tab.reshape(NB, 128, 64)


_PROG = {}


def _get_prog(n_phys):
    if n_phys not in _PROG:
        _PROG[n_phys] = build_program(n_phys=n_phys)
    return _PROG[n_phys]


def kernel(x_prompt, x_sample, mem_prompt, cache_k, cache_v, page_table, state_ret,
           cache_mem_k, cache_mem_v, rel_bias, norm_mix, w_in, lambda_q1, lambda_k1,
           lambda_q2, lambda_k2, subln_a, w_out, norm_mem_q, norm_mem_kv, w_mq, w_mk,
           w_mv, w_mo, norm_mlp, w_up, w_down, norm_final, _compact_pools=False):
    f32 = np.float32
    x_prompt = np.asarray(x_prompt, f32)
    w_in0 = np.asarray(w_in, f32)[0]
    rel_bias = np.asarray(rel_bias, f32)
    cache_k = np.asarray(cache_k, f32)
    cache_v = np.asarray(cache_v, f32)
    page_table = np.asarray(page_table, np.int32)
    n_phys = cache_k.shape[1]
    if _compact_pools:
        n_phys = 256
    nc, stats = _get_prog(n_phys)
    ck_full = cache_k[0].reshape(-1, 512)
    cv_full = cache_v[0].reshape(-1, 512)
    x_sample = np.asarray(x_sample, f32)
    state_ret = np.asarray(state_ret, f32)
    cache_mem_k = np.asarray(cache_mem_k, f32)
    cache_mem_v = np.asarray(cache_mem_v, f32)
    e16 = np.ascontiguousarray(np.tile(np.eye(16, dtype=f32).reshape(1, 256), (64, 1)))
    r84 = np.zeros((8, 8), f32)
    for hh in range(4):
        for mm in range(2):
            r84[hh * 2 + mm, hh] = 1.0
            r84[hh * 2 + mm, 4 + mm] = 1.0
    inv = (np.float32(10000.0) ** (-np.linspace(0.0, 1.0, 32, dtype=np.float32))).astype(f32)
    angd = (np.float32(PAST) * inv).astype(f32)
    roped = np.ascontiguousarray(np.tile(np.concatenate([np.cos(angd), np.sin(angd)]).astype(f32)[None, :], (16, 1)))
    pp = np.arange(128)
    bconst_base = np.zeros((128, 72), f32)
    bconst_base[:, 0] = pp
    reld = PAST - (np.arange(16)[None, :] * 128 + pp[:, None])
    bkd = _t5_bucket_np(reld)

    rope = _rope_table()
    p = np.arange(128)
    maskT = (p[None, :] >= p[:, None]).astype(f32)
    ident = np.eye(128, dtype=f32)
    lamv = np.concatenate([np.asarray(a, f32)[0] for a in (lambda_q1, lambda_k1, lambda_q2, lambda_k2)])[None, :]
    subln = np.asarray(subln_a, f32)[0][None, :]
    wout_full = np.asarray(w_out, f32)[0]
    perm = np.concatenate([np.concatenate([np.arange(r * 128, (r + 1) * 128), 512 + np.arange(r * 128, (r + 1) * 128)]) for r in range(4)])
    wout_p = np.ascontiguousarray(wout_full[perm])

    def g8(v):
        return np.ascontiguousarray(np.asarray(v, f32).reshape(8, 128).T)

    gains = np.concatenate([g8(norm_mem_q[0]), g8(norm_mem_kv[0]), g8(norm_mlp[0])], axis=1)
    rel_d = p[None, :] - p[:, None]
    bk_sub = _t5_bucket_np(128 + rel_d)
    bk_dia = _t5_bucket_np(rel_d)

    in_maps = []
    for c in range(NCORES):
        b, h = c // 4, c % 4
        j = h
        cols = np.concatenate([
            np.arange(h * 128, (h + 1) * 128),
            512 + np.arange(h * 128, (h + 1) * 128),
            1024 + np.arange(h * 128, (h + 1) * 128),
            2048 + np.arange(h * 128, (h + 1) * 128),
            2560 + np.arange(h * 128, (h + 1) * 128),
            1536 + np.arange(h * 64, (h + 1) * 64),
            1792 + np.arange(h * 64, (h + 1) * 64),
        ])
        gam = 1.0 - 2.0 ** (-5.0 - h)
        hconst = np.zeros((128, 8), f32)
        hconst[:, 0] = gam ** (p + 1.0)
        hconst[:, 1] = (64.0 ** -0.5) * gam ** (-(p + 1.0))
        hconst[:, 2] = gam ** 128.0
        hconst[:, 3] = rel_bias[31, h]
        bt = np.empty((128, 2, 128), f32)
        bt[:, 0, :] = rel_bias[bk_sub, h]
        bt[:, 1, :] = np.where(rel_d >= 0, rel_bias[bk_dia, h], f32(NEG))
        idx = np.empty((128, NOWN * 4), np.int32)
        for blk in range(NOWN):
            for r in range(4):
                idx[:, blk * 4 + r] = j * 8192 + r * 2048 + blk * 128 + p
        bconst = bconst_base.copy()
        bd = np.full((128, 17, 4), f32(NEG), f32)
        bd[:, 0:16, :] = rel_bias[bkd, :]
        bd[0, 16, :] = rel_bias[0, :]
        bconst[:, 1:69] = bd.reshape(128, 68)
        ptc = page_table[c * 16:(c + 1) * 16]
        if _compact_pools:
            uniq, invp = np.unique(ptc.reshape(-1), return_inverse=True)
            ck_c = np.zeros((256 * 128, 512), f32)
            cv_c = np.zeros((256 * 128, 512), f32)
            ck_c[:len(uniq) * 128] = cache_k[0][uniq].reshape(-1, 512)
            cv_c[:len(uniq) * 128] = cache_v[0][uniq].reshape(-1, 512)
            ptc = invp.reshape(16, 16).astype(np.int32)
        else:
            ck_c, cv_c = ck_full, cv_full
        in_maps.append({
            "xd": np.ascontiguousarray(x_sample[c * 16:(c + 1) * 16, 0, :]),
            "win": w_in0,
            "ck": ck_c,
            "cv": cv_c,
            "pt": np.ascontiguousarray(ptc.reshape(1, 256)),
            "sst": np.ascontiguousarray(state_ret[0, c * 16:(c + 1) * 16]),
            "cmk": np.ascontiguousarray(cache_mem_k[0, c * 16:(c + 1) * 16].reshape(16, 256, D)),
            "cmv": np.ascontiguousarray(cache_mem_v[0, c * 16:(c + 1) * 16].reshape(16, 256, D)),
            "bconst": bconst,
            "e16": e16,
            "r84": r84,
            "roped": roped,
            "xb": x_prompt[b],
            "wA": np.ascontiguousarray(w_in0[:, cols]),
            "gmix": g8(norm_mix[0]),
            "rope": rope,
            "hconst": hconst,
            "bt": bt.reshape(128, 256),
            "maskT": maskT,
            "ident": ident,
            "lamv": lamv,
            "subln": subln,
            "xown": np.ascontiguousarray(x_prompt[b, j * 2048:(j + 1) * 2048]),
            "idx": idx,
            "wout": wout_p,
            "wmq": np.asarray(w_mq, f32)[0],
            "wmk": np.asarray(w_mk, f32)[0],
            "wmv": np.asarray(w_mv, f32)[0],
            "wmo": np.asarray(w_mo, f32)[0],
            "wup": np.asarray(w_up, f32)[0],
            "wdn": np.asarray(w_down, f32)[0],
            "gains": gains,
            "nfin": np.asarray(norm_final, f32)[None, :],
            "memp": np.asarray(mem_prompt, f32)[b],
        })
    res = run_bass_kernel_spmd(nc, in_maps, core_ids=list(range(NCORES)))
    R = res.results

    y_prompt = np.empty((2, T, D), f32)
    k_prompt = np.empty((1, 2, NB, 128, 4, 128), f32)
    v_prompt = np.empty((1, 2, NB, 128, 4, 128), f32)
    state_ret_prompt = np.empty((1, 2, 4, 64, 128), f32)
    mem_k_prompt = np.empty((1, 2, 256, 4, 256), f32)
    mem_v_prompt = np.empty((1, 2, 256, 4, 256), f32)
    for c in range(NCORES):
        b, h = c // 4, c % 4
        y_prompt[b, h * 2048:(h + 1) * 2048] = R[c]["yown"]
        k_prompt[0, b, :, :, h, :] = R[c]["kout"].reshape(NB, 128, 128)
        v_prompt[0, b, :, :, h, :] = R[c]["vout"].reshape(NB, 128, 128)
        state_ret_prompt[0, b, h] = R[c]["sret"]
        if h == 0:
            mem_k_prompt[0, b] = R[c]["memk"].reshape(256, 4, 256)
            mem_v_prompt[0, b] = R[c]["memv"].reshape(256, 4, 256)
    y_sample = np.empty((DEC, 1, D), f32)
    k_sample = np.empty((1, DEC, 1, 4, 128), f32)
    v_sample = np.empty((1, DEC, 1, 4, 128), f32)
    state_ret_sample = np.empty((1, DEC, 4, 64, 128), f32)
    for c in range(NCORES):
        y_sample[c * 16:(c + 1) * 16, 0] = R[c]["ys"]
        k_sample[0, c * 16:(c + 1) * 16, 0] = R[c]["ks"].reshape(16, 4, 128)
        v_sample[0, c * 16:(c + 1) * 16, 0] = R[c]["vs"].reshape(16, 4, 128)
        state_ret_sample[0, c * 16:(c + 1) * 16] = R[c]["ss"]
    return (y_prompt, y_sample, k_prompt, v_prompt, state_ret_prompt, mem_k_prompt, mem_v_prompt,
            k_sample, v_sample, state_ret_sample)
```

```python
import math
from contextlib import ExitStack

import numpy as np
import ml_dtypes

import concourse.bass as bass
import concourse.mybir as mybir
from concourse.bass_utils import run_bass_kernel_spmd

F32 = mybir.dt.float32
BF16 = mybir.dt.bfloat16
I32 = mybir.dt.int32
AF = mybir.ActivationFunctionType
ALU = mybir.AluOpType
AX = mybir.AxisListType

NCORES = 8
D = 1024
T = 8192
NB = T // 128
NOWN = 16
EPS = 1e-6
NEG = -30000.0
DEC = 128
PAST = 2048
NPG = PAST // 128
ENGS = ["sp", "act", "dve", "pe", "pool"]


class Prog:
    def __init__(self, nc):
        self.nc = nc
        self.ops = []
        self.last_writer = {}
        self.readers = {}

    def add(self, eng, fn, reads=(), writes=(), slot=None, inc=None):
        def _ps(r):
            return isinstance(r, str) and len(r) >= 3 and r.startswith("ps") and r[2].isdigit()
        writes = list(writes) + [r[:3] for r in reads if _ps(r)]
        writes = [r[:3] if _ps(r) else r for r in writes]
        reads = [r for r in reads if not _ps(r)]
        oid = len(self.ops)
        deps = set()
        for r in reads:
            w = self.last_writer.get(r)
            if w is not None:
                deps.add(w)
        for r in writes:
            w = self.last_writer.get(r)
            if w is not None:
                deps.add(w)
            for rd in self.readers.get(r, {}).values():
                deps.add(rd)
        dma = slot is not None
        rkey = ("dma", slot) if dma else eng
        for r in reads:
            self.readers.setdefault(r, {})[rkey] = oid
        for r in writes:
            self.last_writer[r] = oid
            self.readers[r] = {}
        deps.discard(oid)
        self.ops.append(dict(eng=eng, fn=fn, deps=deps, dma=dma, slot=slot, rw=(list(reads), list(writes)),
                             inc=(inc if inc is not None else (16 if dma else 1))))
        return oid

    def barrier(self):
        allres = list(self.last_writer.keys() | self.readers.keys())
        marks = []
        for e in ("act", "dve", "pool"):
            marks.append(self.add(e, self._bar_fn[e], writes=allres + ["bar_" + e]))
        for e in ENGS:
            self.add(e, None, reads=["bar_act", "bar_dve", "bar_pool"], writes=allres + ["barx_" + e])

    def emit(self, es, limit=None):
        nc = self.nc
        if limit is not None:
            self.ops = self.ops[:limit]
            self.ops.append(dict(eng="sp", fn=None, deps=set(i for i, o in enumerate(self.ops) if o["dma"] and o["fn"] is not None),
                                 dma=False, slot=None, inc=1))
        ops = self.ops
        sig = [False] * len(ops)

        def pruned(D_, o):
            return (D_["eng"] == "pe" and o["eng"] == "pe" and not D_["dma"] and not o["dma"])

        eff = []
        for o in ops:
            s_ = set()
            for d in o["deps"]:
                if ops[d]["fn"] is None:
                    s_ |= eff[d]
                else:
                    s_.add(d)
            eff.append(s_)
        for i, o in enumerate(ops):
            if o["dma"] and o["fn"] is not None:
                sig[i] = True
            for d in eff[i]:
                if pruned(ops[d], o):
                    continue
                sig[d] = True
        cnt = {}
        for i, o in enumerate(ops):
            if not sig[i]:
                o["sv"] = None
                continue
            key = ("dma", o["slot"]) if o["dma"] else o["eng"]
            cnt[key] = cnt.get(key, 0) + o["inc"]
            o["key"] = key
            o["sv"] = cnt[key]
        known = {e: {} for e in ENGS}
        nwaits = 0
        for i, o in enumerate(ops):
            kn = known[o["eng"]]
            need = {}
            for d in eff[i]:
                D_ = ops[d]
                if pruned(D_, o):
                    continue
                k, v = D_["key"], D_["sv"]
                if kn.get(k, 0) >= v:
                    continue
                if need.get(k, (0, None))[0] < v:
                    need[k] = (v, d)
            waits = []
            for k, (v, d) in need.items():
                if kn.get(k, 0) >= v:
                    continue
                waits.append((k, v))
                for kk, vv in ops[d]["vc"].items():
                    if kn.get(kk, 0) < vv:
                        kn[kk] = vv
                if kn.get(k, 0) < v:
                    kn[k] = v
            o["waits"] = waits
            nwaits += len(waits)
            vc = dict(kn)
            if sig[i]:
                vc[o["key"]] = max(vc.get(o["key"], 0), o["sv"])
            o["vc"] = vc
        sems = {}
        for n, k in enumerate(sorted(cnt.keys(), key=str)):
            sems[k] = es.enter_context(nc.semaphore("s%d" % n))
        self.stats = dict(nops=len(ops), nwaits=nwaits, nsems=len(sems))
        block = es.enter_context(nc.Block())

        def run(engname):
            def body(e):
                for o in ops:
                    if o["eng"] != engname:
                        continue
                    for (k, v) in o["waits"]:
                        e.wait_ge(sems[k], v)
                    if o["fn"] is not None:
                        ins = o["fn"](e)
                        if o["sv"] is not None:
                            ins.then_inc(sems[o["key"]], o["inc"])
            return body

        block.sync(run("sp"))
        block.scalar(run("act"))
        block.vector(run("dve"))
        block.tensor(run("pe"))
        block.gpsimd(run("pool"))


class Arena:
    def __init__(self, t, size):
        self.t = t
        self.size = size
        self.off = 0
        self.peak = 0

    def alloc(self, cols, dtype=BF16):
        n = cols * (2 if dtype in (F32, I32) else 1)
        self.off = (self.off + 15) // 16 * 16
        assert self.off + n <= self.size, ("SBUF arena overflow", self.off, n, self.size)
        ap = self.t[:, self.off:self.off + n]
        self.off += n
        self.peak = max(self.peak, self.off)
        if dtype != BF16:
            ap = ap.bitcast(dtype)
        return ap


def build_program(nbA=NB, do_cc=True, do_C=True, nblkC=NOWN, do_attn=True, do_front=True, limit=None, n_phys=2560, do_B=True):
    nc = bass.Bass("TRN2", target_bir_lowering=False)

    def din(name, shape, dt=F32):
        return nc.dram_tensor(name, list(shape), dt, kind="ExternalInput").ap()

    def dout(name, shape, dt=F32):
        return nc.dram_tensor(name, list(shape), dt, kind="ExternalOutput").ap()

    xb = din("xb", [T, D])
    wA = din("wA", [D, 768])
    gmix = din("gmix", [128, 8])
    rope = din("rope", [NB, 128, 64])
    hconst = din("hconst", [128, 8])
    btin = din("bt", [128, 256])
    maskT_d = din("maskT", [128, 128])
    ident_d = din("ident", [128, 128])
    lamv_d = din("lamv", [1, 256])
    subln_d = din("subln", [1, 128])
    xown = din("xown", [NOWN * 128, D])
    idx_d = din("idx", [128, NOWN * 4], I32)
    wout_d = din("wout", [D, D])
    wmq_d = din("wmq", [D, D])
    wmk_d = din("wmk", [D, D])
    wmv_d = din("wmv", [D, D])
    wmo_d = din("wmo", [D, D])
    wup_d = din("wup", [D, 4 * D])
    wdn_d = din("wdn", [4 * D, D])
    gains_d = din("gains", [128, 24])
    nfin_d = din("nfin", [1, D])
    memp_d = din("memp", [256, D])
    xd_d = din("xd", [16, D])
    win_d = din("win", [D, 3072])
    ck_d = din("ck", [n_phys * 128, 512])
    cv_d = din("cv", [n_phys * 128, 512])
    pt_d = din("pt", [1, 256], I32)
    sst_d = din("sst", [16, 4, 64, 128])
    cmk_d = din("cmk", [16, 256, D])
    cmv_d = din("cmv", [16, 256, D])
    bconst_d = din("bconst", [128, 72])
    e16_d = din("e16", [64, 256])
    r84_d = din("r84", [8, 8])
    roped_d = din("roped", [16, 64])
    ys_o = dout("ys", [16, D])
    ks_o = dout("ks", [16, 512])
    vs_o = dout("vs", [16, 512])
    ss_o = dout("ss", [16, 4, 64, 128])
    yown = dout("yown", [NOWN * 128, D])
    kout = dout("kout", [T, 128])
    vout = dout("vout", [T, 128])
    sret = dout("sret", [64, 128])
    memk_o = dout("memk", [256, D])
    memv_o = dout("memv", [256, D])
    Ex = nc.dram_tensor("Ex", [T, 256], BF16)
    Gx = nc.dram_tensor("Gx", [4 * T, 256], BF16)

    es = ExitStack()
    ARENA_COLS = 106000
    arena_t = es.enter_context(nc.sbuf_tensor("arena", [128, ARENA_COLS], BF16))
    PS = es.enter_context(nc.psum_tensor("ps", [128, 8, 512], F32))
    A = Arena(arena_t, ARENA_COLS)
    P = Prog(nc)

    def psf(b, lo=0, hi=512, p0=0, p1=128):
        return PS[p0:p1, b, lo:hi]

    def psb(b):
        return PS[:, b, :].bitcast(BF16)

    ident = A.alloc(128, BF16)
    hc = A.alloc(8, F32)
    epsb = A.alloc(1, F32)
    barscr = A.alloc(8, F32)
    P._bar_fn = {
        "act": lambda e: e.activation(out=barscr[:, 0:1], in_=barscr[:, 1:2], func=AF.Copy),
        "dve": lambda e: e.memset(barscr[:, 2:3], 0.0),
        "pool": lambda e: e.memset(barscr[:, 4:5], 0.0),
    }
    identf = A.alloc(128, F32)
    P.add("sp", lambda e: e.dma_start(out=identf, in_=ident_d), writes=["identf"], slot="ident")
    P.add("dve", lambda e: e.tensor_copy(out=ident, in_=identf), reads=["identf"], writes=["ident"])
    P.add("sp", lambda e: e.dma_start(out=hc, in_=hconst), writes=["hc"], slot="hc")
    P.add("dve", lambda e: e.memset(epsb, EPS), writes=["epsb"])
    P.add("dve", lambda e: e.memset(barscr, 0.0), writes=["barscr"])
    XS = A.alloc(D, F32)
    MD = A.alloc(D, BF16)
    E16 = A.alloc(256, F32)
    r84 = A.alloc(8, F32)
    onesf = A.alloc(2, F32)
    selq = A.alloc(128, F32)
    hid_s = A.alloc(16, F32)
    OD4 = A.alloc(D, F32)
    Wsel4 = A.alloc(16, F32)
    P.add("sp", lambda e: e.dma_start(out=E16[0:64], in_=e16_d), writes=["E16"], slot="E16")
    P.add("sp", lambda e: e.dma_start(out=r84[0:8], in_=r84_d), writes=["r84"], slot="r84")
    P.add("pool", lambda e: e.memset(onesf, 1.0), writes=["onesf"])
    P.add("pool", lambda e: e.memset(XS, 0.0), writes=["XS"])
    P.add("pool", lambda e: e.memset(MD, 0.0), writes=["MD"])
    P.add("sp", lambda e: e.dma_start(out=XS[0:16], in_=xd_d), writes=["XS"], slot="XS")
    phase_base = A.off

    def rstd_ops(src_ap, src_res, n, junk, junk_res, ssq, lnv, rstd, tag):
        P.add("act", lambda e: e.activation(out=junk, in_=src_ap, func=AF.Square, accum_out=ssq),
              reads=[src_res], writes=[junk_res, tag + "ssq"])
        P.add("act", lambda e: e.activation(out=lnv, in_=ssq, func=AF.Ln, bias=epsb, scale=1.0 / n),
              reads=[tag + "ssq", "epsb"], writes=[tag + "lnv"])
        P.add("act", lambda e: e.activation(out=rstd, in_=lnv, func=AF.Exp, scale=-0.5),
              reads=[tag + "lnv"], writes=[tag + "rstd"])

    lw_state = {"n": 0}

    def load_weight(dst, dst_res, src, K, N, stage, gain=None, gain_res=None, eng="pool"):
        CH = stage[2]
        srcv = src.rearrange("(k p) n -> p k n", p=128)
        for k0 in range(0, K, 8):
            for n0 in range(0, N, CH):
                nn = min(CH, N - n0)
                kk = min(8, K - k0)
                si = lw_state["n"] % 2
                ce = "act" if (lw_state["n"] % 2 == 0) else "pool"
                lw_state["n"] += 1
                sres = "wstage%d" % si
                stv = stage[si][:, 0:kk * nn].rearrange("p (k n) -> p k n", k=kk)
                P.add("sp", lambda e, stv=stv, k0=k0, kk=kk, n0=n0, nn=nn: e.dma_start(
                    out=stv, in_=srcv[:, k0:k0 + kk, n0:n0 + nn]), writes=[sres], slot=sres)
                if gain is None:
                    if ce == "act":
                        P.add("act", lambda e, stv=stv, k0=k0, kk=kk, n0=n0, nn=nn: e.activation(
                            out=dst[:, k0:k0 + kk, n0:n0 + nn], in_=stv, func=AF.Copy), reads=[sres], writes=[dst_res])
                    else:
                        P.add("pool", lambda e, stv=stv, k0=k0, kk=kk, n0=n0, nn=nn: e.tensor_copy(
                            out=dst[:, k0:k0 + kk, n0:n0 + nn], in_=stv), reads=[sres], writes=[dst_res])
                else:
                    for k in range(kk):
                        if ce == "act":
                            P.add("act", lambda e, stv=stv, k=k, k0=k0, n0=n0, nn=nn: e.activation(
                                out=dst[:, k0 + k, n0:n0 + nn], in_=stv[:, k, :], func=AF.Copy, scale=gain[:, k0 + k:k0 + k + 1]),
                                reads=[sres, gain_res], writes=[dst_res])
                        else:
                            P.add("pool", lambda e, stv=stv, k=k, k0=k0, n0=n0, nn=nn: e.tensor_scalar(
                                out=dst[:, k0 + k, n0:n0 + nn], in0=stv[:, k, :],
                                scalar1=gain[:, k0 + k:k0 + k + 1], scalar2=1.0, op0=ALU.mult, op1=ALU.mult),
                                reads=[sres, gain_res], writes=[dst_res])

    wstage = [A.alloc(8 * 256, F32), A.alloc(8 * 256, F32), 256]
    WA = A.alloc(8 * 768, BF16).rearrange("p (k n) -> p k n", k=8)
    gm = A.alloc(8, F32)
    QK = A.alloc(2 * T, BF16).rearrange("p (m t) -> p m t", m=2)
    VX = A.alloc(NB * 130, BF16).rearrange("p (t c) -> p t c", t=NB)
    XB = [A.alloc(D, F32) for _ in range(2)]
    RP = [A.alloc(64, F32) for _ in range(2)]
    sqj = A.alloc(D, BF16)
    xn = A.alloc(D, BF16)
    hpT = A.alloc(D, BF16).rearrange("p (k t) -> p k t", k=8)
    KV = [A.alloc(256, F32) for _ in range(2)]
    qkbf = A.alloc(256, BF16)
    vrbf = A.alloc(128, BF16)
    sg1 = A.alloc(128, F32)
    sg = A.alloc(128, F32)
    qkr = A.alloc(128, F32).rearrange("p (m d) -> p m d", m=2)
    rt = [A.alloc(64, F32).rearrange("p (m d) -> p m d", m=2) for _ in range(4)]
    rot = A.alloc(128, F32).rearrange("p (m d) -> p m d", m=2)
    qkp = A.alloc(128, BF16).rearrange("p (m d) -> p m d", m=2)
    qkT = A.alloc(256, BF16).rearrange("p (m t) -> p m t", m=2)
    ptr = A.alloc(128, BF16)
    Sst = A.alloc(128, F32)
    Stt = A.alloc(128, F32)
    Sbf = A.alloc(128, BF16)
    sm = A.alloc(16, F32)
    bt = A.alloc(256, F32).rearrange("p (m t) -> p m t", m=2)
    maskT = A.alloc(128, F32)
    lamb = A.alloc(256, F32)
    lamt = A.alloc(128, F32)
    lam8 = A.alloc(8, F32)
    sub8 = A.alloc(128, F32)
    PT = [[A.alloc(512, BF16) for _ in range(2)] for _ in range(2)]
    tmpn = [[A.alloc(128, F32) for _ in range(2)] for _ in range(2)]
    oa = A.alloc(128, F32)
    oa2 = A.alloc(128, F32)
    MRG = [A.alloc(256, BF16) for _ in range(2)]

    P.add("sp", lambda e: e.dma_start(out=gm, in_=gmix), writes=["gm"], slot="gm")
    load_weight(WA, "WA", wA, 8, 768, wstage, gain=gm, gain_res="gm", eng="dve")
    P.add("sp", lambda e: e.dma_start(out=bt.rearrange("p m t -> p (m t)"), in_=btin), writes=["bt"], slot="bt")
    P.add("sp", lambda e: e.dma_start(out=maskT, in_=maskT_d), writes=["maskT"], slot="maskT")
    P.add("sp", lambda e: e.dma_start(out=lamb, in_=lamv_d.partition_broadcast(128)), writes=["lamb"], slot="lamb")
    P.add("sp", lambda e: e.dma_start(out=sub8, in_=subln_d.partition_broadcast(128)), writes=["sub8"], slot="sub8")
    P.add("dve", lambda e: e.tensor_scalar(out=sub8, in0=sub8, scalar1=0.8, scalar2=None, op0=ALU.mult),
          reads=["sub8"], writes=["sub8"])
    P.add("dve", lambda e: e.tensor_tensor(out=lamt[:, 0:64], in0=lamb[:, 0:64], in1=lamb[:, 64:128], op=ALU.mult),
          reads=["lamb"], writes=["lamt"])
    P.add("dve", lambda e: e.tensor_tensor(out=lamt[:, 64:128], in0=lamb[:, 128:192], in1=lamb[:, 192:256], op=ALU.mult),
          reads=["lamb"], writes=["lamt"])
    P.add("dve", lambda e: e.tensor_reduce(out=lam8[:, 0:2], in_=lamt.rearrange("p (a b) -> p a b", a=2),
                                           axis=AX.X, op=ALU.add), reads=["lamt"], writes=["lam8"])
    P.add("act", lambda e: e.activation(out=lam8[:, 2:4], in_=lam8[:, 0:2], func=AF.Exp), reads=["lam8"], writes=["lam8"])
    P.add("dve", lambda e: e.tensor_tensor(out=lam8[:, 4:5], in0=lam8[:, 3:4], in1=lam8[:, 2:3], op=ALU.subtract),
          reads=["lam8"], writes=["lam8"])
    P.add("dve", lambda e: e.tensor_scalar(out=lam8[:, 5:6], in0=lam8[:, 4:5], scalar1=-0.2, scalar2=None, op0=ALU.add),
          reads=["lam8"], writes=["lam8"])
    neglam = lam8[:, 5:6]
    P.add("pool", lambda e: e.memset(VX[:, :, 128:130], 1.0), writes=["VXones"])
    P.add("dve", lambda e: e.memset(Sst, 0.0), writes=["S"])
    P.add("dve", lambda e: e.memset(Sbf, 0.0), writes=["Sbf"])

    qsc, ksc, gC, cfar = hc[:, 0:1], hc[:, 1:2], hc[:, 2:3], hc[:, 3:4]

    def attention(t, mi):
        nk = t + 1
        groups = [list(range(g0, min(g0 + 4, nk))) for g0 in range(0, nk, 4)]
        o1 = psf(7, 0, 129)
        o2 = psf(4, 256, 385)
        sbanks = [(5, 6), (0, 1)]

        def qk(gi):
            b1, b2 = sbanks[gi % 2]
            for j, kb in enumerate(groups[gi]):
                P.add("pe", lambda e, j=j, kb=kb, b1=b1: e.matmul(
                    psf(b1, j * 128, (j + 1) * 128), lhsT=QK[0:64, 1, kb * 128:(kb + 1) * 128],
                    rhs=QK[0:64, 0, t * 128:(t + 1) * 128], start=True, stop=True),
                    reads=[("QK", kb), ("QK", t)], writes=["ps%d" % b1])
                P.add("pe", lambda e, j=j, kb=kb, b2=b2: e.matmul(
                    psf(b2, j * 128, (j + 1) * 128), lhsT=QK[64:128, 1, kb * 128:(kb + 1) * 128],
                    rhs=QK[64:128, 0, t * 128:(t + 1) * 128], start=True, stop=True),
                    reads=[("QK", kb), ("QK", t)], writes=["ps%d" % b2])

        def soft(gi):
            bb = sbanks[gi % 2]
            buf = gi % 2
            kbs = groups[gi]
            nf = sum(1 for kb in kbs if kb <= t - 2)
            for m in range(2):
                b = bb[m]
                pt = PT[m][buf]
                ptres = "pt%d%d" % (m, buf)
                if nf > 0:
                    P.add("act", lambda e, b=b, pt=pt, nf=nf: e.activation(
                        out=pt[:, 0:nf * 128], in_=psf(b, 0, nf * 128), func=AF.Exp, bias=cfar, scale=0.125),
                        reads=["ps%d" % b, "hc"], writes=[ptres])
                for j, kb in enumerate(kbs):
                    if kb <= t - 2:
                        continue
                    w = kb - (t - 1)
                    tm = tmpn[m][w]
                    tres = "tmpn%d%d" % (m, w)
                    P.add("dve", lambda e, b=b, j=j, w=w, tm=tm: e.scalar_tensor_tensor(
                        out=tm, in0=psf(b, j * 128, (j + 1) * 128), scalar=0.125, in1=bt[:, w, :],
                        op0=ALU.mult, op1=ALU.add), reads=["ps%d" % b, "bt"], writes=[tres])
                    P.add("act", lambda e, j=j, tm=tm, pt=pt: e.activation(
                        out=pt[:, j * 128:(j + 1) * 128], in_=tm, func=AF.Exp), reads=[tres], writes=[ptres])

        def pv(gi):
            buf = gi % 2
            for j, kb in enumerate(groups[gi]):
                P.add("pe", lambda e, j=j, kb=kb, buf=buf: e.matmul(
                    o1, lhsT=PT[0][buf][:, j * 128:(j + 1) * 128], rhs=VX[:, kb, 0:129],
                    start=(kb == 0), stop=(kb == t)),
                    reads=["pt0%d" % buf, ("V", kb), "VXones"], writes=["ps7"])
                P.add("pe", lambda e, j=j, kb=kb, buf=buf: e.matmul(
                    o2, lhsT=PT[1][buf][:, j * 128:(j + 1) * 128], rhs=VX[:, kb, 0:129],
                    start=(kb == 0), stop=(kb == t)),
                    reads=["pt1%d" % buf, ("V", kb), "VXones"], writes=["ps4b"])

        ng = len(groups)
        for gi in range(ng):
            qk(gi)
            soft(gi)
            if gi > 0:
                pv(gi - 1)
        pv(ng - 1)
        r1, r2 = sm[:, 4:5], sm[:, 5:6]
        P.add("dve", lambda e: e.reciprocal(out=r1, in_=psf(7, 128, 129)), reads=["ps7"], writes=["r1"])
        P.add("dve", lambda e: e.reciprocal(out=r2, in_=psf(4, 384, 385)), reads=["ps4b"], writes=["r2"])
        P.add("dve", lambda e: e.tensor_tensor(out=r2, in0=r2, in1=neglam, op=ALU.mult), reads=["r2", "lam8"], writes=["r2"])
        P.add("dve", lambda e: e.tensor_scalar(out=oa, in0=psf(7, 0, 128), scalar1=r1, scalar2=None, op0=ALU.mult),
              reads=["ps7", "r1"], writes=["oa"])
        P.add("dve", lambda e: e.scalar_tensor_tensor(out=oa2, in0=psf(4, 256, 384), scalar=r2, in1=oa,
                                                      op0=ALU.mult, op1=ALU.add),
              reads=["ps4b", "r2", "oa"], writes=["oa2"])
        rstd_ops(oa2, "oa2", 128, sqj[:, 0:128], "sqj", sm[:, 6:7], sm[:, 7:8], sm[:, 8:9], "a_")
        P.add("dve", lambda e: e.scalar_tensor_tensor(out=MRG[mi][:, 0:128], in0=oa2, scalar=sm[:, 8:9], in1=sub8,
                                                      op0=ALU.mult, op1=ALU.mult),
              reads=["oa2", "a_rstd", "sub8"], writes=["mrg%d" % mi])

    def block_front(t):
        i = t % 2
        x = XB[i]
        xr = "x%d" % i
        P.add("sp", lambda e: e.dma_start(out=x, in_=xb[t * 128:(t + 1) * 128, :]), writes=[xr], slot=xr)
        P.add("sp", lambda e: e.dma_start(out=RP[i], in_=rope[t]), writes=["rp%d" % i], slot="rp%d" % i)
        rstd_ops(x, xr, D, sqj, "sqj", sm[:, 0:1], sm[:, 1:2], sm[:, 2:3], "x_")
        P.add("dve", lambda e: e.tensor_scalar(out=xn, in0=x, scalar1=sm[:, 2:3], scalar2=None, op0=ALU.mult),
              reads=[xr, "x_rstd"], writes=["xn"])
        for k in range(8):
            P.add("pe", lambda e, k=k: e.transpose(out=psb(0)[:, k * 128:(k + 1) * 128], in_=xn[:, k * 128:(k + 1) * 128],
                                                   identity=ident), reads=["xn", "ident"], writes=["ps0"])
        P.add("act", lambda e: e.activation(out=hpT.rearrange("p k t -> p (k t)"), in_=psb(0), func=AF.Copy),
              reads=["ps0"], writes=["hpT"])
        for k in range(8):
            P.add("pe", lambda e, k=k: e.matmul(psf(1), lhsT=hpT[:, k, :], rhs=WA[:, k, 0:512], start=(k == 0), stop=(k == 7)),
                  reads=["hpT", "WA"], writes=["ps1"])
        for k in range(8):
            P.add("pe", lambda e, k=k: e.matmul(psf(2, 0, 256), lhsT=hpT[:, k, :], rhs=WA[:, k, 512:768], start=(k == 0), stop=(k == 7)),
                  reads=["hpT", "WA"], writes=["ps2"])
        kv = KV[i]
        P.add("dve", lambda e: e.tensor_copy(out=kv, in_=psf(1, 128, 384)), reads=["ps1"], writes=["kv%d" % i])
        P.add("sp", lambda e: e.dma_start(out=kout[t * 128:(t + 1) * 128, :], in_=kv[:, 0:128]),
              reads=["kv%d" % i], writes=[("kout", t)], slot="kv%d" % i)
        P.add("sp", lambda e: e.dma_start(out=vout[t * 128:(t + 1) * 128, :], in_=kv[:, 128:256]),
              reads=["kv%d" % i], writes=[("vout", t)], slot="kv%d" % i)
        P.add("dve", lambda e: e.tensor_copy(out=qkbf, in_=psf(1, 0, 256)), reads=["ps1"], writes=["qkbf"])
        P.add("act", lambda e: e.activation(out=vrbf, in_=psf(1, 384, 512), func=AF.Copy), reads=["ps1"], writes=["vrbf"])
        P.add("act", lambda e: e.activation(out=VX[:, t, 0:128], in_=psf(1, 256, 384), func=AF.Copy),
              reads=["ps1"], writes=[("V", t)])
        for m in range(2):
            P.add("pe", lambda e, m=m: e.transpose(out=psb(3)[:, m * 128:(m + 1) * 128], in_=qkbf[:, m * 128:(m + 1) * 128],
                                                   identity=ident), reads=["qkbf", "ident"], writes=["ps3A"])
        P.add("dve", lambda e: e.tensor_copy(out=QK[:, :, t * 128:(t + 1) * 128],
                                             in_=psb(3)[:, 0:256].rearrange("p (m t) -> p m t", m=2)),
              reads=["ps3A"], writes=[("QK", t)])
        P.add("act", lambda e: e.activation(out=sg1, in_=psf(2, 0, 128), func=AF.Exp, scale=-1.0), reads=["ps2"], writes=["sg1"])
        P.add("dve", lambda e: e.tensor_scalar(out=sg1, in0=sg1, scalar1=1.0, scalar2=None, op0=ALU.add), reads=["sg1"], writes=["sg1"])
        P.add("dve", lambda e: e.reciprocal(out=sg1, in_=sg1), reads=["sg1"], writes=["sg1"])
        P.add("dve", lambda e: e.tensor_tensor(out=sg, in0=psf(2, 0, 128), in1=sg1, op=ALU.mult), reads=["ps2", "sg1"], writes=["sg"])
        P.add("dve", lambda e: e.tensor_copy(out=qkr.rearrange("p m d -> p (m d)"), in_=psf(2, 128, 256)), reads=["ps2"], writes=["qkr"])
        cosb = RP[i][:, 0:32].unsqueeze(1).to_broadcast([128, 2, 32])
        sinb = RP[i][:, 32:64].unsqueeze(1).to_broadcast([128, 2, 32])
        x1, x2 = qkr[:, :, 0:32], qkr[:, :, 32:64]
        rpr = "rp%d" % i
        P.add("pool", lambda e: e.tensor_tensor(out=rt[0], in0=x1, in1=cosb, op=ALU.mult), reads=["qkr", rpr], writes=["rt0"])
        P.add("pool", lambda e: e.tensor_tensor(out=rt[1], in0=x2, in1=sinb, op=ALU.mult), reads=["qkr", rpr], writes=["rt1"])
        P.add("pool", lambda e: e.tensor_tensor(out=rt[2], in0=x2, in1=cosb, op=ALU.mult), reads=["qkr", rpr], writes=["rt2"])
        P.add("pool", lambda e: e.tensor_tensor(out=rt[3], in0=x1, in1=sinb, op=ALU.mult), reads=["qkr", rpr], writes=["rt3"])
        P.add("pool", lambda e: e.tensor_tensor(out=rot[:, :, 0:32], in0=rt[0], in1=rt[1], op=ALU.subtract),
              reads=["rt0", "rt1"], writes=["rot"])
        P.add("pool", lambda e: e.tensor_tensor(out=rot[:, :, 32:64], in0=rt[2], in1=rt[3], op=ALU.add),
              reads=["rt2", "rt3"], writes=["rot"])
        P.add("dve", lambda e: e.tensor_scalar(out=qkp[:, 0, :], in0=rot[:, 0, :], scalar1=qsc, scalar2=None, op0=ALU.mult),
              reads=["rot", "hc"], writes=["qkp"])
        P.add("dve", lambda e: e.tensor_scalar(out=qkp[:, 1, :], in0=rot[:, 1, :], scalar1=ksc, scalar2=None, op0=ALU.mult),
              reads=["rot", "hc"], writes=["qkp"])
        for m in range(2):
            P.add("pe", lambda e, m=m: e.transpose(out=psb(3)[0:64, 256 + m * 128:256 + (m + 1) * 128], in_=qkp[:, m, :],
                                                   identity=ident), reads=["qkp", "ident"], writes=["ps3B"])
        P.add("act", lambda e: e.activation(out=qkT[0:64].rearrange("p m t -> p (m t)"), in_=psb(3)[0:64, 256:512], func=AF.Copy),
              reads=["ps3B"], writes=["qkT"])
        P.add("pe", lambda e: e.matmul(psf(3, 256, 384), lhsT=qkT[0:64, 1, :], rhs=qkT[0:64, 0, :], start=True, stop=True),
              reads=["qkT"], writes=["ps3C"])
        P.add("dve", lambda e: e.tensor_tensor(out=ptr, in0=psf(3, 256, 384), in1=maskT, op=ALU.mult),
              reads=["ps3C", "maskT"], writes=["ptr"])
        P.add("pe", lambda e: e.matmul(psf(4, 0, 128), lhsT=ptr, rhs=vrbf, start=True, stop=False),
              reads=["ptr", "vrbf"], writes=["ps4a"])
        P.add("pe", lambda e: e.matmul(psf(4, 0, 128), lhsT=qkT[0:64, 0, :], rhs=Sbf[0:64, :], start=False, stop=True),
              reads=["qkT", "Sbf"], writes=["ps4a"])
        P.add("pe", lambda e: e.matmul(psf(3, 384, 512, 0, 64), lhsT=qkp[:, 1, :], rhs=vrbf, start=True, stop=True),
              reads=["qkp", "vrbf"], writes=["ps3D"])
        P.add("dve", lambda e: e.tensor_tensor(out=Stt[0:64], in0=Sst[0:64], in1=psf(3, 384, 512, 0, 64), op=ALU.add),
              reads=["S", "ps3D"], writes=["Stt"])
        P.add("dve", lambda e: e.tensor_scalar(out=Sst[0:64], in0=Stt[0:64], scalar1=gC[0:64], scalar2=None, op0=ALU.mult),
              reads=["Stt", "hc"], writes=["S"])
        P.add("act", lambda e: e.activation(out=Sbf[0:64], in_=Sst[0:64], func=AF.Copy), reads=["S"], writes=["Sbf"])
        rstd_ops(psf(4, 0, 128), "ps4a", 128, sqj[:, 128:256], "sqj", sm[:, 9:10], sm[:, 10:11], sm[:, 11:12], "r_")
        P.add("dve", lambda e: e.scalar_tensor_tensor(out=MRG[i][:, 128:256], in0=psf(4, 0, 128), scalar=sm[:, 11:12], in1=sg,
                                                      op0=ALU.mult, op1=ALU.mult),
              reads=["ps4a", "r_rstd", "sg"], writes=["mrg%d" % i])

    for t in range(nbA):
        if do_front:
            block_front(t)
        if do_attn:
            attention(t, t % 2)
        mi = t % 2
        P.add("sp", lambda e, t=t, mi=mi: e.dma_start(out=Ex.ap()[t * 128:(t + 1) * 128, :], in_=MRG[mi]),
              reads=["mrg%d" % mi], writes=[("Ex", t)], slot="mrg%d" % mi)
        if do_cc and t % 16 == 15:
            q = t // 16
            P.add("pool", lambda e, q=q: e.collective_compute(
                "AllGather", ALU.bypass, replica_groups=[[0, 1, 2, 3], [4, 5, 6, 7]],
                ins=[Ex.ap()[q * 2048:(q + 1) * 2048, :].opt()], outs=[Gx.ap()[q * 8192:(q + 1) * 8192, :].opt()]),
                reads=[("Ex", tt) for tt in range(q * 16, q * 16 + 16)], writes=[("Gx", q)], slot="cc", inc=1)
    P.add("sp", lambda e: e.dma_start(out=sret, in_=Sst[0:64]), reads=["S"], writes=["sret"], slot="sret")
    peakA = A.peak

    if do_B:
        zd = A.alloc(3072, F32)
        smb = A.alloc(32, F32)
        sq16 = A.alloc(512, F32)
        a16 = A.alloc(512, F32)
        roped = A.alloc(64, F32)
        b1_base = A.off
        wtB = A.alloc(8 * 512, BF16).rearrange("p (k n) -> p k n", k=8)
        xdb = A.alloc(D, BF16)
        hdT = A.alloc(D, BF16).rearrange("p (k t) -> p k t", k=8)
        qb = A.alloc(512, F32)
        NKB = 8
        Kt = [A.alloc(512, F32) for _ in range(NKB)]
        Vt = [A.alloc(512, F32) for _ in range(NKB)]
        KN = A.alloc(512, F32)
        VN = A.alloc(512, F32)
        prod = A.alloc(512, F32)
        Sal = A.alloc(17 * 8, F32)
        Pal = A.alloc(17 * 8, F32)
        bcs = A.alloc(72, F32)
        ptb = A.alloc(256, I32)
        idxp = A.alloc(256, I32)
        selt = A.alloc(128, F32)
        OD = A.alloc(512, F32)
        Wsel = A.alloc(16, F32)
        peakB = A.off

        P.add("sp", lambda e: e.dma_start(out=bcs, in_=bconst_d), writes=["bcs"], slot="bcs")
        P.add("sp", lambda e: e.dma_start(out=ptb, in_=pt_d.partition_broadcast(128)), writes=["ptb"], slot="ptb")
        P.add("sp", lambda e: e.dma_start(out=roped[0:16], in_=roped_d), writes=["roped"], slot="roped")
        P.add("dve", lambda e: e.tensor_scalar(out=idxp, in0=ptb, scalar1=128.0, scalar2=bcs[:, 0:1], op0=ALU.mult, op1=ALU.add),
              reads=["ptb", "bcs"], writes=["idxp"])
        P.add("pool", lambda e: e.memset(KN, 0.0), writes=["KN"])
        P.add("pool", lambda e: e.memset(VN, 0.0), writes=["VN"])
        rstd_ops(XS, "XS", D, sqj, "sqj", smb[:, 0:1], smb[:, 1:2], smb[:, 2:3], "d_")
        P.add("dve", lambda e: e.tensor_scalar(out=xdb, in0=XS, scalar1=smb[:, 2:3], scalar2=None, op0=ALU.mult),
              reads=["XS", "d_rstd"], writes=["xdb"])
        for k in range(8):
            P.add("pe", lambda e, k=k: e.transpose(out=psb(7)[:, k * 128:(k + 1) * 128], in_=xdb[:, k * 128:(k + 1) * 128], identity=ident),
                  reads=["xdb", "ident"], writes=["ps7"])
        P.add("act", lambda e: e.activation(out=hdT.rearrange("p k t -> p (k t)"), in_=psb(7), func=AF.Copy), reads=["ps7"], writes=["hdT"])
        for ci in range(6):
            load_weight(wtB, "wtB", win_d[:, ci * 512:(ci + 1) * 512], 8, 512, wstage, gain=gm, gain_res="gm", eng="pool")
            for k in range(8):
                P.add("pe", lambda e, k=k: e.matmul(psf(4, 0, 512, 0, 16), lhsT=hdT[:, k, 0:16], rhs=wtB[:, k, :], start=(k == 0), stop=(k == 7)),
                      reads=["hdT", "wtB"], writes=["ps4"])
            P.add("dve", lambda e, ci=ci: e.tensor_copy(out=zd[0:16, ci * 512:(ci + 1) * 512], in_=psf(4, 0, 512, 0, 16)),
                  reads=["ps4"], writes=["zd"])
        P.add("sp", lambda e: e.dma_start(out=ks_o, in_=zd[0:16, 512:1024]), reads=["zd"], writes=["ks_o"], slot="zdo")
        P.add("sp", lambda e: e.dma_start(out=vs_o, in_=zd[0:16, 1024:1536]), reads=["zd"], writes=["vs_o"], slot="zdo")
        coef8 = smb[:, 3:4]
        P.add("dve", lambda e: e.scalar_tensor_tensor(out=coef8[0:8], in0=r84[0:8, 5:6], scalar=neglam[0:8], in1=r84[0:8, 4:5],
                                                      op0=ALU.mult, op1=ALU.add), reads=["r84", "lam8"], writes=["coef8"])
        dm8 = r84[0:8, 0:4].unsqueeze(2).to_broadcast([8, 4, 128])
        biasd = bcs[:, 1:69].rearrange("p (g h) -> p g h", g=17).unsqueeze(3).to_broadcast([128, 17, 4, 2])
        for n in range(16):
            P.add("dve", lambda e, n=n: e.tensor_copy(out=selt[0:16], in_=identf[0:16, n:n + 1].to_broadcast([16, 128])),
                  reads=["identf"], writes=["selt"])
            P.add("pe", lambda e: e.matmul(psf(0), lhsT=selt[0:16], rhs=zd[0:16, 0:512], start=True, stop=True),
                  reads=["selt", "zd"], writes=["ps0"])
            P.add("act", lambda e: e.activation(out=qb, in_=psf(0), func=AF.Copy, scale=0.125), reads=["ps0"], writes=["qb"])
            P.add("sp", lambda e, n=n: e.dma_start(out=KN[0:1, :], in_=zd[n:n + 1, 512:1024]), reads=["zd"], writes=["KN"], slot="KN")
            P.add("sp", lambda e, n=n: e.dma_start(out=VN[0:1, :], in_=zd[n:n + 1, 1024:1536]), reads=["zd"], writes=["VN"], slot="VN")
            for g in range(17):
                if g < 16:
                    kb, kres = Kt[g % NKB], "kt%d" % (g % NKB)
                    P.add("pool", lambda e, kb=kb, n=n, g=g: e.indirect_dma_start(
                        out=kb, out_offset=None, in_=ck_d,
                        in_offset=bass.IndirectOffsetOnAxis(ap=idxp[:, n * 16 + g:n * 16 + g + 1], axis=0)),
                        reads=["idxp"], writes=[kres], slot=kres)
                else:
                    kb, kres = KN, "KN"
                P.add("dve", lambda e, kb=kb: e.tensor_tensor(out=prod, in0=kb, in1=qb, op=ALU.mult), reads=[kres, "qb"], writes=["prod"])
                P.add("dve", lambda e, g=g: e.tensor_reduce(out=Sal[:, g * 8:(g + 1) * 8], in_=prod.rearrange("p (a d) -> p a d", a=8),
                                                            axis=AX.X, op=ALU.add), reads=["prod"], writes=["Sal"])
            Sal4 = Sal.rearrange("p (g h m) -> p g h m", g=17, h=4)
            P.add("dve", lambda e: e.tensor_tensor(out=Sal4, in0=Sal4, in1=biasd, op=ALU.add), reads=["Sal", "bcs"], writes=["Sal"])
            P.add("act", lambda e: e.activation(out=Pal, in_=Sal, func=AF.Exp), reads=["Sal"], writes=["Pal"])
            for g in range(17):
                if g < 16:
                    vb, vres = Vt[g % NKB], "vt%d" % (g % NKB)
                    P.add("pool", lambda e, vb=vb, n=n, g=g: e.indirect_dma_start(
                        out=vb, out_offset=None, in_=cv_d,
                        in_offset=bass.IndirectOffsetOnAxis(ap=idxp[:, n * 16 + g:n * 16 + g + 1], axis=0)),
                        reads=["idxp"], writes=[vres], slot=vres)
                else:
                    vb, vres = VN, "VN"
                P.add("pe", lambda e, g=g, vb=vb: e.matmul(psf(1, 0, 512, 0, 8), lhsT=Pal[:, g * 8:(g + 1) * 8], rhs=vb, start=(g == 0), stop=(g == 16)),
                      reads=["Pal", vres], writes=["ps1"])
                P.add("pe", lambda e, g=g: e.matmul(psf(2, 0, 1, 0, 8), lhsT=Pal[:, g * 8:(g + 1) * 8], rhs=onesf[:, 0:1], start=(g == 0), stop=(g == 16)),
                      reads=["Pal", "onesf"], writes=["ps2"])
            w8 = smb[:, 4:5]
            P.add("dve", lambda e: e.reciprocal(out=w8[0:8], in_=psf(2, 0, 1, 0, 8)), reads=["ps2"], writes=["w8"])
            P.add("dve", lambda e: e.tensor_tensor(out=w8[0:8], in0=w8[0:8], in1=coef8[0:8], op=ALU.mult), reads=["w8", "coef8"], writes=["w8"])
            P.add("dve", lambda e: e.tensor_tensor(out=OD[0:8].rearrange("p (h d) -> p h d", h=4),
                                                   in0=psf(1, 0, 512, 0, 8).rearrange("p (h d) -> p h d", h=4), in1=dm8, op=ALU.mult),
                  reads=["ps1", "r84"], writes=["OD"])
            P.add("dve", lambda e, n=n: e.tensor_scalar(out=Wsel[0:8], in0=E16[0:8, n * 16:(n + 1) * 16], scalar1=w8[0:8], scalar2=None, op0=ALU.mult),
                  reads=["E16", "w8"], writes=["Wsel"])
            P.add("pe", lambda e, n=n: e.matmul(psf(3, 0, 512, 0, 16), lhsT=Wsel[0:8], rhs=OD[0:8], start=(n == 0), stop=(n == 15)),
                  reads=["Wsel", "OD"], writes=["ps3"])
        MD4 = MD[0:16].rearrange("p (h c) -> p h c", h=4)
        P.add("act", lambda e: e.activation(out=sq16[0:16], in_=psf(3, 0, 512, 0, 16), func=AF.Square), reads=["ps3"], writes=["sq16"])
        P.add("dve", lambda e: e.tensor_reduce(out=smb[0:16, 8:12], in_=sq16[0:16].rearrange("p (h d) -> p h d", h=4), axis=AX.X, op=ALU.add),
              reads=["sq16"], writes=["ms4"])
        P.add("act", lambda e: e.activation(out=smb[0:16, 12:16], in_=smb[0:16, 8:12], func=AF.Ln, bias=epsb[0:16], scale=1.0 / 128),
              reads=["ms4", "epsb"], writes=["l4"])
        P.add("act", lambda e: e.activation(out=smb[0:16, 16:20], in_=smb[0:16, 12:16], func=AF.Exp, scale=-0.5), reads=["l4"], writes=["r4"])
        P.add("dve", lambda e: e.tensor_tensor(out=a16[0:16].rearrange("p (h d) -> p h d", h=4),
                                               in0=psf(3, 0, 512, 0, 16).rearrange("p (h d) -> p h d", h=4),
                                               in1=smb[0:16, 16:20].unsqueeze(2).to_broadcast([16, 4, 128]), op=ALU.mult),
              reads=["ps3", "r4"], writes=["a16"])
        P.add("dve", lambda e: e.tensor_tensor(out=MD4[:, :, 0:128], in0=a16[0:16].rearrange("p (h d) -> p h d", h=4),
                                               in1=sub8[0:16].unsqueeze(1).to_broadcast([16, 4, 128]), op=ALU.mult),
              reads=["a16", "sub8"], writes=["MD"])
        P.barrier()
        A.off = b1_base
        rtd = [A.alloc(8 * 32, F32).rearrange("p (a d) -> p a d", a=8) for _ in range(4)]
        rotd = A.alloc(512, F32).rearrange("p (a d) -> p a d", a=8)
        prd = A.alloc(256, F32)
        ord16 = A.alloc(512, F32).rearrange("p (h d) -> p h d", h=4)
        qTd = A.alloc(64, F32)
        QZ = A.alloc(4 * 16 * 16, F32).rearrange("p (h n c) -> p h n c", h=4, n=16)
        Sn = [A.alloc(512, F32).rearrange("p (h d) -> p h d", h=4) for _ in range(2)]
        Snw = [A.alloc(512, F32).rearrange("p (h d) -> p h d", h=4) for _ in range(2)]
        VZn = A.alloc(512, F32)
        sgd = A.alloc(512, F32)
        qkd = zd[0:16, 1536:2048].rearrange("p (a d) -> p a d", a=8)
        cosd = roped[0:16, 0:32].unsqueeze(1).to_broadcast([16, 8, 32])
        sind = roped[0:16, 32:64].unsqueeze(1).to_broadcast([16, 8, 32])
        d1, d2 = qkd[:, :, 0:32], qkd[:, :, 32:64]
        P.add("dve", lambda e: e.tensor_tensor(out=rtd[0][0:16], in0=d1, in1=cosd, op=ALU.mult), reads=["zd", "roped"], writes=["rtd0"])
        P.add("dve", lambda e: e.tensor_tensor(out=rtd[1][0:16], in0=d2, in1=sind, op=ALU.mult), reads=["zd", "roped"], writes=["rtd1"])
        P.add("dve", lambda e: e.tensor_tensor(out=rtd[2][0:16], in0=d2, in1=cosd, op=ALU.mult), reads=["zd", "roped"], writes=["rtd2"])
        P.add("dve", lambda e: e.tensor_tensor(out=rtd[3][0:16], in0=d1, in1=sind, op=ALU.mult), reads=["zd", "roped"], writes=["rtd3"])
        P.add("dve", lambda e: e.tensor_tensor(out=rotd[0:16, :, 0:32], in0=rtd[0][0:16], in1=rtd[1][0:16], op=ALU.subtract),
              reads=["rtd0", "rtd1"], writes=["rotd"])
        P.add("dve", lambda e: e.tensor_tensor(out=rotd[0:16, :, 32:64], in0=rtd[2][0:16], in1=rtd[3][0:16], op=ALU.add),
              reads=["rtd2", "rtd3"], writes=["rotd"])
        P.add("dve", lambda e: e.tensor_scalar(out=rotd[0:16, 4:8, :], in0=rotd[0:16, 4:8, :], scalar1=0.125, scalar2=None, op0=ALU.mult),
              reads=["rotd"], writes=["rotd"])
        P.add("dve", lambda e: e.tensor_tensor(out=prd[0:16].rearrange("p (h d) -> p h d", h=4), in0=rotd[0:16, 0:4, :], in1=rotd[0:16, 4:8, :], op=ALU.mult),
              reads=["rotd"], writes=["prd"])
        P.add("dve", lambda e: e.tensor_reduce(out=smb[0:16, 20:24], in_=prd[0:16].rearrange("p (h d) -> p h d", h=4), axis=AX.X, op=ALU.add),
              reads=["prd"], writes=["qk4"])
        vr4 = zd[0:16, 2048:2560].rearrange("p (h d) -> p h d", h=4)
        P.add("dve", lambda e: e.tensor_tensor(out=ord16[0:16], in0=vr4, in1=smb[0:16, 20:24].unsqueeze(2).to_broadcast([16, 4, 128]), op=ALU.mult),
              reads=["zd", "qk4"], writes=["ord16"])
        for h in range(4):
            P.add("pe", lambda e, h=h: e.transpose(out=psf(0, h * 16, (h + 1) * 16, 0, 64), in_=rotd[0:16, h, :], identity=identf[0:16, 0:16]),
                  reads=["rotd", "identf"], writes=["ps0"])
        P.add("dve", lambda e: e.tensor_copy(out=qTd[0:64], in_=psf(0, 0, 64, 0, 64)), reads=["ps0"], writes=["qTd"])
        qT3 = qTd[0:64].rearrange("p (h n) -> p h n", h=4)
        P.add("dve", lambda e: e.tensor_tensor(out=QZ[0:64], in0=qT3.unsqueeze(3).to_broadcast([64, 4, 16, 16]),
                                               in1=E16[0:64].rearrange("p (n c) -> p n c", n=16).unsqueeze(1).to_broadcast([64, 4, 16, 16]), op=ALU.mult),
              reads=["qTd", "E16"], writes=["QZ"])
        gams = [1.0 - 2.0 ** (-5.0 - h) for h in range(4)]
        for n in range(16):
            i = n % 2
            P.add("sp", lambda e, n=n, i=i: e.dma_start(out=Sn[i][0:64], in_=sst_d[n].rearrange("h d e -> d h e")), writes=["sn%d" % i], slot="sn%d" % i)
            for h in range(4):
                P.add("pe", lambda e, n=n, h=h, i=i: e.matmul(psf(4 + h, 0, 128, 0, 16), lhsT=QZ[0:64, h, n, :], rhs=Sn[i][0:64, h, :],
                                                              start=(n == 0), stop=(n == 15)),
                      reads=["QZ", "sn%d" % i], writes=["ps%d" % (4 + h)])
            P.add("dve", lambda e, n=n: e.tensor_scalar(out=VZn[0:16], in0=zd[0:16, 2048:2560], scalar1=identf[0:16, n:n + 1], scalar2=None, op0=ALU.mult),
                  reads=["zd", "identf"], writes=["VZn"])
            for h in range(4):
                P.add("pe", lambda e, h=h: e.matmul(psf(1, h * 128, (h + 1) * 128, 0, 64), lhsT=rotd[0:16, 4 + h, :], rhs=VZn[0:16, h * 128:(h + 1) * 128],
                                                    start=True, stop=True), reads=["rotd", "VZn"], writes=["ps1"])
            for h in range(4):
                P.add("dve", lambda e, h=h, i=i: e.scalar_tensor_tensor(out=Snw[i][0:64, h, :], in0=Sn[i][0:64, h, :], scalar=gams[h],
                                                                        in1=psf(1, h * 128, (h + 1) * 128, 0, 64), op0=ALU.mult, op1=ALU.add),
                      reads=["sn%d" % i, "ps1"], writes=["snw%d" % i])
            P.add("sp", lambda e, n=n, i=i: e.dma_start(out=ss_o[n].rearrange("h d e -> d h e"), in_=Snw[i][0:64]),
                  reads=["snw%d" % i], writes=[("ss_o", n)], slot="snw%d" % i)
        for h in range(4):
            P.add("dve", lambda e, h=h: e.scalar_tensor_tensor(out=ord16[0:16, h, :], in0=psf(4 + h, 0, 128, 0, 16), scalar=gams[h], in1=ord16[0:16, h, :],
                                                               op0=ALU.mult, op1=ALU.add), reads=["ps%d" % (4 + h), "ord16"], writes=["ord16"])
        o2d = ord16[0:16].rearrange("p h d -> p (h d)")
        P.add("act", lambda e: e.activation(out=sq16[0:16], in_=o2d, func=AF.Square), reads=["ord16"], writes=["sq16"])
        P.add("dve", lambda e: e.tensor_reduce(out=smb[0:16, 8:12], in_=sq16[0:16].rearrange("p (h d) -> p h d", h=4), axis=AX.X, op=ALU.add),
              reads=["sq16"], writes=["ms4"])
        P.add("act", lambda e: e.activation(out=smb[0:16, 12:16], in_=smb[0:16, 8:12], func=AF.Ln, bias=epsb[0:16], scale=1.0 / 128),
              reads=["ms4", "epsb"], writes=["l4"])
        P.add("act", lambda e: e.activation(out=smb[0:16, 16:20], in_=smb[0:16, 12:16], func=AF.Exp, scale=-0.5), reads=["l4"], writes=["r4"])
        grd = zd[0:16, 2560:3072]
        P.add("act", lambda e: e.activation(out=sgd[0:16], in_=grd, func=AF.Exp, scale=-1.0), reads=["zd"], writes=["sgd"])
        P.add("dve", lambda e: e.tensor_scalar(out=sgd[0:16], in0=sgd[0:16], scalar1=1.0, scalar2=None, op0=ALU.add), reads=["sgd"], writes=["sgd"])
        P.add("dve", lambda e: e.reciprocal(out=sgd[0:16], in_=sgd[0:16]), reads=["sgd"], writes=["sgd"])
        P.add("dve", lambda e: e.tensor_tensor(out=sgd[0:16], in0=sgd[0:16], in1=grd, op=ALU.mult), reads=["sgd", "zd"], writes=["sgd"])
        P.add("dve", lambda e: e.tensor_tensor(out=a16[0:16].rearrange("p (h d) -> p h d", h=4), in0=ord16[0:16],
                                               in1=smb[0:16, 16:20].unsqueeze(2).to_broadcast([16, 4, 128]), op=ALU.mult),
              reads=["ord16", "r4"], writes=["a16"])
        P.add("dve", lambda e: e.tensor_tensor(out=MD4[:, :, 128:256], in0=a16[0:16].rearrange("p (h d) -> p h d", h=4),
                                               in1=sgd[0:16].rearrange("p (h d) -> p h d", h=4), op=ALU.mult),
              reads=["a16", "sgd"], writes=["MD"])

    if not do_C:
        P.add("sp", None, reads=[("kout", t) for t in range(nbA)] + [("vout", t) for t in range(nbA)] + ["sret"] + [("Ex", t) for t in range(nbA)] + (["ks_o", "vs_o"] + [("ss_o", n) for n in range(16)] if do_B else []))
        P.emit(es, limit)
        es.close()
        return nc, dict(P.stats)
    P.barrier()
    A.off = phase_base
    NBLK = nblkC
    wstage_all = A.alloc(8 * 256, F32)
    wstage = [wstage_all[:, 0:1024], wstage_all[:, 1024:2048], 128]
    gains = A.alloc(24, F32)
    idxt = A.alloc(NOWN * 4, I32)
    X2 = A.alloc(NBLK * D, F32).rearrange("p (b n) -> p b n", b=NBLK)
    H2T = A.alloc(8 * (NBLK + 1) * 128, BF16).rearrange("p (k t) -> p k t", k=8)
    cT = A.alloc(D, BF16).rearrange("p (k t) -> p k t", k=8)
    hbf = A.alloc(D, BF16)
    smc = A.alloc(16, F32)
    sqc = hbf
    qmb = A.alloc(D, F32)
    sweep_base = A.off
    Wout = A.alloc(8 * D, BF16).rearrange("p (k n) -> p k n", k=8)
    Wmq = A.alloc(8 * D, BF16).rearrange("p (k n) -> p k n", k=8)
    Wmo = A.alloc(8 * D, BF16).rearrange("p (k n) -> p k n", k=8)
    mkT = A.alloc(8 * 256, BF16).rearrange("p (c m) -> p c m", c=8)
    mvx = A.alloc(2 * 4 * 258, BF16).rearrange("p (a h c) -> p a h c", a=2, h=4)
    memT = A.alloc(8 * 256, BF16).rearrange("p (k t) -> p k t", k=8)
    regR = A.off
    memx = A.alloc(2 * D, F32).rearrange("p (a n) -> p a n", a=2)
    memo = A.alloc(2 * 512, F32).rearrange("p (a n) -> p a n", a=2)
    wtmp = A.alloc(8 * 512, BF16).rearrange("p (k n) -> p k n", k=8)
    peakC1 = A.off

    P.add("sp", lambda e: e.dma_start(out=gains, in_=gains_d), writes=["gains"], slot="gains")
    P.add("sp", lambda e: e.dma_start(out=idxt, in_=idx_d), writes=["idxt"], slot="idxt")
    gmq, gkv, gmlp = gains[:, 0:8], gains[:, 8:16], gains[:, 16:24]
    load_weight(Wout, "Wout", wout_d, 8, D, wstage)
    load_weight(Wmq, "Wmq", wmq_d, 8, D, wstage, gain=gmq, gain_res="gains")
    load_weight(Wmo, "Wmo", wmo_d, 8, D, wstage)
    P.add("pool", lambda e: e.memset(mvx[:, :, :, 256:258], 1.0), writes=["mvxones"])

    def transposes_to(dst, dst_res, src, src_res, bank, nk=8, evac="act"):
        for k in range(nk):
            P.add("pe", lambda e, k=k: e.transpose(out=psb(bank)[:, k * 128:(k + 1) * 128], in_=src[:, k * 128:(k + 1) * 128],
                                                   identity=ident), reads=(src_res if isinstance(src_res, list) else [src_res]) + ["ident"], writes=["ps%d" % bank])
        if evac == "act":
            P.add("act", lambda e: e.activation(out=dst, in_=psb(bank)[:, 0:nk * 128].rearrange("p (k t) -> p k t", k=nk), func=AF.Copy),
                  reads=["ps%d" % bank], writes=[dst_res])
        else:
            P.add("dve", lambda e: e.tensor_copy(out=dst, in_=psb(bank)[:, 0:nk * 128].rearrange("p (k t) -> p k t", k=nk)),
                  reads=["ps%d" % bank], writes=[dst_res])

    P.add("sp", lambda e: e.dma_start(out=memx, in_=memp_d.rearrange("(a p) n -> p a n", p=128)), writes=["memx"], slot="memx")
    for a in range(2):
        rstd_ops(memx[:, a, :], "memx", D, sqc, "hbf", smc[:, 0:1], smc[:, 1:2], smc[:, 2:3], "m_")
        P.add("dve", lambda e, a=a: e.tensor_scalar(out=hbf, in0=memx[:, a, :], scalar1=smc[:, 2:3], scalar2=None, op0=ALU.mult),
              reads=["memx", "m_rstd"], writes=["hbf"])
        transposes_to(memT[:, :, a * 128:(a + 1) * 128], "memT", hbf, "hbf", 0)
    for which, wsrc, outd in (("k", wmk_d, memk_o), ("v", wmv_d, memv_o)):
        for half in range(2):
            load_weight(wtmp, "wtmp", wsrc[:, half * 512:(half + 1) * 512], 8, 512, wstage, gain=gkv, gain_res="gains")
            for a in range(2):
                for k in range(8):
                    P.add("pe", lambda e, a=a, k=k: e.matmul(psf(1 + a), lhsT=memT[:, k, a * 128:(a + 1) * 128], rhs=wtmp[:, k, :],
                                                             start=(k == 0), stop=(k == 7)),
                          reads=["memT", "wtmp"], writes=["ps%d" % (1 + a)])
                P.add("dve", lambda e, a=a: e.tensor_copy(out=memo[:, a, :], in_=psf(1 + a)),
                      reads=["ps%d" % (1 + a)], writes=["memo"])
                if which == "v":
                    P.add("act", lambda e, a=a, half=half: e.activation(
                        out=mvx[:, a, half * 2:half * 2 + 2, 0:256], in_=psf(1 + a).rearrange("p (h c) -> p h c", h=2), func=AF.Copy),
                        reads=["ps%d" % (1 + a)], writes=["mvx"])
            if which == "k":
                for ct in range(4):
                    for k in range(8):
                        P.add("pe", lambda e, ct=ct, k=k: e.matmul(psf(3, 0, 256), lhsT=wtmp[:, k, ct * 128:(ct + 1) * 128], rhs=memT[:, k, :],
                                                                   start=(k == 0), stop=(k == 7)),
                              reads=["memT", "wtmp"], writes=["ps3"])
                    P.add("act", lambda e, ct=ct, half=half: e.activation(out=mkT[:, half * 4 + ct, :], in_=psf(3, 0, 256), func=AF.Copy),
                          reads=["ps3"], writes=["mkT"])
            P.add("sp", lambda e, outd=outd, half=half: e.dma_start(
                out=outd.rearrange("(a p) n -> p a n", p=128)[:, :, half * 512:(half + 1) * 512], in_=memo),
                reads=["memo"], writes=[("memout", which, half)], slot="memo")

    def rms_to_T(x_ap, x_res, dst, dst_res, tag):
        rstd_ops(x_ap, x_res, D, sqc, "hbf", smc[:, 3:4], smc[:, 4:5], smc[:, 5:6], tag)
        P.add("dve", lambda e: e.tensor_scalar(out=hbf, in0=x_ap, scalar1=smc[:, 5:6], scalar2=None, op0=ALU.mult),
              reads=[x_res, tag + "rstd"], writes=["hbf"])
        transposes_to(dst, dst_res, hbf, "hbf", 0)

    def proj_add(x_ap, x_res, lT, lT_res, W, W_res):
        for nt in range(2):
            for k in range(8):
                P.add("pe", lambda e, nt=nt, k=k: e.matmul(psf(1 + nt), lhsT=lT[:, k, :], rhs=W[:, k, nt * 512:(nt + 1) * 512],
                                                           start=(k == 0), stop=(k == 7)),
                      reads=[lT_res, W_res], writes=["ps%d" % (1 + nt)])
            P.add("dve", lambda e, nt=nt: e.tensor_tensor(out=x_ap[:, nt * 512:(nt + 1) * 512], in0=x_ap[:, nt * 512:(nt + 1) * 512],
                                                          in1=psf(1 + nt), op=ALU.add),
                  reads=[x_res, "ps%d" % (1 + nt)], writes=[x_res])

    P.barrier()
    A.off = regR
    MGall = A.alloc(8 * 256, BF16)
    MG = [MGall[:, 0:1024], MGall[:, 1024:2048]]
    qmd = MGall.bitcast(F32)
    qmT = A.alloc(D, BF16).rearrange("p (k t) -> p k t", k=8)
    PTm = A.alloc(256, BF16)
    om = A.alloc(D, BF16)
    peakC1 = max(peakC1, A.off)
    mkt = wstage_all[:, 0:1024]
    mvt = wstage_all[:, 1024:2048]

    def mem_attend_prompt():
        for ct in range(8):
            bank = 5 + ct // 4
            for k in range(8):
                P.add("pe", lambda e, ct=ct, k=k, bank=bank: e.matmul(psf(bank, (ct % 4) * 128, (ct % 4 + 1) * 128),
                                                                     lhsT=Wmq[:, k, ct * 128:(ct + 1) * 128], rhs=cT[:, k, :],
                                                                     start=(k == 0), stop=(k == 7)),
                      reads=["cT", "Wmq"], writes=["ps%d" % bank])
        for hb in range(2):
            P.add("act", lambda e, hb=hb: e.activation(out=qmT[:, hb * 4:hb * 4 + 4, :], in_=psf(5 + hb).rearrange("p (c t) -> p c t", c=4), func=AF.Copy),
                  reads=["ps%d" % (5 + hb)], writes=["qmT"])
        for hm in range(4):
            for mc in range(2):
                for c in range(2):
                    P.add("pe", lambda e, hm=hm, mc=mc, c=c: e.matmul(psf(3, mc * 128, (mc + 1) * 128),
                                                                     lhsT=mkT[:, hm * 2 + c, mc * 128:(mc + 1) * 128], rhs=qmT[:, hm * 2 + c, :],
                                                                     start=(c == 0), stop=(c == 1)),
                          reads=["mkT", "qmT"], writes=["ps3"])
            P.add("act", lambda e: e.activation(out=PTm, in_=psf(3, 0, 256), func=AF.Exp, scale=1.0 / 16.0), reads=["ps3"], writes=["PTm"])
            for mc in range(2):
                P.add("pe", lambda e, hm=hm, mc=mc: e.matmul(psf(4, 0, 257), lhsT=PTm[:, mc * 128:(mc + 1) * 128], rhs=mvx[:, mc, hm, 0:257],
                                                             start=(mc == 0), stop=(mc == 1)),
                      reads=["PTm", "mvx", "mvxones"], writes=["ps4"])
            P.add("dve", lambda e: e.reciprocal(out=smc[:, 6:7], in_=psf(4, 256, 257)), reads=["ps4"], writes=["rm"])
            P.add("dve", lambda e, hm=hm: e.tensor_scalar(out=om[:, hm * 256:(hm + 1) * 256], in0=psf(4, 0, 256), scalar1=smc[:, 6:7],
                                                          scalar2=None, op0=ALU.mult), reads=["ps4", "rm"], writes=["om"])

    def mem_attend_sample():
        mgres = [("mg%d" % i_, r_) for i_ in range(2) for r_ in range(4)]
        for nt in range(2):
            for k in range(8):
                P.add("pe", lambda e, nt=nt, k=k: e.matmul(psf(4 + nt, 0, 512, 0, 16), lhsT=cT[:, k, 0:16], rhs=Wmq[:, k, nt * 512:(nt + 1) * 512],
                                                           start=(k == 0), stop=(k == 7)), reads=["cT", "Wmq"], writes=["ps%d" % (4 + nt)])
            P.add("act", lambda e, nt=nt: e.activation(out=qmd[0:16, nt * 512:(nt + 1) * 512], in_=psf(4 + nt, 0, 512, 0, 16), func=AF.Copy, scale=1.0 / 16.0),
                  reads=["ps%d" % (4 + nt)], writes=mgres)
        P.add("pool", lambda e: e.memset(om, 0.0), writes=["om"])
        selc = smc[:, 12:13]
        Sm = hid_s
        for n in range(16):
            P.add("dve", lambda e, n=n: e.tensor_copy(out=selq[0:16], in_=identf[0:16, n:n + 1].to_broadcast([16, 128])),
                  reads=["identf"], writes=["selq"])
            for nt in range(2):
                P.add("pe", lambda e, nt=nt: e.matmul(psf(4 + nt), lhsT=selq[0:16], rhs=qmd[0:16, nt * 512:(nt + 1) * 512], start=True, stop=True),
                      reads=["selq"] + mgres, writes=["ps%d" % (4 + nt)])
                P.add("act", lambda e, nt=nt: e.activation(out=qmb[:, nt * 512:(nt + 1) * 512], in_=psf(4 + nt), func=AF.Copy),
                      reads=["ps%d" % (4 + nt)], writes=["qmb"])
            for mc in range(2):
                P.add("sp", lambda e, n=n, mc=mc: e.dma_start(out=mkt, in_=cmk_d[n, mc * 128:(mc + 1) * 128, :]), writes=["wstage0"], slot="wstage0")
                P.add("pool", lambda e: e.tensor_tensor(out=mkt, in0=mkt, in1=qmb, op=ALU.mult), reads=["qmb"], writes=["wstage0"])
                P.add("dve", lambda e, mc=mc: e.tensor_reduce(out=Sm[:, mc * 4:(mc + 1) * 4], in_=mkt.rearrange("p (h d) -> p h d", h=4), axis=AX.X, op=ALU.add),
                      reads=["wstage0"], writes=["Sm"])
            P.add("act", lambda e: e.activation(out=Sm[:, 8:16], in_=Sm[:, 0:8], func=AF.Exp), reads=["Sm"], writes=["Pm"])
            for mc in range(2):
                P.add("sp", lambda e, n=n, mc=mc: e.dma_start(out=mvt, in_=cmv_d[n, mc * 128:(mc + 1) * 128, :]), writes=["wstage1"], slot="wstage1")
                for nt in range(2):
                    P.add("pe", lambda e, mc=mc, nt=nt: e.matmul(psf(1 + nt, 0, 512, 0, 4), lhsT=Sm[:, 8 + mc * 4:8 + (mc + 1) * 4], rhs=mvt[:, nt * 512:(nt + 1) * 512],
                                                                start=(mc == 0), stop=(mc == 1)), reads=["Pm", "wstage1"], writes=["ps%d" % (1 + nt)])
                P.add("pe", lambda e, mc=mc: e.matmul(psf(3, 0, 1, 0, 4), lhsT=Sm[:, 8 + mc * 4:8 + (mc + 1) * 4], rhs=onesf[:, 0:1], start=(mc == 0), stop=(mc == 1)),
                      reads=["Pm", "onesf"], writes=["ps3"])
            w4 = smc[:, 13:14]
            P.add("dve", lambda e: e.reciprocal(out=w4[0:4], in_=psf(3, 0, 1, 0, 4)), reads=["ps3"], writes=["w4"])
            dm4 = identf[0:4, 0:4].unsqueeze(2).to_broadcast([4, 4, 256])
            for nt in range(2):
                P.add("dve", lambda e, nt=nt: e.tensor_tensor(out=OD4[0:4, nt * 512:(nt + 1) * 512].rearrange("p (h d) -> p h d", h=2),
                                                              in0=psf(1 + nt, 0, 512, 0, 4).rearrange("p (h d) -> p h d", h=2),
                                                              in1=identf[0:4, 2 * nt:2 * nt + 2].unsqueeze(2).to_broadcast([4, 2, 256]), op=ALU.mult),
                      reads=["ps%d" % (1 + nt), "identf"], writes=["OD4"])
            P.add("dve", lambda e, n=n: e.tensor_scalar(out=Wsel4[0:4], in0=E16[0:4, n * 16:(n + 1) * 16], scalar1=w4[0:4], scalar2=None, op0=ALU.mult),
                  reads=["E16", "w4"], writes=["Wsel4"])
            for nt in range(2):
                P.add("pe", lambda e, n=n, nt=nt: e.matmul(psf(6 + nt, 0, 512, 0, 16), lhsT=Wsel4[0:4], rhs=OD4[0:4, nt * 512:(nt + 1) * 512],
                                                           start=(n == 0), stop=(n == 15)), reads=["Wsel4", "OD4"], writes=["ps%d" % (6 + nt)])
        for nt in range(2):
            P.add("dve", lambda e, nt=nt: e.tensor_copy(out=om[0:16, nt * 512:(nt + 1) * 512], in_=psf(6 + nt, 0, 512, 0, 16)),
                  reads=["ps%d" % (6 + nt)], writes=["om"])

    def sweep1(blk, x, xr, mg, mgres, sample):
        transposes_to(cT, "cT", mg, mgres, 0)
        proj_add(x, xr, cT, "cT", Wout, "Wout")
        rms_to_T(x, xr, cT, "cT", "q_")
        if sample:
            mem_attend_sample()
        else:
            mem_attend_prompt()
        transposes_to(cT, "cT", om, "om", 0)
        proj_add(x, xr, cT, "cT", Wmo, "Wmo")
        rms_to_T(x, xr, H2T[:, :, blk * 128:(blk + 1) * 128], ("H2T", blk), "h_")

    for blk in range(NBLK):
        i = blk % 2
        x = X2[:, blk, :]
        xr = ("X2", blk)
        mg = MG[i]
        mgr = "mg%d" % i
        P.add("sp", lambda e, blk=blk, x=x: e.dma_start(out=x, in_=xown[blk * 128:(blk + 1) * 128, :]), writes=[xr], slot="x2ld%d" % i)
        for r in range(4):
            P.add("pool", lambda e, r=r, blk=blk, mg=mg: e.indirect_dma_start(
                out=mg[:, r * 256:(r + 1) * 256], out_offset=None, in_=Gx.ap(),
                in_offset=bass.IndirectOffsetOnAxis(ap=idxt[:, blk * 4 + r:blk * 4 + r + 1], axis=0)),
                reads=[("Gx", q_) for q_ in range(4)] + ["idxt"], writes=[(mgr, r)], slot="%s_%d" % (mgr, r))
        sweep1(blk, x, xr, mg, [(mgr, r) for r in range(4)], False)
    if do_B:
        sweep1(NBLK, XS, "XS", MD, ["MD"], True)

    P.barrier()
    A.off = sweep_base
    Wq = [A.alloc(8 * 1024, BF16).rearrange("p (k n) -> p k n", k=8) for _ in range(2)]
    hid = A.alloc(8 * 512, BF16).rearrange("p (f t) -> p f t", f=8)
    sqh = [A.alloc(512, F32) for _ in range(2)]
    yo = [A.alloc(D, F32) for _ in range(2)]
    nfin = A.alloc(D, F32)
    P.add("sp", lambda e: e.dma_start(out=nfin, in_=nfin_d.partition_broadcast(128)), writes=["nfin"], slot="nfin")
    NT4 = NBLK // 4
    for q in range(4):
        load_weight(Wq[0], "Wup", wup_d[:, q * 1024:(q + 1) * 1024], 8, 1024, wstage, gain=gmlp, gain_res="gains")
        load_weight(Wq[1], "Wdn", wdn_d[q * 1024:(q + 1) * 1024, :], 8, 1024, wstage)
        for tt in range(NT4):
            h2res = [("H2T", tt * 4 + bb) for bb in range(4)]
            for fc in range(8):
                bank = 5 + fc % 2
                for k in range(8):
                    P.add("pe", lambda e, fc=fc, k=k, bank=bank, tt=tt: e.matmul(psf(bank), lhsT=Wq[0][:, k, fc * 128:(fc + 1) * 128],
                                                                             rhs=H2T[:, k, tt * 512:(tt + 1) * 512], start=(k == 0), stop=(k == 7)),
                          reads=h2res + ["Wup"], writes=["ps%d" % bank])
                sq = sqh[fc % 2]
                sqr = "sqh%d" % (fc % 2)
                P.add("act", lambda e, bank=bank, sq=sq: e.activation(out=sq, in_=psf(bank), func=AF.Square), reads=["ps%d" % bank], writes=[sqr])
                P.add("dve", lambda e, bank=bank, sq=sq, fc=fc: e.scalar_tensor_tensor(out=hid[:, fc, :], in0=psf(bank), scalar=0.0, in1=sq,
                                                                                   op0=ALU.is_gt, op1=ALU.mult),
                      reads=["ps%d" % bank, sqr], writes=["hid"])
            for bb in range(4):
                blk = tt * 4 + bb
                x = X2[:, blk, :]
                xr = ("X2", blk)
                for nt in range(2):
                    for fc in range(8):
                        P.add("pe", lambda e, nt=nt, fc=fc, bb=bb: e.matmul(psf(1 + nt), lhsT=hid[:, fc, bb * 128:(bb + 1) * 128],
                                                                         rhs=Wq[1][:, fc, nt * 512:(nt + 1) * 512], start=(fc == 0), stop=(fc == 7)),
                              reads=["hid", "Wdn"], writes=["ps%d" % (1 + nt)])
                    P.add("dve", lambda e, nt=nt, x=x: e.tensor_tensor(out=x[:, nt * 512:(nt + 1) * 512], in0=x[:, nt * 512:(nt + 1) * 512],
                                                                      in1=psf(1 + nt), op=ALU.add),
                          reads=[xr, "ps%d" % (1 + nt)], writes=[xr])
        if do_B:
            for fc in range(8):
                bank = 5 + fc % 2
                for k in range(8):
                    P.add("pe", lambda e, fc=fc, k=k, bank=bank: e.matmul(psf(bank, 0, 128), lhsT=Wq[0][:, k, fc * 128:(fc + 1) * 128],
                                                                      rhs=H2T[:, k, NBLK * 128:(NBLK + 1) * 128], start=(k == 0), stop=(k == 7)),
                          reads=[("H2T", NBLK), "Wup"], writes=["ps%d" % bank])
                sq = sqh[fc % 2]
                sqr = "sqh%d" % (fc % 2)
                P.add("act", lambda e, bank=bank, sq=sq: e.activation(out=sq[:, 0:128], in_=psf(bank, 0, 128), func=AF.Square), reads=["ps%d" % bank], writes=[sqr])
                P.add("dve", lambda e, bank=bank, sq=sq, fc=fc: e.scalar_tensor_tensor(out=hid[:, fc, 0:128], in0=psf(bank, 0, 128), scalar=0.0, in1=sq[:, 0:128],
                                                                                   op0=ALU.is_gt, op1=ALU.mult),
                      reads=["ps%d" % bank, sqr], writes=["hid"])
            for nt in range(2):
                for fc in range(8):
                    P.add("pe", lambda e, nt=nt, fc=fc: e.matmul(psf(1 + nt), lhsT=hid[:, fc, 0:128], rhs=Wq[1][:, fc, nt * 512:(nt + 1) * 512],
                                                                start=(fc == 0), stop=(fc == 7)), reads=["hid", "Wdn"], writes=["ps%d" % (1 + nt)])
                P.add("dve", lambda e, nt=nt: e.tensor_tensor(out=XS[:, nt * 512:(nt + 1) * 512], in0=XS[:, nt * 512:(nt + 1) * 512],
                                                              in1=psf(1 + nt), op=ALU.add), reads=["XS", "ps%d" % (1 + nt)], writes=["XS"])
    for blk in range(NBLK + (1 if do_B else 0)):
        i = blk % 2
        x = X2[:, blk, :] if blk < NBLK else XS
        xr = ("X2", blk) if blk < NBLK else "XS"
        rstd_ops(x, xr, D, sqc, "hbf", smc[:, 7:8], smc[:, 8:9], smc[:, 9:10], "f_")
        P.add("dve", lambda e, x=x, i=i: e.scalar_tensor_tensor(out=yo[i], in0=x, scalar=smc[:, 9:10], in1=nfin, op0=ALU.mult, op1=ALU.mult),
              reads=[xr, "f_rstd", "nfin"], writes=["yo%d" % i])
        if blk < NBLK:
            P.add("sp", lambda e, blk=blk, i=i: e.dma_start(out=yown[blk * 128:(blk + 1) * 128, :], in_=yo[i]),
                  reads=["yo%d" % i], writes=[("yown", blk)], slot="yo%d" % i)
        else:
            P.add("sp", lambda e, i=i: e.dma_start(out=ys_o, in_=yo[i][0:16, :]), reads=["yo%d" % i], writes=["ys_o"], slot="yo%d" % i)

    P.add("sp", None, reads=[("yown", blk) for blk in range(NBLK)] + [("kout", t) for t in range(NB)]
          + [("vout", t) for t in range(NB)] + ["sret"] + [("memout", w, hf) for w in "kv" for hf in range(2)]
          + (["ys_o", "ks_o", "vs_o"] + [("ss_o", n) for n in range(16)] if do_B else []))
    P.emit(es, limit)
    es.close()
    return nc, dict(P.stats, peakA=peakA, peakC1=peakC1, peakC2=A.off)


def _t5_bucket_np(n):
    n = np.maximum(n, 0)
    nf = np.maximum(n, 1).astype(np.float32)
    large = 16 + (np.log(nf / np.float32(16)) / np.float32(math.log(128 / 16)) * np.float32(16)).astype(np.int32)
    large = np.minimum(large, 31)
    return np.where(n < 16, n, large)


def _rope_table():
    inv = (np.float32(10000.0) ** (-np.linspace(0.0, 1.0, 32, dtype=np.float32))).astype(np.float32)
    pos = np.arange(T, dtype=np.float32)
    ang = (pos[:, None] * inv[None, :]).astype(np.float32)
    tab = np.concatenate([np.cos(ang), np.sin(ang)], axis=1).astype(np.float32)
    return tab.reshape(NB, 128, 64)


_PROG = {}


def _get_prog(n_phys):
    if n_phys not in _PROG:
        _PROG[n_phys] = build_program(n_phys=n_phys)
    return _PROG[n_phys]


def kernel(x_prompt, x_sample, mem_prompt, cache_k, cache_v, page_table, state_ret,
           cache_mem_k, cache_mem_v, rel_bias, norm_mix, w_in, lambda_q1, lambda_k1,
           lambda_q2, lambda_k2, subln_a, w_out, norm_mem_q, norm_mem_kv, w_mq, w_mk,
           w_mv, w_mo, norm_mlp, w_up, w_down, norm_final, _compact_pools=False):
    f32 = np.float32
    x_prompt = np.asarray(x_prompt, f32)
    w_in0 = np.asarray(w_in, f32)[0]
    rel_bias = np.asarray(rel_bias, f32)
    cache_k = np.asarray(cache_k, f32)
    cache_v = np.asarray(cache_v, f32)
    page_table = np.asarray(page_table, np.int32)
    n_phys = cache_k.shape[1]
    if _compact_pools:
        n_phys = 256
    nc, stats = _get_prog(n_phys)
    ck_full = cache_k[0].reshape(-1, 512)
    cv_full = cache_v[0].reshape(-1, 512)
    x_sample = np.asarray(x_sample, f32)
    state_ret = np.asarray(state_ret, f32)
    cache_mem_k = np.asarray(cache_mem_k, f32)
    cache_mem_v = np.asarray(cache_mem_v, f32)
    e16 = np.ascontiguousarray(np.tile(np.eye(16, dtype=f32).reshape(1, 256), (64, 1)))
    r84 = np.zeros((8, 8), f32)
    for hh in range(4):
        for mm in range(2):
            r84[hh * 2 + mm, hh] = 1.0
            r84[hh * 2 + mm, 4 + mm] = 1.0
    inv = (np.float32(10000.0) ** (-np.linspace(0.0, 1.0, 32, dtype=np.float32))).astype(f32)
    angd = (np.float32(PAST) * inv).astype(f32)
    roped = np.ascontiguousarray(np.tile(np.concatenate([np.cos(angd), np.sin(angd)]).astype(f32)[None, :], (16, 1)))
    pp = np.arange(128)
    bconst_base = np.zeros((128, 72), f32)
    bconst_base[:, 0] = pp
    reld = PAST - (np.arange(16)[None, :] * 128 + pp[:, None])
    bkd = _t5_bucket_np(reld)

    rope = _rope_table()
    p = np.arange(128)
    maskT = (p[None, :] >= p[:, None]).astype(f32)
    ident = np.eye(128, dtype=f32)
    lamv = np.concatenate([np.asarray(a, f32)[0] for a in (lambda_q1, lambda_k1, lambda_q2, lambda_k2)])[None, :]
    subln = np.asarray(subln_a, f32)[0][None, :]
    wout_full = np.asarray(w_out, f32)[0]
    perm = np.concatenate([np.concatenate([np.arange(r * 128, (r + 1) * 128), 512 + np.arange(r * 128, (r + 1) * 128)]) for r in range(4)])
    wout_p = np.ascontiguousarray(wout_full[perm])

    def g8(v):
        return np.ascontiguousarray(np.asarray(v, f32).reshape(8, 128).T)

    gains = np.concatenate([g8(norm_mem_q[0]), g8(norm_mem_kv[0]), g8(norm_mlp[0])], axis=1)
    rel_d = p[None, :] - p[:, None]
    bk_sub = _t5_bucket_np(128 + rel_d)
    bk_dia = _t5_bucket_np(rel_d)

    in_maps = []
    for c in range(NCORES):
        b, h = c // 4, c % 4
        j = h
        cols = np.concatenate([
            np.arange(h * 128, (h + 1) * 128),
            512 + np.arange(h * 128, (h + 1) * 128),
            1024 + np.arange(h * 128, (h + 1) * 128),
            2048 + np.arange(h * 128, (h + 1) * 128),
            2560 + np.arange(h * 128, (h + 1) * 128),
            1536 + np.arange(h * 64, (h + 1) * 64),
            1792 + np.arange(h * 64, (h + 1) * 64),
        ])
        gam = 1.0 - 2.0 ** (-5.0 - h)
        hconst = np.zeros((128, 8), f32)
        hconst[:, 0] = gam ** (p + 1.0)
        hconst[:, 1] = (64.0 ** -0.5) * gam ** (-(p + 1.0))
        hconst[:, 2] = gam ** 128.0
        hconst[:, 3] = rel_bias[31, h]
        bt = np.empty((128, 2, 128), f32)
        bt[:, 0, :] = rel_bias[bk_sub, h]
        bt[:, 1, :] = np.where(rel_d >= 0, rel_bias[bk_dia, h], f32(NEG))
        idx = np.empty((128, NOWN * 4), np.int32)
        for blk in range(NOWN):
            for r in range(4):
                idx[:, blk * 4 + r] = j * 8192 + r * 2048 + blk * 128 + p
        bconst = bconst_base.copy()
        bd = np.full((128, 17, 4), f32(NEG), f32)
        bd[:, 0:16, :] = rel_bias[bkd, :]
        bd[0, 16, :] = rel_bias[0, :]
        bconst[:, 1:69] = bd.reshape(128, 68)
        ptc = page_table[c * 16:(c + 1) * 16]
        if _compact_pools:
            uniq, invp = np.unique(ptc.reshape(-1), return_inverse=True)
            ck_c = np.zeros((256 * 128, 512), f32)
            cv_c = np.zeros((256 * 128, 512), f32)
            ck_c[:len(uniq) * 128] = cache_k[0][uniq].reshape(-1, 512)
            cv_c[:len(uniq) * 128] = cache_v[0][uniq].reshape(-1, 512)
            ptc = invp.reshape(16, 16).astype(np.int32)
        else:
            ck_c, cv_c = ck_full, cv_full
        in_maps.append({
            "xd": np.ascontiguousarray(x_sample[c * 16:(c + 1) * 16, 0, :]),
            "win": w_in0,
            "ck": ck_c,
            "cv": cv_c,
            "pt": np.ascontiguousarray(ptc.reshape(1, 256)),
            "sst": np.ascontiguousarray(state_ret[0, c * 16:(c + 1) * 16]),
            "cmk": np.ascontiguousarray(cache_mem_k[0, c * 16:(c + 1) * 16].reshape(16, 256, D)),
            "cmv": np.ascontiguousarray(cache_mem_v[0, c * 16:(c + 1) * 16].reshape(16, 256, D)),
            "bconst": bconst,
            "e16": e16,
            "r84": r84,
            "roped": roped,
            "xb": x_prompt[b],
            "wA": np.ascontiguousarray(w_in0[:, cols]),
            "gmix": g8(norm_mix[0]),
            "rope": rope,
            "hconst": hconst,
            "bt": bt.reshape(128, 256),
            "maskT": maskT,
            "ident": ident,
            "lamv": lamv,
            "subln": subln,
            "xown": np.ascontiguousarray(x_prompt[b, j * 2048:(j + 1) * 2048]),
            "idx": idx,
            "wout": wout_p,
            "wmq": np.asarray(w_mq, f32)[0],
            "wmk": np.asarray(w_mk, f32)[0],
            "wmv": np.asarray(w_mv, f32)[0],
            "wmo": np.asarray(w_mo, f32)[0],
            "wup": np.asarray(w_up, f32)[0],
            "wdn": np.asarray(w_down, f32)[0],
            "gains": gains,
            "nfin": np.asarray(norm_final, f32)[None, :],
            "memp": np.asarray(mem_prompt, f32)[b],
        })
    res = run_bass_kernel_spmd(nc, in_maps, core_ids=list(range(NCORES)))
    R = res.results

    y_prompt = np.empty((2, T, D), f32)
    k_prompt = np.empty((1, 2, NB, 128, 4, 128), f32)
    v_prompt = np.empty((1, 2, NB, 128, 4, 128), f32)
    state_ret_prompt = np.empty((1, 2, 4, 64, 128), f32)
    mem_k_prompt = np.empty((1, 2, 256, 4, 256), f32)
    mem_v_prompt = np.empty((1, 2, 256, 4, 256), f32)
    for c in range(NCORES):
        b, h = c // 4, c % 4
        y_prompt[b, h * 2048:(h + 1) * 2048] = R[c]["yown"]
        k_prompt[0, b, :, :, h, :] = R[c]["kout"].reshape(NB, 128, 128)
        v_prompt[0, b, :, :, h, :] = R[c]["vout"].reshape(NB, 128, 128)
        state_ret_prompt[0, b, h] = R[c]["sret"]
        if h == 0:
            mem_k_prompt[0, b] = R[c]["memk"].reshape(256, 4, 256)
            mem_v_prompt[0, b] = R[c]["memv"].reshape(256, 4, 256)
    y_sample = np.empty((DEC, 1, D), f32)
    k_sample = np.empty((1, DEC, 1, 4, 128), f32)
    v_sample = np.empty((1, DEC, 1, 4, 128), f32)
    state_ret_sample = np.empty((1, DEC, 4, 64, 128), f32)
    for c in range(NCORES):
        y_sample[c * 16:(c + 1) * 16, 0] = R[c]["ys"]
        k_sample[0, c * 16:(c + 1) * 16, 0] = R[c]["ks"].reshape(16, 4, 128)
        v_sample[0, c * 16:(c + 1) * 16, 0] = R[c]["vs"].reshape(16, 4, 128)
        state_ret_sample[0, c * 16:(c + 1) * 16] = R[c]["ss"]
    return (y_prompt, y_sample, k_prompt, v_prompt, state_ret_prompt, mem_k_prompt, mem_v_prompt,
            k_sample, v_sample, state_ret_sample)
```

```python
import math
from contextlib import ExitStack

import numpy as np
import ml_dtypes

import concourse.bass as bass
import concourse.mybir as mybir
from concourse.bass_utils import run_bass_kernel_spmd

F32 = mybir.dt.float32
BF16 = mybir.dt.bfloat16
I32 = mybir.dt.int32
AF = mybir.ActivationFunctionType
ALU = mybir.AluOpType
AX = mybir.AxisListType

NCORES = 8
D = 1024
T = 8192
NB = T // 128
NOWN = 16
EPS = 1e-6
NEG = -30000.0
DEC = 128
PAST = 2048
NPG = PAST // 128
ENGS = ["sp", "act", "dve", "pe", "pool"]


class Prog:
    def __init__(self, nc):
        self.nc = nc
        self.ops = []
        self.last_writer = {}
        self.readers = {}

    def add(self, eng, fn, reads=(), writes=(), slot=None, inc=None):
        def _ps(r):
            return isinstance(r, str) and len(r) >= 3 and r.startswith("ps") and r[2].isdigit()
        writes = list(writes) + [r[:3] for r in reads if _ps(r)]
        writes = [r[:3] if _ps(r) else r for r in writes]
        reads = [r for r in reads if not _ps(r)]
        oid = len(self.ops)
        deps = set()
        for r in reads:
            w = self.last_writer.get(r)
            if w is not None:
                deps.add(w)
        for r in writes:
            w = self.last_writer.get(r)
            if w is not None:
                deps.add(w)
            for rd in self.readers.get(r, {}).values():
                deps.add(rd)
        dma = slot is not None
        rkey = ("dma", slot) if dma else eng
        for r in reads:
            self.readers.setdefault(r, {})[rkey] = oid
        for r in writes:
            self.last_writer[r] = oid
            self.readers[r] = {}
        deps.discard(oid)
        self.ops.append(dict(eng=eng, fn=fn, deps=deps, dma=dma, slot=slot, rw=(list(reads), list(writes)),
                             inc=(inc if inc is not None else (16 if dma else 1))))
        return oid

    def barrier(self):
        allres = list(self.last_writer.keys() | self.readers.keys())
        marks = []
        for e in ("act", "dve", "pool"):
            marks.append(self.add(e, self._bar_fn[e], writes=allres + ["bar_" + e]))
        for e in ENGS:
            self.add(e, None, reads=["bar_act", "bar_dve", "bar_pool"], writes=allres + ["barx_" + e])

    def emit(self, es, limit=None):
        nc = self.nc
        if limit is not None:
            self.ops = self.ops[:limit]
            self.ops.append(dict(eng="sp", fn=None, deps=set(i for i, o in enumerate(self.ops) if o["dma"] and o["fn"] is not None),
                                 dma=False, slot=None, inc=1))
        ops = self.ops
        sig = [False] * len(ops)

        def pruned(D_, o):
            return (D_["eng"] == "pe" and o["eng"] == "pe" and not D_["dma"] and not o["dma"])

        eff = []
        for o in ops:
            s_ = set()
            for d in o["deps"]:
                if ops[d]["fn"] is None:
                    s_ |= eff[d]
                else:
                    s_.add(d)
            eff.append(s_)
        for i, o in enumerate(ops):
            if o["dma"] and o["fn"] is not None:
                sig[i] = True
            for d in eff[i]:
                if pruned(ops[d], o):
                    continue
                sig[d] = True
        cnt = {}
        for i, o in enumerate(ops):
            if not sig[i]:
                o["sv"] = None
                continue
            key = ("dma", o["slot"]) if o["dma"] else o["eng"]
            cnt[key] = cnt.get(key, 0) + o["inc"]
            o["key"] = key
            o["sv"] = cnt[key]
        known = {e: {} for e in ENGS}
        nwaits = 0
        for i, o in enumerate(ops):
            kn = known[o["eng"]]
            need = {}
            for d in eff[i]:
                D_ = ops[d]
                if pruned(D_, o):
                    continue
                k, v = D_["key"], D_["sv"]
                if kn.get(k, 0) >= v:
                    continue
                if need.get(k, (0, None))[0] < v:
                    need[k] = (v, d)
            waits = []
            for k, (v, d) in need.items():
                if kn.get(k, 0) >= v:
                    continue
                waits.append((k, v))
                for kk, vv in ops[d]["vc"].items():
                    if kn.get(kk, 0) < vv:
                        kn[kk] = vv
                if kn.get(k, 0) < v:
                    kn[k] = v
            o["waits"] = waits
            nwaits += len(waits)
            vc = dict(kn)
            if sig[i]:
                vc[o["key"]] = max(vc.get(o["key"], 0), o["sv"])
            o["vc"] = vc
        sems = {}
        for n, k in enumerate(sorted(cnt.keys(), key=str)):
            sems[k] = es.enter_context(nc.semaphore("s%d" % n))
        self.stats = dict(nops=len(ops), nwaits=nwaits, nsems=len(sems))
        block = es.enter_context(nc.Block())

        def run(engname):
            def body(e):
                for o in ops:
                    if o["eng"] != engname:
                        continue
                    for (k, v) in o["waits"]:
                        e.wait_ge(sems[k], v)
                    if o["fn"] is not None:
                        ins = o["fn"](e)
                        if o["sv"] is not None:
                            ins.then_inc(sems[o["key"]], o["inc"])
            return body

        block.sync(run("sp"))
        block.scalar(run("act"))
        block.vector(run("dve"))
        block.tensor(run("pe"))
        block.gpsimd(run("pool"))


class Arena:
    def __init__(self, t, size):
        self.t = t
        self.size = size
        self.off = 0
        self.peak = 0

    def alloc(self, cols, dtype=BF16):
        n = cols * (2 if dtype in (F32, I32) else 1)
        self.off = (self.off + 15) // 16 * 16
        assert self.off + n <= self.size, ("SBUF arena overflow", self.off, n, self.size)
        ap = self.t[:, self.off:self.off + n]
        self.off += n
        self.peak = max(self.peak, self.off)
        if dtype != BF16:
            ap = ap.bitcast(dtype)
        return ap


def build_program(nbA=NB, do_cc=True, do_C=True, nblkC=NOWN, do_attn=True, do_front=True, limit=None, n_phys=2560, do_B=True):
    nc = bass.Bass("TRN2", target_bir_lowering=False)

    def din(name, shape, dt=F32):
        return nc.dram_tensor(name, list(shape), dt, kind="ExternalInput").ap()

    def dout(name, shape, dt=F32):
        return nc.dram_tensor(name, list(shape), dt, kind="ExternalOutput").ap()

    xb = din("xb", [T, D])
    wA = din("wA", [D, 768])
    gmix = din("gmix", [128, 8])
    rope = din("rope", [NB, 128, 64])
    hconst = din("hconst", [128, 8])
    btin = din("bt", [128, 256])
    maskT_d = din("maskT", [128, 128])
    ident_d = din("ident", [128, 128])
    lamv_d = din("lamv", [1, 256])
    subln_d = din("subln", [1, 128])
    xown = din("xown", [NOWN * 128, D])
    idx_d = din("idx", [128, NOWN * 4], I32)
    wout_d = din("wout", [D, D])
    wmq_d = din("wmq", [D, D])
    wmk_d = din("wmk", [D, D])
    wmv_d = din("wmv", [D, D])
    wmo_d = din("wmo", [D, D])
    wup_d = din("wup", [D, 4 * D])
    wdn_d = din("wdn", [4 * D, D])
    gains_d = din("gains", [128, 24])
    nfin_d = din("nfin", [1, D])
    memp_d = din("memp", [256, D])
    xd_d = din("xd", [16, D])
    win_d = din("win", [D, 3072])
    ck_d = din("ck", [n_phys * 128, 512])
    cv_d = din("cv", [n_phys * 128, 512])
    pt_d = din("pt", [1, 256], I32)
    sst_d = din("sst", [16, 4, 64, 128])
    cmk_d = din("cmk", [16, 256, D])
    cmv_d = din("cmv", [16, 256, D])
    bconst_d = din("bconst", [128, 72])
    e16_d = din("e16", [64, 256])
    r84_d = din("r84", [8, 8])
    roped_d = din("roped", [16, 64])
    ys_o = dout("ys", [16, D])
    ks_o = dout("ks", [16, 512])
    vs_o = dout("vs", [16, 512])
    ss_o = dout("ss", [16, 4, 64, 128])
    yown = dout("yown", [NOWN * 128, D])
    kout = dout("kout", [T, 128])
    vout = dout("vout", [T, 128])
    sret = dout("sret", [64, 128])
    memk_o = dout("memk", [256, D])
    memv_o = dout("memv", [256, D])
    Ex = nc.dram_tensor("Ex", [T, 256], BF16)
    Gx = nc.dram_tensor("Gx", [4 * T, 256], BF16)

    es = ExitStack()
    ARENA_COLS = 106000
    arena_t = es.enter_context(nc.sbuf_tensor("arena", [128, ARENA_COLS], BF16))
    PS = es.enter_context(nc.psum_tensor("ps", [128, 8, 512], F32))
    A = Arena(arena_t, ARENA_COLS)
    P = Prog(nc)

    def psf(b, lo=0, hi=512, p0=0, p1=128):
        return PS[p0:p1, b, lo:hi]

    def psb(b):
        return PS[:, b, :].bitcast(BF16)

    ident = A.alloc(128, BF16)
    hc = A.alloc(8, F32)
    epsb = A.alloc(1, F32)
    barscr = A.alloc(8, F32)
    P._bar_fn = {
        "act": lambda e: e.activation(out=barscr[:, 0:1], in_=barscr[:, 1:2], func=AF.Copy),
        "dve": lambda e: e.memset(barscr[:, 2:3], 0.0),
        "pool": lambda e: e.memset(barscr[:, 4:5], 0.0),
    }
    identf = A.alloc(128, F32)
    P.add("sp", lambda e: e.dma_start(out=identf, in_=ident_d), writes=["identf"], slot="ident")
    P.add("dve", lambda e: e.tensor_copy(out=ident, in_=identf), reads=["identf"], writes=["ident"])
    P.add("sp", lambda e: e.dma_start(out=hc, in_=hconst), writes=["hc"], slot="hc")
    P.add("dve", lambda e: e.memset(epsb, EPS), writes=["epsb"])
    P.add("dve", lambda e: e.memset(barscr, 0.0), writes=["barscr"])
    XS = A.alloc(D, F32)
    MD = A.alloc(D, BF16)
    E16 = A.alloc(256, F32)
    r84 = A.alloc(8, F32)
    onesf = A.alloc(2, F32)
    selq = A.alloc(128, F32)
    hid_s = A.alloc(16, F32)
    OD4 = A.alloc(D, F32)
    Wsel4 = A.alloc(16, F32)
    P.add("sp", lambda e: e.dma_start(out=E16[0:64], in_=e16_d), writes=["E16"], slot="E16")
    P.add("sp", lambda e: e.dma_start(out=r84[0:8], in_=r84_d), writes=["r84"], slot="r84")
    P.add("pool", lambda e: e.memset(onesf, 1.0), writes=["onesf"])
    P.add("pool", lambda e: e.memset(XS, 0.0), writes=["XS"])
    P.add("pool", lambda e: e.memset(MD, 0.0), writes=["MD"])
    P.add("sp", lambda e: e.dma_start(out=XS[0:16], in_=xd_d), writes=["XS"], slot="XS")
    phase_base = A.off

    def rstd_ops(src_ap, src_res, n, junk, junk_res, ssq, lnv, rstd, tag):
        P.add("act", lambda e: e.activation(out=junk, in_=src_ap, func=AF.Square, accum_out=ssq),
              reads=[src_res], writes=[junk_res, tag + "ssq"])
        P.add("act", lambda e: e.activation(out=lnv, in_=ssq, func=AF.Ln, bias=epsb, scale=1.0 / n),
              reads=[tag + "ssq", "epsb"], writes=[tag + "lnv"])
        P.add("act", lambda e: e.activation(out=rstd, in_=lnv, func=AF.Exp, scale=-0.5),
              reads=[tag + "lnv"], writes=[tag + "rstd"])

    lw_state = {"n": 0}

    def load_weight(dst, dst_res, src, K, N, stage, gain=None, gain_res=None, eng="pool"):
        CH = stage[2]
        srcv = src.rearrange("(k p) n -> p k n", p=128)
        for k0 in range(0, K, 8):
            for n0 in range(0, N, CH):
                nn = min(CH, N - n0)
                kk = min(8, K - k0)
                si = lw_state["n"] % 2
                ce = "act" if (lw_state["n"] % 2 == 0) else "pool"
                lw_state["n"] += 1
                sres = "wstage%d" % si
                stv = stage[si][:, 0:kk * nn].rearrange("p (k n) -> p k n", k=kk)
                P.add("sp", lambda e, stv=stv, k0=k0, kk=kk, n0=n0, nn=nn: e.dma_start(
                    out=stv, in_=srcv[:, k0:k0 + kk, n0:n0 + nn]), writes=[sres], slot=sres)
                if gain is None:
                    if ce == "act":
                        P.add("act", lambda e, stv=stv, k0=k0, kk=kk, n0=n0, nn=nn: e.activation(
                            out=dst[:, k0:k0 + kk, n0:n0 + nn], in_=stv, func=AF.Copy), reads=[sres], writes=[dst_res])
                    else:
                        P.add("pool", lambda e, stv=stv, k0=k0, kk=kk, n0=n0, nn=nn: e.tensor_copy(
                            out=dst[:, k0:k0 + kk, n0:n0 + nn], in_=stv), reads=[sres], writes=[dst_res])
                else:
                    for k in range(kk):
                        if ce == "act":
                            P.add("act", lambda e, stv=stv, k=k, k0=k0, n0=n0, nn=nn: e.activation(
                                out=dst[:, k0 + k, n0:n0 + nn], in_=stv[:, k, :], func=AF.Copy, scale=gain[:, k0 + k:k0 + k + 1]),
                                reads=[sres, gain_res], writes=[dst_res])
                        else:
                            P.add("pool", lambda e, stv=stv, k=k, k0=k0, n0=n0, nn=nn: e.tensor_scalar(
                                out=dst[:, k0 + k, n0:n0 + nn], in0=stv[:, k, :],
                                scalar1=gain[:, k0 + k:k0 + k + 1], scalar2=1.0, op0=ALU.mult, op1=ALU.mult),
                                reads=[sres, gain_res], writes=[dst_res])

    wstage = [A.alloc(8 * 256, F32), A.alloc(8 * 256, F32), 256]
    WA = A.alloc(8 * 768, BF16).rearrange("p (k n) -> p k n", k=8)
    gm = A.alloc(8, F32)
    QK = A.alloc(2 * T, BF16).rearrange("p (m t) -> p m t", m=2)
    VX = A.alloc(NB * 130, BF16).rearrange("p (t c) -> p t c", t=NB)
    XB = [A.alloc(D, F32) for _ in range(2)]
    RP = [A.alloc(64, F32) for _ in range(2)]
    sqj = A.alloc(D, BF16)
    xn = A.alloc(D, BF16)
    hpT = A.alloc(D, BF16).rearrange("p (k t) -> p k t", k=8)
    KV = [A.alloc(256, F32) for _ in range(2)]
    qkbf = A.alloc(256, BF16)
    vrbf = A.alloc(128, BF16)
    sg1 = A.alloc(128, F32)
    sg = A.alloc(128, F32)
    qkr = A.alloc(128, F32).rearrange("p (m d) -> p m d", m=2)
    rt = [A.alloc(64, F32).rearrange("p (m d) -> p m d", m=2) for _ in range(4)]
    rot = A.alloc(128, F32).rearrange("p (m d) -> p m d", m=2)
    qkp = A.alloc(128, BF16).rearrange("p (m d) -> p m d", m=2)
    qkT = A.alloc(256, BF16).rearrange("p (m t) -> p m t", m=2)
    ptr = A.alloc(128, BF16)
    Sst = A.alloc(128, F32)
    Stt = A.alloc(128, F32)
    Sbf = A.alloc(128, BF16)
    sm = A.alloc(16, F32)
    bt = A.alloc(256, F32).rearrange("p (m t) -> p m t", m=2)
    maskT = A.alloc(128, F32)
    lamb = A.alloc(256, F32)
    lamt = A.alloc(128, F32)
    lam8 = A.alloc(8, F32)
    sub8 = A.alloc(128, F32)
    PT = [[A.alloc(512, BF16) for _ in range(2)] for _ in range(2)]
    tmpn = [[A.alloc(128, F32) for _ in range(2)] for _ in range(2)]
    oa = A.alloc(128, F32)
    oa2 = A.alloc(128, F32)
    MRG = [A.alloc(256, BF16) for _ in range(2)]

    P.add("sp", lambda e: e.dma_start(out=gm, in_=gmix), writes=["gm"], slot="gm")
    load_weight(WA, "WA", wA, 8, 768, wstage, gain=gm, gain_res="gm", eng="dve")
    P.add("sp", lambda e: e.dma_start(out=bt.rearrange("p m t -> p (m t)"), in_=btin), writes=["bt"], slot="bt")
    P.add("sp", lambda e: e.dma_start(out=maskT, in_=maskT_d), writes=["maskT"], slot="maskT")
    P.add("sp", lambda e: e.dma_start(out=lamb, in_=lamv_d.partition_broadcast(128)), writes=["lamb"], slot="lamb")
    P.add("sp", lambda e: e.dma_start(out=sub8, in_=subln_d.partition_broadcast(128)), writes=["sub8"], slot="sub8")
    P.add("dve", lambda e: e.tensor_scalar(out=sub8, in0=sub8, scalar1=0.8, scalar2=None, op0=ALU.mult),
          reads=["sub8"], writes=["sub8"])
    P.add("dve", lambda e: e.tensor_tensor(out=lamt[:, 0:64], in0=lamb[:, 0:64], in1=lamb[:, 64:128], op=ALU.mult),
          reads=["lamb"], writes=["lamt"])
    P.add("dve", lambda e: e.tensor_tensor(out=lamt[:, 64:128], in0=lamb[:, 128:192], in1=lamb[:, 192:256], op=ALU.mult),
          reads=["lamb"], writes=["lamt"])
    P.add("dve", lambda e: e.tensor_reduce(out=lam8[:, 0:2], in_=lamt.rearrange("p (a b) -> p a b", a=2),
                                           axis=AX.X, op=ALU.add), reads=["lamt"], writes=["lam8"])
    P.add("act", lambda e: e.activation(out=lam8[:, 2:4], in_=lam8[:, 0:2], func=AF.Exp), reads=["lam8"], writes=["lam8"])
    P.add("dve", lambda e: e.tensor_tensor(out=lam8[:, 4:5], in0=lam8[:, 3:4], in1=lam8[:, 2:3], op=ALU.subtract),
          reads=["lam8"], writes=["lam8"])
    P.add("dve", lambda e: e.tensor_scalar(out=lam8[:, 5:6], in0=lam8[:, 4:5], scalar1=-0.2, scalar2=None, op0=ALU.add),
          reads=["lam8"], writes=["lam8"])
    neglam = lam8[:, 5:6]
    P.add("pool", lambda e: e.memset(VX[:, :, 128:130], 1.0), writes=["VXones"])
    P.add("dve", lambda e: e.memset(Sst, 0.0), writes=["S"])
    P.add("dve", lambda e: e.memset(Sbf, 0.0), writes=["Sbf"])

    qsc, ksc, gC, cfar = hc[:, 0:1], hc[:, 1:2], hc[:, 2:3], hc[:, 3:4]

    def attention(t, mi):
        nk = t + 1
        groups = [list(range(g0, min(g0 + 4, nk))) for g0 in range(0, nk, 4)]
        o1 = psf(7, 0, 129)
        o2 = psf(4, 256, 385)
        sbanks = [(5, 6), (0, 1)]

        def qk(gi):
            b1, b2 = sbanks[gi % 2]
            for j, kb in enumerate(groups[gi]):
                P.add("pe", lambda e, j=j, kb=kb, b1=b1: e.matmul(
                    psf(b1, j * 128, (j + 1) * 128), lhsT=QK[0:64, 1, kb * 128:(kb + 1) * 128],
                    rhs=QK[0:64, 0, t * 128:(t + 1) * 128], start=True, stop=True),
                    reads=[("QK", kb), ("QK", t)], writes=["ps%d" % b1])
                P.add("pe", lambda e, j=j, kb=kb, b2=b2: e.matmul(
                    psf(b2, j * 128, (j + 1) * 128), lhsT=QK[64:128, 1, kb * 128:(kb + 1) * 128],
                    rhs=QK[64:128, 0, t * 128:(t + 1) * 128], start=True, stop=True),
                    reads=[("QK", kb), ("QK", t)], writes=["ps%d" % b2])

        def soft(gi):
            bb = sbanks[gi % 2]
            buf = gi % 2
            kbs = groups[gi]
            nf = sum(1 for kb in kbs if kb <= t - 2)
            for m in range(2):
                b = bb[m]
                pt = PT[m][buf]
                ptres = "pt%d%d" % (m, buf)
                if nf > 0:
                    P.add("act", lambda e, b=b, pt=pt, nf=nf: e.activation(
                        out=pt[:, 0:nf * 128], in_=psf(b, 0, nf * 128), func=AF.Exp, bias=cfar, scale=0.125),
                        reads=["ps%d" % b, "hc"], writes=[ptres])
                for j, kb in enumerate(kbs):
                    if kb <= t - 2:
                        continue
                    w = kb - (t - 1)
                    tm = tmpn[m][w]
                    tres = "tmpn%d%d" % (m, w)
                    P.add("dve", lambda e, b=b, j=j, w=w, tm=tm: e.scalar_tensor_tensor(
                        out=tm, in0=psf(b, j * 128, (j + 1) * 128), scalar=0.125, in1=bt[:, w, :],
                        op0=ALU.mult, op1=ALU.add), reads=["ps%d" % b, "bt"], writes=[tres])
                    P.add("act", lambda e, j=j, tm=tm, pt=pt: e.activation(
                        out=pt[:, j * 128:(j + 1) * 128], in_=tm, func=AF.Exp), reads=[tres], writes=[ptres])

        def pv(gi):
            buf = gi % 2
            for j, kb in enumerate(groups[gi]):
                P.add("pe", lambda e, j=j, kb=kb, buf=buf: e.matmul(
                    o1, lhsT=PT[0][buf][:, j * 128:(j + 1) * 128], rhs=VX[:, kb, 0:129],
                    start=(kb == 0), stop=(kb == t)),
                    reads=["pt0%d" % buf, ("V", kb), "VXones"], writes=["ps7"])
                P.add("pe", lambda e, j=j, kb=kb, buf=buf: e.matmul(
                    o2, lhsT=PT[1][buf][:, j * 128:(j + 1) * 128], rhs=VX[:, kb, 0:129],
                    start=(kb == 0), stop=(kb == t)),
                    reads=["pt1%d" % buf, ("V", kb), "VXones"], writes=["ps4b"])

        ng = len(groups)
        for gi in range(ng):
            qk(gi)
            soft(gi)
            if gi > 0:
                pv(gi - 1)
        pv(ng - 1)
        r1, r2 = sm[:, 4:5], sm[:, 5:6]
        P.add("dve", lambda e: e.reciprocal(out=r1, in_=psf(7, 128, 129)), reads=["ps7"], writes=["r1"])
        P.add("dve", lambda e: e.reciprocal(out=r2, in_=psf(4, 384, 385)), reads=["ps4b"], writes=["r2"])
        P.add("dve", lambda e: e.tensor_tensor(out=r2, in0=r2, in1=neglam, op=ALU.mult), reads=["r2", "lam8"], writes=["r2"])
        P.add("dve", lambda e: e.tensor_scalar(out=oa, in0=psf(7, 0, 128), scalar1=r1, scalar2=None, op0=ALU.mult),
              reads=["ps7", "r1"], writes=["oa"])
        P.add("dve", lambda e: e.scalar_tensor_tensor(out=oa2, in0=psf(4, 256, 384), scalar=r2, in1=oa,
                                                      op0=ALU.mult, op1=ALU.add),
              reads=["ps4b", "r2", "oa"], writes=["oa2"])
        rstd_ops(oa2, "oa2", 128, sqj[:, 0:128], "sqj", sm[:, 6:7], sm[:, 7:8], sm[:, 8:9], "a_")
        P.add("dve", lambda e: e.scalar_tensor_tensor(out=MRG[mi][:, 0:128], in0=oa2, scalar=sm[:, 8:9], in1=sub8,
                                                      op0=ALU.mult, op1=ALU.mult),
              reads=["oa2", "a_rstd", "sub8"], writes=["mrg%d" % mi])

    def block_front(t):
        i = t % 2
        x = XB[i]
        xr = "x%d" % i
        P.add("sp", lambda e: e.dma_start(out=x, in_=xb[t * 128:(t + 1) * 128, :]), writes=[xr], slot=xr)
        P.add("sp", lambda e: e.dma_start(out=RP[i], in_=rope[t]), writes=["rp%d" % i], slot="rp%d" % i)
        rstd_ops(x, xr, D, sqj, "sqj", sm[:, 0:1], sm[:, 1:2], sm[:, 2:3], "x_")
        P.add("dve", lambda e: e.tensor_scalar(out=xn, in0=x, scalar1=sm[:, 2:3], scalar2=None, op0=ALU.mult),
              reads=[xr, "x_rstd"], writes=["xn"])
        for k in range(8):
            P.add("pe", lambda e, k=k: e.transpose(out=psb(0)[:, k * 128:(k + 1) * 128], in_=xn[:, k * 128:(k + 1) * 128],
                                                   identity=ident), reads=["xn", "ident"], writes=["ps0"])
        P.add("act", lambda e: e.activation(out=hpT.rearrange("p k t -> p (k t)"), in_=psb(0), func=AF.Copy),
              reads=["ps0"], writes=["hpT"])
        for k in range(8):
            P.add("pe", lambda e, k=k: e.matmul(psf(1), lhsT=hpT[:, k, :], rhs=WA[:, k, 0:512], start=(k == 0), stop=(k == 7)),
                  reads=["hpT", "WA"], writes=["ps1"])
        for k in range(8):
            P.add("pe", lambda e, k=k: e.matmul(psf(2, 0, 256), lhsT=hpT[:, k, :], rhs=WA[:, k, 512:768], start=(k == 0), stop=(k == 7)),
                  reads=["hpT", "WA"], writes=["ps2"])
        kv = KV[i]
        P.add("dve", lambda e: e.tensor_copy(out=kv, in_=psf(1, 128, 384)), reads=["ps1"], writes=["kv%d" % i])
        P.add("sp", lambda e: e.dma_start(out=kout[t * 128:(t + 1) * 128, :], in_=kv[:, 0:128]),
              reads=["kv%d" % i], writes=[("kout", t)], slot="kv%d" % i)
        P.add("sp", lambda e: e.dma_start(out=vout[t * 128:(t + 1) * 128, :], in_=kv[:, 128:256]),
              reads=["kv%d" % i], writes=[("vout", t)], slot="kv%d" % i)
        P.add("dve", lambda e: e.tensor_copy(out=qkbf, in_=psf(1, 0, 256)), reads=["ps1"], writes=["qkbf"])
        P.add("act", lambda e: e.activation(out=vrbf, in_=psf(1, 384, 512), func=AF.Copy), reads=["ps1"], writes=["vrbf"])
        P.add("act", lambda e: e.activation(out=VX[:, t, 0:128], in_=psf(1, 256, 384), func=AF.Copy),
              reads=["ps1"], writes=[("V", t)])
        for m in range(2):
            P.add("pe", lambda e, m=m: e.transpose(out=psb(3)[:, m * 128:(m + 1) * 128], in_=qkbf[:, m * 128:(m + 1) * 128],
                                                   identity=ident), reads=["qkbf", "ident"], writes=["ps3A"])
        P.add("dve", lambda e: e.tensor_copy(out=QK[:, :, t * 128:(t + 1) * 128],
                                             in_=psb(3)[:, 0:256].rearrange("p (m t) -> p m t", m=2)),
              reads=["ps3A"], writes=[("QK", t)])
        P.add("act", lambda e: e.activation(out=sg1, in_=psf(2, 0, 128), func=AF.Exp, scale=-1.0), reads=["ps2"], writes=["sg1"])
        P.add("dve", lambda e: e.tensor_scalar(out=sg1, in0=sg1, scalar1=1.0, scalar2=None, op0=ALU.add), reads=["sg1"], writes=["sg1"])
        P.add("dve", lambda e: e.reciprocal(out=sg1, in_=sg1), reads=["sg1"], writes=["sg1"])
        P.add("dve", lambda e: e.tensor_tensor(out=sg, in0=psf(2, 0, 128), in1=sg1, op=ALU.mult), reads=["ps2", "sg1"], writes=["sg"])
        P.add("dve", lambda e: e.tensor_copy(out=qkr.rearrange("p m d -> p (m d)"), in_=psf(2, 128, 256)), reads=["ps2"], writes=["qkr"])
        cosb = RP[i][:, 0:32].unsqueeze(1).to_broadcast([128, 2, 32])
        sinb = RP[i][:, 32:64].unsqueeze(1).to_broadcast([128, 2, 32])
        x1, x2 = qkr[:, :, 0:32], qkr[:, :, 32:64]
        rpr = "rp%d" % i
        P.add("pool", lambda e: e.tensor_tensor(out=rt[0], in0=x1, in1=cosb, op=ALU.mult), reads=["qkr", rpr], writes=["rt0"])
        P.add("pool", lambda e: e.tensor_tensor(out=rt[1], in0=x2, in1=sinb, op=ALU.mult), reads=["qkr", rpr], writes=["rt1"])
        P.add("pool", lambda e: e.tensor_tensor(out=rt[2], in0=x2, in1=cosb, op=ALU.mult), reads=["qkr", rpr], writes=["rt2"])
        P.add("pool", lambda e: e.tensor_tensor(out=rt[3], in0=x1, in1=sinb, op=ALU.mult), reads=["qkr", rpr], writes=["rt3"])
        P.add("pool", lambda e: e.tensor_tensor(out=rot[:, :, 0:32], in0=rt[0], in1=rt[1], op=ALU.subtract),
              reads=["rt0", "rt1"], writes=["rot"])
        P.add("pool", lambda e: e.tensor_tensor(out=rot[:, :, 32:64], in0=rt[2], in1=rt[3], op=ALU.add),
              reads=["rt2", "rt3"], writes=["rot"])
        P.add("dve", lambda e: e.tensor_scalar(out=qkp[:, 0, :], in0=rot[:, 0, :], scalar1=qsc, scalar2=None, op0=ALU.mult),
              reads=["rot", "hc"], writes=["qkp"])
        P.add("dve", lambda e: e.tensor_scalar(out=qkp[:, 1, :], in0=rot[:, 1, :], scalar1=ksc, scalar2=None, op0=ALU.mult),
              reads=["rot", "hc"], writes=["qkp"])
        for m in range(2):
            P.add("pe", lambda e, m=m: e.transpose(out=psb(3)[0:64, 256 + m * 128:256 + (m + 1) * 128], in_=qkp[:, m, :],
                                                   identity=ident), reads=["qkp", "ident"], writes=["ps3B"])
        P.add("act", lambda e: e.activation(out=qkT[0:64].rearrange("p m t -> p (m t)"), in_=psb(3)[0:64, 256:512], func=AF.Copy),
              reads=["ps3B"], writes=["qkT"])
        P.add("pe", lambda e: e.matmul(psf(3, 256, 384), lhsT=qkT[0:64, 1, :], rhs=qkT[0:64, 0, :], start=True, stop=True),
              reads=["qkT"], writes=["ps3C"])
        P.add("dve", lambda e: e.tensor_tensor(out=ptr, in0=psf(3, 256, 384), in1=maskT, op=ALU.mult),
              reads=["ps3C", "maskT"], writes=["ptr"])
        P.add("pe", lambda e: e.matmul(psf(4, 0, 128), lhsT=ptr, rhs=vrbf, start=True, stop=False),
              reads=["ptr", "vrbf"], writes=["ps4a"])
        P.add("pe", lambda e: e.matmul(psf(4, 0, 128), lhsT=qkT[0:64, 0, :], rhs=Sbf[0:64, :], start=False, stop=True),
              reads=["qkT", "Sbf"], writes=["ps4a"])
        P.add("pe", lambda e: e.matmul(psf(3, 384, 512, 0, 64), lhsT=qkp[:, 1, :], rhs=vrbf, start=True, stop=True),
              reads=["qkp", "vrbf"], writes=["ps3D"])
        P.add("dve", lambda e: e.tensor_tensor(out=Stt[0:64], in0=Sst[0:64], in1=psf(3, 384, 512, 0, 64), op=ALU.add),
              reads=["S", "ps3D"], writes=["Stt"])
        P.add("dve", lambda e: e.tensor_scalar(out=Sst[0:64], in0=Stt[0:64], scalar1=gC[0:64], scalar2=None, op0=ALU.mult),
              reads=["Stt", "hc"], writes=["S"])
        P.add("act", lambda e: e.activation(out=Sbf[0:64], in_=Sst[0:64], func=AF.Copy), reads=["S"], writes=["Sbf"])
        rstd_ops(psf(4, 0, 128), "ps4a", 128, sqj[:, 128:256], "sqj", sm[:, 9:10], sm[:, 10:11], sm[:, 11:12], "r_")
        P.add("dve", lambda e: e.scalar_tensor_tensor(out=MRG[i][:, 128:256], in0=psf(4, 0, 128), scalar=sm[:, 11:12], in1=sg,
                                                      op0=ALU.mult, op1=ALU.mult),
              reads=["ps4a", "r_rstd", "sg"], writes=["mrg%d" % i])

    for t in range(nbA):
        if do_front:
            block_front(t)
        if do_attn:
            attention(t, t % 2)
        mi = t % 2
        P.add("sp", lambda e, t=t, mi=mi: e.dma_start(out=Ex.ap()[t * 128:(t + 1) * 128, :], in_=MRG[mi]),
              reads=["mrg%d" % mi], writes=[("Ex", t)], slot="mrg%d" % mi)
        if do_cc and t % 16 == 15:
            q = t // 16
            P.add("pool", lambda e, q=q: e.collective_compute(
                "AllGather", ALU.bypass, replica_groups=[[0, 1, 2, 3], [4, 5, 6, 7]],
                ins=[Ex.ap()[q * 2048:(q + 1) * 2048, :].opt()], outs=[Gx.ap()[q * 8192:(q + 1) * 8192, :].opt()]),
                reads=[("Ex", tt) for tt in range(q * 16, q * 16 + 16)], writes=[("Gx", q)], slot="cc", inc=1)
    P.add("sp", lambda e: e.dma_start(out=sret, in_=Sst[0:64]), reads=["S"], writes=["sret"], slot="sret")
    peakA = A.peak

    if do_B:
        zd = A.alloc(3072, F32)
        smb = A.alloc(32, F32)
        sq16 = A.alloc(512, F32)
        a16 = A.alloc(512, F32)
        roped = A.alloc(64, F32)
        b1_base = A.off
        wtB = A.alloc(8 * 512, BF16).rearrange("p (k n) -> p k n", k=8)
        xdb = A.alloc(D, BF16)
        hdT = A.alloc(D, BF16).rearrange("p (k t) -> p k t", k=8)
        qb = A.alloc(512, F32)
        NKB = 8
        Kt = [A.alloc(512, F32) for _ in range(NKB)]
        Vt = [A.alloc(512, F32) for _ in range(NKB)]
        KN = A.alloc(512, F32)
        VN = A.alloc(512, F32)
        prod = A.alloc(512, F32)
        Sal = A.alloc(17 * 8, F32)
        Pal = A.alloc(17 * 8, F32)
        bcs = A.alloc(72, F32)
        ptb = A.alloc(256, I32)
        idxp = A.alloc(256, I32)
        selt = A.alloc(128, F32)
        OD = A.alloc(512, F32)
        Wsel = A.alloc(16, F32)
        peakB = A.off

        P.add("sp", lambda e: e.dma_start(out=bcs, in_=bconst_d), writes=["bcs"], slot="bcs")
        P.add("sp", lambda e: e.dma_start(out=ptb, in_=pt_d.partition_broadcast(128)), writes=["ptb"], slot="ptb")
        P.add("sp", lambda e: e.dma_start(out=roped[0:16], in_=roped_d), writes=["roped"], slot="roped")
        P.add("dve", lambda e: e.tensor_scalar(out=idxp, in0=ptb, scalar1=128.0, scalar2=bcs[:, 0:1], op0=ALU.mult, op1=ALU.add),
              reads=["ptb", "bcs"], writes=["idxp"])
        P.add("pool", lambda e: e.memset(KN, 0.0), writes=["KN"])
        P.add("pool", lambda e: e.memset(VN, 0.0), writes=["VN"])
        rstd_ops(XS, "XS", D, sqj, "sqj", smb[:, 0:1], smb[:, 1:2], smb[:, 2:3], "d_")
        P.add("dve", lambda e: e.tensor_scalar(out=xdb, in0=XS, scalar1=smb[:, 2:3], scalar2=None, op0=ALU.mult),
              reads=["XS", "d_rstd"], writes=["xdb"])
        for k in range(8):
            P.add("pe", lambda e, k=k: e.transpose(out=psb(7)[:, k * 128:(k + 1) * 128], in_=xdb[:, k * 128:(k + 1) * 128], identity=ident),
                  reads=["xdb", "ident"], writes=["ps7"])
        P.add("act", lambda e: e.activation(out=hdT.rearrange("p k t -> p (k t)"), in_=psb(7), func=AF.Copy), reads=["ps7"], writes=["hdT"])
        for ci in range(6):
            load_weight(wtB, "wtB", win_d[:, ci * 512:(ci + 1) * 512], 8, 512, wstage, gain=gm, gain_res="gm", eng="pool")
            for k in range(8):
                P.add("pe", lambda e, k=k: e.matmul(psf(4, 0, 512, 0, 16), lhsT=hdT[:, k, 0:16], rhs=wtB[:, k, :], start=(k == 0), stop=(k == 7)),
                      reads=["hdT", "wtB"], writes=["ps4"])
            P.add("dve", lambda e, ci=ci: e.tensor_copy(out=zd[0:16, ci * 512:(ci + 1) * 512], in_=psf(4, 0, 512, 0, 16)),
                  reads=["ps4"], writes=["zd"])
        P.add("sp", lambda e: e.dma_start(out=ks_o, in_=zd[0:16, 512:1024]), reads=["zd"], writes=["ks_o"], slot="zdo")
        P.add("sp", lambda e: e.dma_start(out=vs_o, in_=zd[0:16, 1024:1536]), reads=["zd"], writes=["vs_o"], slot="zdo")
        coef8 = smb[:, 3:4]
        P.add("dve", lambda e: e.scalar_tensor_tensor(out=coef8[0:8], in0=r84[0:8, 5:6], scalar=neglam[0:8], in1=r84[0:8, 4:5],
                                                      op0=ALU.mult, op1=ALU.add), reads=["r84", "lam8"], writes=["coef8"])
        dm8 = r84[0:8, 0:4].unsqueeze(2).to_broadcast([8, 4, 128])
        biasd = bcs[:, 1:69].rearrange("p (g h) -> p g h", g=17).unsqueeze(3).to_broadcast([128, 17, 4, 2])
        for n in range(16):
            P.add("dve", lambda e, n=n: e.tensor_copy(out=selt[0:16], in_=identf[0:16, n:n + 1].to_broadcast([16, 128])),
                  reads=["identf"], writes=["selt"])
            P.add("pe", lambda e: e.matmul(psf(0), lhsT=selt[0:16], rhs=zd[0:16, 0:512], start=True, stop=True),
                  reads=["selt", "zd"], writes=["ps0"])
            P.add("act", lambda e: e.activation(out=qb, in_=psf(0), func=AF.Copy, scale=0.125), reads=["ps0"], writes=["qb"])
            P.add("sp", lambda e, n=n: e.dma_start(out=KN[0:1, :], in_=zd[n:n + 1, 512:1024]), reads=["zd"], writes=["KN"], slot="KN")
            P.add("sp", lambda e, n=n: e.dma_start(out=VN[0:1, :], in_=zd[n:n + 1, 1024:1536]), reads=["zd"], writes=["VN"], slot="VN")
            for g in range(17):
                if g < 16:
                    kb, kres = Kt[g % NKB], "kt%d" % (g % NKB)
                    P.add("pool", lambda e, kb=kb, n=n, g=g: e.indirect_dma_start(
                        out=kb, out_offset=None, in_=ck_d,
                        in_offset=bass.IndirectOffsetOnAxis(ap=idxp[:, n * 16 + g:n * 16 + g + 1], axis=0)),
                        reads=["idxp"], writes=[kres], slot=kres)
                else:
                    kb, kres = KN, "KN"
                P.add("dve", lambda e, kb=kb: e.tensor_tensor(out=prod, in0=kb, in1=qb, op=ALU.mult), reads=[kres, "qb"], writes=["prod"])
                P.add("dve", lambda e, g=g: e.tensor_reduce(out=Sal[:, g * 8:(g + 1) * 8], in_=prod.rearrange("p (a d) -> p a d", a=8),
                                                            axis=AX.X, op=ALU.add), reads=["prod"], writes=["Sal"])
            Sal4 = Sal.rearrange("p (g h m) -> p g h m", g=17, h=4)
            P.add("dve", lambda e: e.tensor_tensor(out=Sal4, in0=Sal4, in1=biasd, op=ALU.add), reads=["Sal", "bcs"], writes=["Sal"])
            P.add("act", lambda e: e.activation(out=Pal, in_=Sal, func=AF.Exp), reads=["Sal"], writes=["Pal"])
            for g in range(17):
                if g < 16:
                    vb, vres = Vt[g % NKB], "vt%d" % (g % NKB)
                    P.add("pool", lambda e, vb=vb, n=n, g=g: e.indirect_dma_start(
                        out=vb, out_offset=None, in_=cv_d,
                        in_offset=bass.IndirectOffsetOnAxis(ap=idxp[:, n * 16 + g:n * 16 + g + 1], axis=0)),
                        reads=["idxp"], writes=[vres], slot=vres)
                else:
                    vb, vres = VN, "VN"
                P.add("pe", lambda e, g=g, vb=vb: e.matmul(psf(1, 0, 512, 0, 8), lhsT=Pal[:, g * 8:(g + 1) * 8], rhs=vb, start=(g == 0), stop=(g == 16)),
                      reads=["Pal", vres], writes=["ps1"])
                P.add("pe", lambda e, g=g: e.matmul(psf(2, 0, 1, 0, 8), lhsT=Pal[:, g * 8:(g + 1) * 8], rhs=onesf[:, 0:1], start=(g == 0), stop=(g == 16)),
                      reads=["Pal", "onesf"], writes=["ps2"])
            w8 = smb[:, 4:5]
            P.add("dve", lambda e: e.reciprocal(out=w8[0:8], in_=psf(2, 0, 1, 0, 8)), reads=["ps2"], writes=["w8"])
            P.add("dve", lambda e: e.tensor_tensor(out=w8[0:8], in0=w8[0:8], in1=coef8[0:8], op=ALU.mult), reads=["w8", "coef8"], writes=["w8"])
            P.add("dve", lambda e: e.tensor_tensor(out=OD[0:8].rearrange("p (h d) -> p h d", h=4),
                                                   in0=psf(1, 0, 512, 0, 8).rearrange("p (h d) -> p h d", h=4), in1=dm8, op=ALU.mult),
                  reads=["ps1", "r84"], writes=["OD"])
            P.add("dve", lambda e, n=n: e.tensor_scalar(out=Wsel[0:8], in0=E16[0:8, n * 16:(n + 1) * 16], scalar1=w8[0:8], scalar2=None, op0=ALU.mult),
                  reads=["E16", "w8"], writes=["Wsel"])
            P.add("pe", lambda e, n=n: e.matmul(psf(3, 0, 512, 0, 16), lhsT=Wsel[0:8], rhs=OD[0:8], start=(n == 0), stop=(n == 15)),
                  reads=["Wsel", "OD"], writes=["ps3"])
        MD4 = MD[0:16].rearrange("p (h c) -> p h c", h=4)
        P.add("act", lambda e: e.activation(out=sq16[0:16], in_=psf(3, 0, 512, 0, 16), func=AF.Square), reads=["ps3"], writes=["sq16"])
        P.add("dve", lambda e: e.tensor_reduce(out=smb[0:16, 8:12], in_=sq16[0:16].rearrange("p (h d) -> p h d", h=4), axis=AX.X, op=ALU.add),
              reads=["sq16"], writes=["ms4"])
        P.add("act", lambda e: e.activation(out=smb[0:16, 12:16], in_=smb[0:16, 8:12], func=AF.Ln, bias=epsb[0:16], scale=1.0 / 128),
              reads=["ms4", "epsb"], writes=["l4"])
        P.add("act", lambda e: e.activation(out=smb[0:16, 16:20], in_=smb[0:16, 12:16], func=AF.Exp, scale=-0.5), reads=["l4"], writes=["r4"])
        P.add("dve", lambda e: e.tensor_tensor(out=a16[0:16].rearrange("p (h d) -> p h d", h=4),
                                               in0=psf(3, 0, 512, 0, 16).rearrange("p (h d) -> p h d", h=4),
                                               in1=smb[0:16, 16:20].unsqueeze(2).to_broadcast([16, 4, 128]), op=ALU.mult),
              reads=["ps3", "r4"], writes=["a16"])
        P.add("dve", lambda e: e.tensor_tensor(out=MD4[:, :, 0:128], in0=a16[0:16].rearrange("p (h d) -> p h d", h=4),
                                               in1=sub8[0:16].unsqueeze(1).to_broadcast([16, 4, 128]), op=ALU.mult),
              reads=["a16", "sub8"], writes=["MD"])
        P.barrier()
        A.off = b1_base
        rtd = [A.alloc(8 * 32, F32).rearrange("p (a d) -> p a d", a=8) for _ in range(4)]
        rotd = A.alloc(512, F32).rearrange("p (a d) -> p a d", a=8)
        prd = A.alloc(256, F32)
        ord16 = A.alloc(512, F32).rearrange("p (h d) -> p h d", h=4)
        qTd = A.alloc(64, F32)
        QZ = A.alloc(4 * 16 * 16, F32).rearrange("p (h n c) -> p h n c", h=4, n=16)
        Sn = [A.alloc(512, F32).rearrange("p (h d) -> p h d", h=4) for _ in range(2)]
        Snw = [A.alloc(512, F32).rearrange("p (h d) -> p h d", h=4) for _ in range(2)]
        VZn = A.alloc(512, F32)
        sgd = A.alloc(512, F32)
        qkd = zd[0:16, 1536:2048].rearrange("p (a d) -> p a d", a=8)
        cosd = roped[0:16, 0:32].unsqueeze(1).to_broadcast([16, 8, 32])
        sind = roped[0:16, 32:64].unsqueeze(1).to_broadcast([16, 8, 32])
        d1, d2 = qkd[:, :, 0:32], qkd[:, :, 32:64]
        P.add("dve", lambda e: e.tensor_tensor(out=rtd[0][0:16], in0=d1, in1=cosd, op=ALU.mult), reads=["zd", "roped"], writes=["rtd0"])
        P.add("dve", lambda e: e.tensor_tensor(out=rtd[1][0:16], in0=d2, in1=sind, op=ALU.mult), reads=["zd", "roped"], writes=["rtd1"])
        P.add("dve", lambda e: e.tensor_tensor(out=rtd[2][0:16], in0=d2, in1=cosd, op=ALU.mult), reads=["zd", "roped"], writes=["rtd2"])
        P.add("dve", lambda e: e.tensor_tensor(out=rtd[3][0:16], in0=d1, in1=sind, op=ALU.mult), reads=["zd", "roped"], writes=["rtd3"])
        P.add("dve", lambda e: e.tensor_tensor(out=rotd[0:16, :, 0:32], in0=rtd[0][0:16], in1=rtd[1][0:16], op=ALU.subtract),
              reads=["rtd0", "rtd1"], writes=["rotd"])
        P.add("dve", lambda e: e.tensor_tensor(out=rotd[0:16, :, 32:64], in0=rtd[2][0:16], in1=rtd[3][0:16], op=ALU.add),
              reads=["rtd2", "rtd3"], writes=["rotd"])
        P.add("dve", lambda e: e.tensor_scalar(out=rotd[0:16, 4:8, :], in0=rotd[0:16, 4:8, :], scalar1=0.125, scalar2=None, op0=ALU.mult),
              reads=["rotd"], writes=["rotd"])
        P.add("dve", lambda e: e.tensor_tensor(out=prd[0:16].rearrange("p (h d) -> p h d", h=4), in0=rotd[0:16, 0:4, :], in1=rotd[0:16, 4:8, :], op=ALU.mult),
              reads=["rotd"], writes=["prd"])
        P.add("dve", lambda e: e.tensor_reduce(out=smb[0:16, 20:24], in_=prd[0:16].rearrange("p (h d) -> p h d", h=4), axis=AX.X, op=ALU.add),
              reads=["prd"], writes=["qk4"])
        vr4 = zd[0:16, 2048:2560].rearrange("p (h d) -> p h d", h=4)
        P.add("dve", lambda e: e.tensor_tensor(out=ord16[0:16], in0=vr4, in1=smb[0:16, 20:24].unsqueeze(2).to_broadcast([16, 4, 128]), op=ALU.mult),
              reads=["zd", "qk4"], writes=["ord16"])
        for h in range(4):
            P.add("pe", lambda e, h=h: e.transpose(out=psf(0, h * 16, (h + 1) * 16, 0, 64), in_=rotd[0:16, h, :], identity=identf[0:16, 0:16]),
                  reads=["rotd", "identf"], writes=["ps0"])
        P.add("dve", lambda e: e.tensor_copy(out=qTd[0:64], in_=psf(0, 0, 64, 0, 64)), reads=["ps0"], writes=["qTd"])
        qT3 = qTd[0:64].rearrange("p (h n) -> p h n", h=4)
        P.add("dve", lambda e: e.tensor_tensor(out=QZ[0:64], in0=qT3.unsqueeze(3).to_broadcast([64, 4, 16, 16]),
                                               in1=E16[0:64].rearrange("p (n c) -> p n c", n=16).unsqueeze(1).to_broadcast([64, 4, 16, 16]), op=ALU.mult),
              reads=["qTd", "E16"], writes=["QZ"])
        gams = [1.0 - 2.0 ** (-5.0 - h) for h in range(4)]
        for n in range(16):
            i = n % 2
            P.add("sp", lambda e, n=n, i=i: e.dma_start(out=Sn[i][0:64], in_=sst_d[n].rearrange("h d e -> d h e")), writes=["sn%d" % i], slot="sn%d" % i)
            for h in range(4):
                P.add("pe", lambda e, n=n, h=h, i=i: e.matmul(psf(4 + h, 0, 128, 0, 16), lhsT=QZ[0:64, h, n, :], rhs=Sn[i][0:64, h, :],
                                                              start=(n == 0), stop=(n == 15)),
                      reads=["QZ", "sn%d" % i], writes=["ps%d" % (4 + h)])
            P.add("dve", lambda e, n=n: e.tensor_scalar(out=VZn[0:16], in0=zd[0:16, 2048:2560], scalar1=identf[0:16, n:n + 1], scalar2=None, op0=ALU.mult),
                  reads=["zd", "identf"], writes=["VZn"])
            for h in range(4):
                P.add("pe", lambda e, h=h: e.matmul(psf(1, h * 128, (h + 1) * 128, 0, 64), lhsT=rotd[0:16, 4 + h, :], rhs=VZn[0:16, h * 128:(h + 1) * 128],
                                                    start=True, stop=True), reads=["rotd", "VZn"], writes=["ps1"])
            for h in range(4):
                P.add("dve", lambda e, h=h, i=i: e.scalar_tensor_tensor(out=Snw[i][0:64, h, :], in0=Sn[i][0:64, h, :], scalar=gams[h],
                                                                        in1=psf(1, h * 128, (h + 1) * 128, 0, 64), op0=ALU.mult, op1=ALU.add),
                      reads=["sn%d" % i, "ps1"], writes=["snw%d" % i])
            P.add("sp", lambda e, n=n, i=i: e.dma_start(out=ss_o[n].rearrange("h d e -> d h e"), in_=Snw[i][0:64]),
                  reads=["snw%d" % i], writes=[("ss_o", n)], slot="snw%d" % i)
        for h in range(4):
            P.add("dve", lambda e, h=h: e.scalar_tensor_tensor(out=ord16[0:16, h, :], in0=psf(4 + h, 0, 128, 0, 16), scalar=gams[h], in1=ord16[0:16, h, :],
                                                               op0=ALU.mult, op1=ALU.add), reads=["ps%d" % (4 + h), "ord16"], writes=["ord16"])
        o2d = ord16[0:16].rearrange("p h d -> p (h d)")
        P.add("act", lambda e: e.activation(out=sq16[0:16], in_=o2d, func=AF.Square), reads=["ord16"], writes=["sq16"])
        P.add("dve", lambda e: e.tensor_reduce(out=smb[0:16, 8:12], in_=sq16[0:16].rearrange("p (h d) -> p h d", h=4), axis=AX.X, op=ALU.add),
              reads=["sq16"], writes=["ms4"])
        P.add("act", lambda e: e.activation(out=smb[0:16, 12:16], in_=smb[0:16, 8:12], func=AF.Ln, bias=epsb[0:16], scale=1.0 / 128),
              reads=["ms4", "epsb"], writes=["l4"])
        P.add("act", lambda e: e.activation(out=smb[0:16, 16:20], in_=smb[0:16, 12:16], func=AF.Exp, scale=-0.5), reads=["l4"], writes=["r4"])
        grd = zd[0:16, 2560:3072]
        P.add("act", lambda e: e.activation(out=sgd[0:16], in_=grd, func=AF.Exp, scale=-1.0), reads=["zd"], writes=["sgd"])
        P.add("dve", lambda e: e.tensor_scalar(out=sgd[0:16], in0=sgd[0:16], scalar1=1.0, scalar2=None, op0=ALU.add), reads=["sgd"], writes=["sgd"])
        P.add("dve", lambda e: e.reciprocal(out=sgd[0:16], in_=sgd[0:16]), reads=["sgd"], writes=["sgd"])
        P.add("dve", lambda e: e.tensor_tensor(out=sgd[0:16], in0=sgd[0:16], in1=grd, op=ALU.mult), reads=["sgd", "zd"], writes=["sgd"])
        P.add("dve", lambda e: e.tensor_tensor(out=a16[0:16].rearrange("p (h d) -> p h d", h=4), in0=ord16[0:16],
                                               in1=smb[0:16, 16:20].unsqueeze(2).to_broadcast([16, 4, 128]), op=ALU.mult),
              reads=["ord16", "r4"], writes=["a16"])
        P.add("dve", lambda e: e.tensor_tensor(out=MD4[:, :, 128:256], in0=a16[0:16].rearrange("p (h d) -> p h d", h=4),
                                               in1=sgd[0:16].rearrange("p (h d) -> p h d", h=4), op=ALU.mult),
              reads=["a16", "sgd"], writes=["MD"])

    if not do_C:
        P.add("sp", None, reads=[("kout", t) for t in range(nbA)] + [("vout", t) for t in range(nbA)] + ["sret"] + [("Ex", t) for t in range(nbA)] + (["ks_o", "vs_o"] + [("ss_o", n) for n in range(16)] if do_B else []))
        P.emit(es, limit)
        es.close()
        return nc, dict(P.stats)
    P.barrier()
    A.off = phase_base
    NBLK = nblkC
    wstage_all = A.alloc(8 * 256, F32)
    wstage = [wstage_all[:, 0:1024], wstage_all[:, 1024:2048], 128]
    gains = A.alloc(24, F32)
    idxt = A.alloc(NOWN * 4, I32)
    X2 = A.alloc(NBLK * D, F32).rearrange("p (b n) -> p b n", b=NBLK)
    H2T = A.alloc(8 * (NBLK + 1) * 128, BF16).rearrange("p (k t) -> p k t", k=8)
    cT = A.alloc(D, BF16).rearrange("p (k t) -> p k t", k=8)
    hbf = A.alloc(D, BF16)
    smc = A.alloc(16, F32)
    sqc = hbf
    qmb = A.alloc(D, F32)
    sweep_base = A.off
    Wout = A.alloc(8 * D, BF16).rearrange("p (k n) -> p k n", k=8)
    Wmq = A.alloc(8 * D, BF16).rearrange("p (k n) -> p k n", k=8)
    Wmo = A.alloc(8 * D, BF16).rearrange("p (k n) -> p k n", k=8)
    mkT = A.alloc(8 * 256, BF16).rearrange("p (c m) -> p c m", c=8)
    mvx = A.alloc(2 * 4 * 258, BF16).rearrange("p (a h c) -> p a h c", a=2, h=4)
    memT = A.alloc(8 * 256, BF16).rearrange("p (k t) -> p k t", k=8)
    regR = A.off
    memx = A.alloc(2 * D, F32).rearrange("p (a n) -> p a n", a=2)
    memo = A.alloc(2 * 512, F32).rearrange("p (a n) -> p a n", a=2)
    wtmp = A.alloc(8 * 512, BF16).rearrange("p (k n) -> p k n", k=8)
    peakC1 = A.off

    P.add("sp", lambda e: e.dma_start(out=gains, in_=gains_d), writes=["gains"], slot="gains")
    P.add("sp", lambda e: e.dma_start(out=idxt, in_=idx_d), writes=["idxt"], slot="idxt")
    gmq, gkv, gmlp = gains[:, 0:8], gains[:, 8:16], gains[:, 16:24]
    load_weight(Wout, "Wout", wout_d, 8, D, wstage)
    load_weight(Wmq, "Wmq", wmq_d, 8, D, wstage, gain=gmq, gain_res="gains")
    load_weight(Wmo, "Wmo", wmo_d, 8, D, wstage)
    P.add("pool", lambda e: e.memset(mvx[:, :, :, 256:258], 1.0), writes=["mvxones"])

    def transposes_to(dst, dst_res, src, src_res, bank, nk=8, evac="act"):
        for k in range(nk):
            P.add("pe", lambda e, k=k: e.transpose(out=psb(bank)[:, k * 128:(k + 1) * 128], in_=src[:, k * 128:(k + 1) * 128],
                                                   identity=ident), reads=(src_res if isinstance(src_res, list) else [src_res]) + ["ident"], writes=["ps%d" % bank])
        if evac == "act":
            P.add("act", lambda e: e.activation(out=dst, in_=psb(bank)[:, 0:nk * 128].rearrange("p (k t) -> p k t", k=nk), func=AF.Copy),
                  reads=["ps%d" % bank], writes=[dst_res])
        else:
            P.add("dve", lambda e: e.tensor_copy(out=dst, in_=psb(bank)[:, 0:nk * 128].rearrange("p (k t) -> p k t", k=nk)),
                  reads=["ps%d" % bank], writes=[dst_res])

    P.add("sp", lambda e: e.dma_start(out=memx, in_=memp_d.rearrange("(a p) n -> p a n", p=128)), writes=["memx"], slot="memx")
    for a in range(2):
        rstd_ops(memx[:, a, :], "memx", D, sqc, "hbf", smc[:, 0:1], smc[:, 1:2], smc[:, 2:3], "m_")
        P.add("dve", lambda e, a=a: e.tensor_scalar(out=hbf, in0=memx[:, a, :], scalar1=smc[:, 2:3], scalar2=None, op0=ALU.mult),
              reads=["memx", "m_rstd"], writes=["hbf"])
        transposes_to(memT[:, :, a * 128:(a + 1) * 128], "memT", hbf, "hbf", 0)
    for which, wsrc, outd in (("k", wmk_d, memk_o), ("v", wmv_d, memv_o)):
        for half in range(2):
            load_weight(wtmp, "wtmp", wsrc[:, half * 512:(half + 1) * 512], 8, 512, wstage, gain=gkv, gain_res="gains")
            for a in range(2):
                for k in range(8):
                    P.add("pe", lambda e, a=a, k=k: e.matmul(psf(1 + a), lhsT=memT[:, k, a * 128:(a + 1) * 128], rhs=wtmp[:, k, :],
                                                             start=(k == 0), stop=(k == 7)),
                          reads=["memT", "wtmp"], writes=["ps%d" % (1 + a)])
                P.add("dve", lambda e, a=a: e.tensor_copy(out=memo[:, a, :], in_=psf(1 + a)),
                      reads=["ps%d" % (1 + a)], writes=["memo"])
                if which == "v":
                    P.add("act", lambda e, a=a, half=half: e.activation(
                        out=mvx[:, a, half * 2:half * 2 + 2, 0:256], in_=psf(1 + a).rearrange("p (h c) -> p h c", h=2), func=AF.Copy),
                        reads=["ps%d" % (1 + a)], writes=["mvx"])
            if which == "k":
                for ct in range(4):
                    for k in range(8):
                        P.add("pe", lambda e, ct=ct, k=k: e.matmul(psf(3, 0, 256), lhsT=wtmp[:, k, ct * 128:(ct + 1) * 128], rhs=memT[:, k, :],
                                                                   start=(k == 0), stop=(k == 7)),
                              reads=["memT", "wtmp"], writes=["ps3"])
                    P.add("act", lambda e, ct=ct, half=half: e.activation(out=mkT[:, half * 4 + ct, :], in_=psf(3, 0, 256), func=AF.Copy),
                          reads=["ps3"], writes=["mkT"])
            P.add("sp", lambda e, outd=outd, half=half: e.dma_start(
                out=outd.rearrange("(a p) n -> p a n", p=128)[:, :, half * 512:(half + 1) * 512], in_=memo),
                reads=["memo"], writes=[("memout", which, half)], slot="memo")

    def rms_to_T(x_ap, x_res, dst, dst_res, tag):
        rstd_ops(x_ap, x_res, D, sqc, "hbf", smc[:, 3:4], smc[:, 4:5], smc[:, 5:6], tag)
        P.add("dve", lambda e: e.tensor_scalar(out=hbf, in0=x_ap, scalar1=smc[:, 5:6], scalar2=None, op0=ALU.mult),
              reads=[x_res, tag + "rstd"], writes=["hbf"])
        transposes_to(dst, dst_res, hbf, "hbf", 0)

    def proj_add(x_ap, x_res, lT, lT_res, W, W_res):
        for nt in range(2):
            for k in range(8):
                P.add("pe", lambda e, nt=nt, k=k: e.matmul(psf(1 + nt), lhsT=lT[:, k, :], rhs=W[:, k, nt * 512:(nt + 1) * 512],
                                                           start=(k == 0), stop=(k == 7)),
                      reads=[lT_res, W_res], writes=["ps%d" % (1 + nt)])
            P.add("dve", lambda e, nt=nt: e.tensor_tensor(out=x_ap[:, nt * 512:(nt + 1) * 512], in0=x_ap[:, nt * 512:(nt + 1) * 512],
                                                          in1=psf(1 + nt), op=ALU.add),
                  reads=[x_res, "ps%d" % (1 + nt)], writes=[x_res])

    P.barrier()
    A.off = regR
    MGall = A.alloc(8 * 256, BF16)
    MG = [MGall[:, 0:1024], MGall[:, 1024:2048]]
    qmd = MGall.bitcast(F32)
    qmT = A.alloc(D, BF16).rearrange("p (k t) -> p k t", k=8)
    PTm = A.alloc(256, BF16)
    om = A.alloc(D, BF16)
    peakC1 = max(peakC1, A.off)
    mkt = wstage_all[:, 0:1024]
    mvt = wstage_all[:, 1024:2048]

    def mem_attend_prompt():
        for ct in range(8):
            bank = 5 + ct // 4
            for k in range(8):
                P.add("pe", lambda e, ct=ct, k=k, bank=bank: e.matmul(psf(bank, (ct % 4) * 128, (ct % 4 + 1) * 128),
                                                                     lhsT=Wmq[:, k, ct * 128:(ct + 1) * 128], rhs=cT[:, k, :],
                                                                     start=(k == 0), stop=(k == 7)),
                      reads=["cT", "Wmq"], writes=["ps%d" % bank])
        for hb in range(2):
            P.add("act", lambda e, hb=hb: e.activation(out=qmT[:, hb * 4:hb * 4 + 4, :], in_=psf(5 + hb).rearrange("p (c t) -> p c t", c=4), func=AF.Copy),
                  reads=["ps%d" % (5 + hb)], writes=["qmT"])
        for hm in range(4):
            for mc in range(2):
                for c in range(2):
                    P.add("pe", lambda e, hm=hm, mc=mc, c=c: e.matmul(psf(3, mc * 128, (mc + 1) * 128),
                                                                     lhsT=mkT[:, hm * 2 + c, mc * 128:(mc + 1) * 128], rhs=qmT[:, hm * 2 + c, :],
                                                                     start=(c == 0), stop=(c == 1)),
                          reads=["mkT", "qmT"], writes=["ps3"])
            P.add("act", lambda e: e.activation(out=PTm, in_=psf(3, 0, 256), func=AF.Exp, scale=1.0 / 16.0), reads=["ps3"], writes=["PTm"])
            for mc in range(2):
                P.add("pe", lambda e, hm=hm, mc=mc: e.matmul(psf(4, 0, 257), lhsT=PTm[:, mc * 128:(mc + 1) * 128], rhs=mvx[:, mc, hm, 0:257],
                                                             start=(mc == 0), stop=(mc == 1)),
                      reads=["PTm", "mvx", "mvxones"], writes=["ps4"])
            P.add("dve", lambda e: e.reciprocal(out=smc[:, 6:7], in_=psf(4, 256, 257)), reads=["ps4"], writes=["rm"])
            P.add("dve", lambda e, hm=hm: e.tensor_scalar(out=om[:, hm * 256:(hm + 1) * 256], in0=psf(4, 0, 256), scalar1=smc[:, 6:7],
                                                          scalar2=None, op0=ALU.mult), reads=["ps4", "rm"], writes=["om"])

    def mem_attend_sample():
        mgres = [("mg%d" % i_, r_) for i_ in range(2) for r_ in range(4)]
        for nt in range(2):
            for k in range(8):
                P.add("pe", lambda e, nt=nt, k=k: e.matmul(psf(4 + nt, 0, 512, 0, 16), lhsT=cT[:, k, 0:16], rhs=Wmq[:, k, nt * 512:(nt + 1) * 512],
                                                           start=(k == 0), stop=(k == 7)), reads=["cT", "Wmq"], writes=["ps%d" % (4 + nt)])
            P.add("act", lambda e, nt=nt: e.activation(out=qmd[0:16, nt * 512:(nt + 1) * 512], in_=psf(4 + nt, 0, 512, 0, 16), func=AF.Copy, scale=1.0 / 16.0),
                  reads=["ps%d" % (4 + nt)], writes=mgres)
        P.add("pool", lambda e: e.memset(om, 0.0), writes=["om"])
        Sm = hid_s
        wf = Wout.rearrange("p k n -> p (k n)").bitcast(F32)
        mkts = [wf[:, 0:1024], wf[:, 1024:2048]]
        mvts = [wf[:, 2048:3072], wf[:, 3072:4096]]
        P.add("sp", None, writes=["Wout", "mkt0", "mkt1", "mvt0", "mvt1"])
        for n in range(16):
            P.add("dve", lambda e, n=n: e.tensor_copy(out=selq[0:16], in_=identf[0:16, n:n + 1].to_broadcast([16, 128])),
                  reads=["identf"], writes=["selq"])
            for nt in range(2):
                P.add("pe", lambda e, nt=nt: e.matmul(psf(4 + nt), lhsT=selq[0:16], rhs=qmd[0:16, nt * 512:(nt + 1) * 512], start=True, stop=True),
                      reads=["selq"] + mgres, writes=["ps%d" % (4 + nt)])
                P.add("act", lambda e, nt=nt: e.activation(out=qmb[:, nt * 512:(nt + 1) * 512], in_=psf(4 + nt), func=AF.Copy),
                      reads=["ps%d" % (4 + nt)], writes=["qmb"])
            for mc in range(2):
                mkt = mkts[mc]
                mkr = "mkt%d" % mc
                P.add("sp", lambda e, n=n, mc=mc, mkt=mkt: e.dma_start(out=mkt, in_=cmk_d[n, mc * 128:(mc + 1) * 128, :]), writes=[mkr], slot=mkr)
                P.add("sp", lambda e, n=n, mc=mc: e.dma_start(out=mvts[mc], in_=cmv_d[n, mc * 128:(mc + 1) * 128, :]), writes=["mvt%d" % mc], slot="mvt%d" % mc)
                P.add("pool", lambda e, mkt=mkt: e.tensor_tensor(out=mkt, in0=mkt, in1=qmb, op=ALU.mult), reads=["qmb"], writes=[mkr])
                P.add("dve", lambda e, mc=mc, mkt=mkt: e.tensor_reduce(out=Sm[:, mc * 4:(mc + 1) * 4], in_=mkt.rearrange("p (h d) -> p h d", h=4), axis=AX.X, op=ALU.add),
                      reads=[mkr], writes=["Sm"])
            P.add("act", lambda e: e.activation(out=Sm[:, 8:16], in_=Sm[:, 0:8], func=AF.Exp), reads=["Sm"], writes=["Pm"])
            for mc in range(2):
                for nt in range(2):
                    P.add("pe", lambda e, mc=mc, nt=nt: e.matmul(psf(1 + nt, 0, 512, 0, 4), lhsT=Sm[:, 8 + mc * 4:8 + (mc + 1) * 4], rhs=mvts[mc][:, nt * 512:(nt + 1) * 512],
                                                                start=(mc == 0), stop=(mc == 1)), reads=["Pm", "mvt%d" % mc], writes=["ps%d" % (1 + nt)])
                P.add("pe", lambda e, mc=mc: e.matmul(psf(3, 0, 1, 0, 4), lhsT=Sm[:, 8 + mc * 4:8 + (mc + 1) * 4], rhs=onesf[:, 0:1], start=(mc == 0), stop=(mc == 1)),
                      reads=["Pm", "onesf"], writes=["ps3"])
            w4 = smc[:, 13:14]
            P.add("dve", lambda e: e.reciprocal(out=w4[0:4], in_=psf(3, 0, 1, 0, 4)), reads=["ps3"], writes=["w4"])
            dm4 = identf[0:4, 0:4].unsqueeze(2).to_broadcast([4, 4, 256])
            for nt in range(2):
                P.add("dve", lambda e, nt=nt: e.tensor_tensor(out=OD4[0:4, nt * 512:(nt + 1) * 512].rearrange("p (h d) -> p h d", h=2),
                                                              in0=psf(1 + nt, 0, 512, 0, 4).rearrange("p (h d) -> p h d", h=2),
                                                              in1=identf[0:4, 2 * nt:2 * nt + 2].unsqueeze(2).to_broadcast([4, 2, 256]), op=ALU.mult),
                      reads=["ps%d" % (1 + nt), "identf"], writes=["OD4"])
            P.add("dve", lambda e, n=n: e.tensor_scalar(out=Wsel4[0:4], in0=E16[0:4, n * 16:(n + 1) * 16], scalar1=w4[0:4], scalar2=None, op0=ALU.mult),
                  reads=["E16", "w4"], writes=["Wsel4"])
            for nt in range(2):
                P.add("pe", lambda e, n=n, nt=nt: e.matmul(psf(6 + nt, 0, 512, 0, 16), lhsT=Wsel4[0:4], rhs=OD4[0:4, nt * 512:(nt + 1) * 512],
                                                           start=(n == 0), stop=(n == 15)), reads=["Wsel4", "OD4"], writes=["ps%d" % (6 + nt)])
        for nt in range(2):
            P.add("dve", lambda e, nt=nt: e.tensor_copy(out=om[0:16, nt * 512:(nt + 1) * 512], in_=psf(6 + nt, 0, 512, 0, 16)),
                  reads=["ps%d" % (6 + nt)], writes=["om"])

    def sweep1(blk, x, xr, mg, mgres, sample):
        transposes_to(cT, "cT", mg, mgres, 0)
        proj_add(x, xr, cT, "cT", Wout, "Wout")
        rms_to_T(x, xr, cT, "cT", "q_")
        if sample:
            mem_attend_sample()
        else:
            mem_attend_prompt()
        transposes_to(cT, "cT", om, "om", 0)
        proj_add(x, xr, cT, "cT", Wmo, "Wmo")
        rms_to_T(x, xr, H2T[:, :, blk * 128:(blk + 1) * 128], ("H2T", blk), "h_")

    for blk in range(NBLK):
        i = blk % 2
        x = X2[:, blk, :]
        xr = ("X2", blk)
        mg = MG[i]
        mgr = "mg%d" % i
        P.add("sp", lambda e, blk=blk, x=x: e.dma_start(out=x, in_=xown[blk * 128:(blk + 1) * 128, :]), writes=[xr], slot="x2ld%d" % i)
        for r in range(4):
            P.add("pool", lambda e, r=r, blk=blk, mg=mg: e.indirect_dma_start(
                out=mg[:, r * 256:(r + 1) * 256], out_offset=None, in_=Gx.ap(),
                in_offset=bass.IndirectOffsetOnAxis(ap=idxt[:, blk * 4 + r:blk * 4 + r + 1], axis=0)),
                reads=[("Gx", q_) for q_ in range(4)] + ["idxt"], writes=[(mgr, r)], slot="%s_%d" % (mgr, r))
        sweep1(blk, x, xr, mg, [(mgr, r) for r in range(4)], False)
    if do_B:
        sweep1(NBLK, XS, "XS", MD, ["MD"], True)

    P.barrier()
    A.off = sweep_base
    Wu = [A.alloc(8 * 512, BF16).rearrange("p (k n) -> p k n", k=8) for _ in range(2)]
    Wd = [A.alloc(4 * 1024, BF16).rearrange("p (k n) -> p k n", k=4) for _ in range(2)]
    hid = A.alloc(4 * 512, BF16).rearrange("p (f t) -> p f t", f=4)
    sqh = [A.alloc(512, F32) for _ in range(2)]
    yo = [A.alloc(D, F32) for _ in range(2)]
    nfin = A.alloc(D, F32)
    P.add("sp", lambda e: e.dma_start(out=nfin, in_=nfin_d.partition_broadcast(128)), writes=["nfin"], slot="nfin")
    NT4 = NBLK // 4

    def load_eighth(q8):
        i8 = q8 % 2
        load_weight(Wu[i8], "Wup%d" % i8, wup_d[:, q8 * 512:(q8 + 1) * 512], 8, 512, wstage, gain=gmlp, gain_res="gains")
        load_weight(Wd[i8], "Wdn%d" % i8, wdn_d[q8 * 512:(q8 + 1) * 512, :], 4, 1024, wstage)

    def mlp_tile(i8, ncols, rhs_of_k, h2res, xs):
        wur, wdr = "Wup%d" % i8, "Wdn%d" % i8
        for fc in range(4):
            bank = 5 + fc % 2
            for k in range(8):
                P.add("pe", lambda e, fc=fc, k=k, bank=bank: e.matmul(psf(bank, 0, ncols), lhsT=Wu[i8][:, k, fc * 128:(fc + 1) * 128],
                                                                  rhs=rhs_of_k(k), start=(k == 0), stop=(k == 7)),
                      reads=h2res + [wur], writes=["ps%d" % bank])
            sq = sqh[fc % 2]
            sqr = "sqh%d" % (fc % 2)
            P.add("act", lambda e, bank=bank, sq=sq: e.activation(out=sq[:, 0:ncols], in_=psf(bank, 0, ncols), func=AF.Square),
                  reads=["ps%d" % bank], writes=[sqr])
            P.add("dve", lambda e, bank=bank, sq=sq, fc=fc: e.scalar_tensor_tensor(out=hid[:, fc, 0:ncols], in0=psf(bank, 0, ncols), scalar=0.0,
                                                                               in1=sq[:, 0:ncols], op0=ALU.is_gt, op1=ALU.mult),
                  reads=["ps%d" % bank, sqr], writes=["hid"])
        for bb, (x, xr) in enumerate(xs):
            for nt in range(2):
                for fc in range(4):
                    P.add("pe", lambda e, nt=nt, fc=fc, bb=bb: e.matmul(psf(1 + nt), lhsT=hid[:, fc, bb * 128:(bb + 1) * 128],
                                                                     rhs=Wd[i8][:, fc, nt * 512:(nt + 1) * 512], start=(fc == 0), stop=(fc == 3)),
                          reads=["hid", wdr], writes=["ps%d" % (1 + nt)])
                P.add("dve", lambda e, nt=nt, x=x: e.tensor_tensor(out=x[:, nt * 512:(nt + 1) * 512], in0=x[:, nt * 512:(nt + 1) * 512],
                                                                  in1=psf(1 + nt), op=ALU.add),
                      reads=[xr, "ps%d" % (1 + nt)], writes=[xr])

    load_eighth(0)
    for q8 in range(8):
        if q8 + 1 < 8:
            load_eighth(q8 + 1)
        i8 = q8 % 2
        for tt in range(NT4):
            mlp_tile(i8, 512, (lambda k, tt=tt: H2T[:, k, tt * 512:(tt + 1) * 512]), [("H2T", tt * 4 + bb) for bb in range(4)],
                     [(X2[:, tt * 4 + bb, :], ("X2", tt * 4 + bb)) for bb in range(4)])
        if do_B:
            mlp_tile(i8, 128, (lambda k: H2T[:, k, NBLK * 128:(NBLK + 1) * 128]), [("H2T", NBLK)], [(XS, "XS")])
    for blk in range(NBLK + (1 if do_B else 0)):
        i = blk % 2
        x = X2[:, blk, :] if blk < NBLK else XS
        xr = ("X2", blk) if blk < NBLK else "XS"
        rstd_ops(x, xr, D, sqc, "hbf", smc[:, 7:8], smc[:, 8:9], smc[:, 9:10], "f_")
        P.add("dve", lambda e, x=x, i=i: e.scalar_tensor_tensor(out=yo[i], in0=x, scalar=smc[:, 9:10], in1=nfin, op0=ALU.mult, op1=ALU.mult),
              reads=[xr, "f_rstd", "nfin"], writes=["yo%d" % i])
        if blk < NBLK:
            P.add("sp", lambda e, blk=blk, i=i: e.dma_start(out=yown[blk * 128:(blk + 1) * 128, :], in_=yo[i]),
                  reads=["yo%d" % i], writes=[("yown", blk)], slot="yo%d" % i)
        else:
            P.add("sp", lambda e, i=i: e.dma_start(out=ys_o, in_=yo[i][0:16, :]), reads=["yo%d" % i], writes=["ys_o"], slot="yo%d" % i)

    P.add("sp", None, reads=[("yown", blk) for blk in range(NBLK)] + [("kout", t) for t in range(NB)]
          + [("vout", t) for t in range(NB)] + ["sret"] + [("memout", w, hf) for w in "kv" for hf in range(2)]
          + (["ys_o", "ks_o", "vs_o"] + [("ss_o", n) for n in range(16)] if do_B else []))
    P.emit(es, limit)
    es.close()
    return nc, dict(P.stats, peakA=peakA, peakC1=peakC1, peakC2=A.off)


def _t5_bucket_np(n):
    n = np.maximum(n, 0)
    nf = np.maximum(n, 1).astype(np.float32)
    large = 16 + (np.log(nf / np.float32(16)) / np.float32(math.log(128 / 16)) * np.float32(16)).astype(np.int32)
    large = np.minimum(large, 31)
    return np.where(n < 16, n, large)


def _rope_table():
    inv = (np.float32(10000.0) ** (-np.linspace(0.0, 1.0, 32, dtype=np.float32))).astype(np.float32)
    pos = np.arange(T, dtype=np.float32)
    ang = (pos[:, None] * inv[None, :]).astype(np.float32)
    tab = np.concatenate([np.cos(ang), np.sin(ang)], axis=1).astype(np.float32)
    return tab.reshape(NB, 128, 64)


_PROG = {}


def _get_prog(n_phys):
    if n_phys not in _PROG:
        _PROG[n_phys] = build_program(n_phys=n_phys)
    return _PROG[n_phys]


def kernel(x_prompt, x_sample, mem_prompt, cache_k, cache_v, page_table, state_ret,
           cache_mem_k, cache_mem_v, rel_bias, norm_mix, w_in, lambda_q1, lambda_k1,
           lambda_q2, lambda_k2, subln_a, w_out, norm_mem_q, norm_mem_kv, w_mq, w_mk,
           w_mv, w_mo, norm_mlp, w_up, w_down, norm_final, _compact_pools=False):
    f32 = np.float32
    x_prompt = np.asarray(x_prompt, f32)
    w_in0 = np.asarray(w_in, f32)[0]
    rel_bias = np.asarray(rel_bias, f32)
    cache_k = np.asarray(cache_k, f32)
    cache_v = np.asarray(cache_v, f32)
    page_table = np.asarray(page_table, np.int32)
    n_phys = cache_k.shape[1]
    if _compact_pools:
        n_phys = 256
    nc, stats = _get_prog(n_phys)
    ck_full = cache_k[0].reshape(-1, 512)
    cv_full = cache_v[0].reshape(-1, 512)
    x_sample = np.asarray(x_sample, f32)
    state_ret = np.asarray(state_ret, f32)
    cache_mem_k = np.asarray(cache_mem_k, f32)
    cache_mem_v = np.asarray(cache_mem_v, f32)
    e16 = np.ascontiguousarray(np.tile(np.eye(16, dtype=f32).reshape(1, 256), (64, 1)))
    r84 = np.zeros((8, 8), f32)
    for hh in range(4):
        for mm in range(2):
            r84[hh * 2 + mm, hh] = 1.0
            r84[hh * 2 + mm, 4 + mm] = 1.0
    inv = (np.float32(10000.0) ** (-np.linspace(0.0, 1.0, 32, dtype=np.float32))).astype(f32)
    angd = (np.float32(PAST) * inv).astype(f32)
    roped = np.ascontiguousarray(np.tile(np.concatenate([np.cos(angd), np.sin(angd)]).astype(f32)[None, :], (16, 1)))
    pp = np.arange(128)
    bconst_base = np.zeros((128, 72), f32)
    bconst_base[:, 0] = pp
    reld = PAST - (np.arange(16)[None, :] * 128 + pp[:, None])
    bkd = _t5_bucket_np(reld)

    rope = _rope_table()
    p = np.arange(128)
    maskT = (p[None, :] >= p[:, None]).astype(f32)
    ident = np.eye(128, dtype=f32)
    lamv = np.concatenate([np.asarray(a, f32)[0] for a in (lambda_q1, lambda_k1, lambda_q2, lambda_k2)])[None, :]
    subln = np.asarray(subln_a, f32)[0][None, :]
    wout_full = np.asarray(w_out, f32)[0]
    perm = np.concatenate([np.concatenate([np.arange(r * 128, (r + 1) * 128), 512 + np.arange(r * 128, (r + 1) * 128)]) for r in range(4)])
    wout_p = np.ascontiguousarray(wout_full[perm])

    def g8(v):
        return np.ascontiguousarray(np.asarray(v, f32).reshape(8, 128).T)

    gains = np.concatenate([g8(norm_mem_q[0]), g8(norm_mem_kv[0]), g8(norm_mlp[0])], axis=1)
    rel_d = p[None, :] - p[:, None]
    bk_sub = _t5_bucket_np(128 + rel_d)
    bk_dia = _t5_bucket_np(rel_d)

    in_maps = []
    for c in range(NCORES):
        b, h = c // 4, c % 4
        j = h
        cols = np.concatenate([
            np.arange(h * 128, (h + 1) * 128),
            512 + np.arange(h * 128, (h + 1) * 128),
            1024 + np.arange(h * 128, (h + 1) * 128),
            2048 + np.arange(h * 128, (h + 1) * 128),
            2560 + np.arange(h * 128, (h + 1) * 128),
            1536 + np.arange(h * 64, (h + 1) * 64),
            1792 + np.arange(h * 64, (h + 1) * 64),
        ])
        gam = 1.0 - 2.0 ** (-5.0 - h)
        hconst = np.zeros((128, 8), f32)
        hconst[:, 0] = gam ** (p + 1.0)
        hconst[:, 1] = (64.0 ** -0.5) * gam ** (-(p + 1.0))
        hconst[:, 2] = gam ** 128.0
        hconst[:, 3] = rel_bias[31, h]
        bt = np.empty((128, 2, 128), f32)
        bt[:, 0, :] = rel_bias[bk_sub, h]
        bt[:, 1, :] = np.where(rel_d >= 0, rel_bias[bk_dia, h], f32(NEG))
        idx = np.empty((128, NOWN * 4), np.int32)
        for blk in range(NOWN):
            for r in range(4):
                idx[:, blk * 4 + r] = j * 8192 + r * 2048 + blk * 128 + p
        bconst = bconst_base.copy()
        bd = np.full((128, 17, 4), f32(NEG), f32)
        bd[:, 0:16, :] = rel_bias[bkd, :]
        bd[0, 16, :] = rel_bias[0, :]
        bconst[:, 1:69] = bd.reshape(128, 68)
        ptc = page_table[c * 16:(c + 1) * 16]
        if _compact_pools:
            uniq, invp = np.unique(ptc.reshape(-1), return_inverse=True)
            ck_c = np.zeros((256 * 128, 512), f32)
            cv_c = np.zeros((256 * 128, 512), f32)
            ck_c[:len(uniq) * 128] = cache_k[0][uniq].reshape(-1, 512)
            cv_c[:len(uniq) * 128] = cache_v[0][uniq].reshape(-1, 512)
            ptc = invp.reshape(16, 16).astype(np.int32)
        else:
            ck_c, cv_c = ck_full, cv_full
        in_maps.append({
            "xd": np.ascontiguousarray(x_sample[c * 16:(c + 1) * 16, 0, :]),
            "win": w_in0,
            "ck": ck_c,
            "cv": cv_c,
            "pt": np.ascontiguousarray(ptc.reshape(1, 256)),
            "sst": np.ascontiguousarray(state_ret[0, c * 16:(c + 1) * 16]),
            "cmk": np.ascontiguousarray(cache_mem_k[0, c * 16:(c + 1) * 16].reshape(16, 256, D)),
            "cmv": np.ascontiguousarray(cache_mem_v[0, c * 16:(c + 1) * 16].reshape(16, 256, D)),
            "bconst": bconst,
            "e16": e16,
            "r84": r84,
            "roped": roped,
            "xb": x_prompt[b],
            "wA": np.ascontiguousarray(w_in0[:, cols]),
            "gmix": g8(norm_mix[0]),
            "rope": rope,
            "hconst": hconst,
            "bt": bt.reshape(128, 256),
            "maskT": maskT,
            "ident": ident,
            "lamv": lamv,
            "subln": subln,
            "xown": np.ascontiguousarray(x_prompt[b, j * 2048:(j + 1) * 2048]),
            "idx": idx,
            "wout": wout_p,
            "wmq": np.asarray(w_mq, f32)[0],
            "wmk": np.asarray(w_mk, f32)[0],
            "wmv": np.asarray(w_mv, f32)[0],
            "wmo": np.asarray(w_mo, f32)[0],
            "wup": np.asarray(w_up, f32)[0],
            "wdn": np.asarray(w_down, f32)[0],
            "gains": gains,
            "nfin": np.asarray(norm_final, f32)[None, :],
            "memp": np.asarray(mem_prompt, f32)[b],
        })
    res = run_bass_kernel_spmd(nc, in_maps, core_ids=list(range(NCORES)))
    R = res.results

    y_prompt = np.empty((2, T, D), f32)
    k_prompt = np.empty((1, 2, NB, 128, 4, 128), f32)
    v_prompt = np.empty((1, 2, NB, 128, 4, 128), f32)
    state_ret_prompt = np.empty((1, 2, 4, 64, 128), f32)
    mem_k_prompt = np.empty((1, 2, 256, 4, 256), f32)
    mem_v_prompt = np.empty((1, 2, 256, 4, 256), f32)
    for c in range(NCORES):
        b, h = c // 4, c % 4
        y_prompt[b, h * 2048:(h + 1) * 2048] = R[c]["yown"]
        k_prompt[0, b, :, :, h, :] = R[c]["kout"].reshape(NB, 128, 128)
        v_prompt[0, b, :, :, h, :] = R[c]["vout"].reshape(NB, 128, 128)
        state_ret_prompt[0, b, h] = R[c]["sret"]
        if h == 0:
            mem_k_prompt[0, b] = R[c]["memk"].reshape(256, 4, 256)
            mem_v_prompt[0, b] = R[c]["memv"].reshape(256, 4, 256)
    y_sample = np.empty((DEC, 1, D), f32)
    k_sample = np.empty((1, DEC, 1, 4, 128), f32)
    v_sample = np.empty((1, DEC, 1, 4, 128), f32)
    state_ret_sample = np.empty((1, DEC, 4, 64, 128), f32)
    for c in range(NCORES):
        y_sample[c * 16:(c + 1) * 16, 0] = R[c]["ys"]
        k_sample[0, c * 16:(c + 1) * 16, 0] = R[c]["ks"].reshape(16, 4, 128)
        v_sample[0, c * 16:(c + 1) * 16, 0] = R[c]["vs"].reshape(16, 4, 128)
        state_ret_sample[0, c * 16:(c + 1) * 16] = R[c]["ss"]
    return (y_prompt, y_sample, k_prompt, v_prompt, state_ret_prompt, mem_k_prompt, mem_v_prompt,
            k_sample, v_sample, state_ret_sample)
```

```python
import math
from contextlib import ExitStack

import numpy as np
import ml_dtypes

import concourse.bass as bass
import concourse.mybir as mybir
from concourse.bass_utils import run_bass_kernel_spmd

F32 = mybir.dt.float32
BF16 = mybir.dt.bfloat16
I32 = mybir.dt.int32
AF = mybir.ActivationFunctionType
ALU = mybir.AluOpType
AX = mybir.AxisListType

NCORES = 8
D = 1024
T = 8192
NB = T // 128
NOWN = 16
EPS = 1e-6
NEG = -30000.0
DEC = 128
PAST = 2048
NPG = PAST // 128
ENGS = ["sp", "act", "dve", "pe", "pool"]


class Prog:
    def __init__(self, nc):
        self.nc = nc
        self.ops = []
        self.last_writer = {}
        self.readers = {}

    def add(self, eng, fn, reads=(), writes=(), slot=None, inc=None):
        def _ps(r):
            return isinstance(r, str) and len(r) >= 3 and r.startswith("ps") and r[2].isdigit()
        writes = list(writes) + [r[:3] for r in reads if _ps(r)]
        writes = [r[:3] if _ps(r) else r for r in writes]
        reads = [r for r in reads if not _ps(r)]
        oid = len(self.ops)
        deps = set()
        for r in reads:
            w = self.last_writer.get(r)
            if w is not None:
                deps.add(w)
        for r in writes:
            w = self.last_writer.get(r)
            if w is not None:
                deps.add(w)
            for rd in self.readers.get(r, {}).values():
                deps.add(rd)
        dma = slot is not None
        rkey = ("dma", slot) if dma else eng
        for r in reads:
            self.readers.setdefault(r, {})[rkey] = oid
        for r in writes:
            self.last_writer[r] = oid
            self.readers[r] = {}
        deps.discard(oid)
        self.ops.append(dict(eng=eng, fn=fn, deps=deps, dma=dma, slot=slot, rw=(list(reads), list(writes)),
                             inc=(inc if inc is not None else (16 if dma else 1))))
        return oid

    def barrier(self):
        allres = list(self.last_writer.keys() | self.readers.keys())
        marks = []
        for e in ("act", "dve", "pool"):
            marks.append(self.add(e, self._bar_fn[e], writes=allres + ["bar_" + e]))
        for e in ENGS:
            self.add(e, None, reads=["bar_act", "bar_dve", "bar_pool"], writes=allres + ["barx_" + e])

    def emit(self, es, limit=None):
        nc = self.nc
        if limit is not None:
            self.ops = self.ops[:limit]
            self.ops.append(dict(eng="sp", fn=None, deps=set(i for i, o in enumerate(self.ops) if o["dma"] and o["fn"] is not None),
                                 dma=False, slot=None, inc=1))
        ops = self.ops
        sig = [False] * len(ops)

        def pruned(D_, o):
            return (D_["eng"] == "pe" and o["eng"] == "pe" and not D_["dma"] and not o["dma"])

        eff = []
        for o in ops:
            s_ = set()
            for d in o["deps"]:
                if ops[d]["fn"] is None:
                    s_ |= eff[d]
                else:
                    s_.add(d)
            eff.append(s_)
        for i, o in enumerate(ops):
            if o["dma"] and o["fn"] is not None:
                sig[i] = True
            for d in eff[i]:
                if pruned(ops[d], o):
                    continue
                sig[d] = True
        cnt = {}
        for i, o in enumerate(ops):
            if not sig[i]:
                o["sv"] = None
                continue
            key = ("dma", o["slot"]) if o["dma"] else o["eng"]
            cnt[key] = cnt.get(key, 0) + o["inc"]
            o["key"] = key
            o["sv"] = cnt[key]
        known = {e: {} for e in ENGS}
        nwaits = 0
        for i, o in enumerate(ops):
            kn = known[o["eng"]]
            need = {}
            for d in eff[i]:
                D_ = ops[d]
                if pruned(D_, o):
                    continue
                k, v = D_["key"], D_["sv"]
                if kn.get(k, 0) >= v:
                    continue
                if need.get(k, (0, None))[0] < v:
                    need[k] = (v, d)
            waits = []
            for k, (v, d) in need.items():
                if kn.get(k, 0) >= v:
                    continue
                waits.append((k, v))
                for kk, vv in ops[d]["vc"].items():
                    if kn.get(kk, 0) < vv:
                        kn[kk] = vv
                if kn.get(k, 0) < v:
                    kn[k] = v
            o["waits"] = waits
            nwaits += len(waits)
            vc = dict(kn)
            if sig[i]:
                vc[o["key"]] = max(vc.get(o["key"], 0), o["sv"])
            o["vc"] = vc
        sems = {}
        for n, k in enumerate(sorted(cnt.keys(), key=str)):
            sems[k] = es.enter_context(nc.semaphore("s%d" % n))
        self.stats = dict(nops=len(ops), nwaits=nwaits, nsems=len(sems))
        block = es.enter_context(nc.Block())

        def run(engname):
            def body(e):
                for o in ops:
                    if o["eng"] != engname:
                        continue
                    for (k, v) in o["waits"]:
                        e.wait_ge(sems[k], v)
                    if o["fn"] is not None:
                        ins = o["fn"](e)
                        if o["sv"] is not None:
                            ins.then_inc(sems[o["key"]], o["inc"])
            return body

        block.sync(run("sp"))
        block.scalar(run("act"))
        block.vector(run("dve"))
        block.tensor(run("pe"))
        block.gpsimd(run("pool"))


class Arena:
    def __init__(self, t, size):
        self.t = t
        self.size = size
        self.off = 0
        self.peak = 0

    def alloc(self, cols, dtype=BF16):
        n = cols * (2 if dtype in (F32, I32) else 1)
        self.off = (self.off + 15) // 16 * 16
        assert self.off + n <= self.size, ("SBUF arena overflow", self.off, n, self.size)
        ap = self.t[:, self.off:self.off + n]
        self.off += n
        self.peak = max(self.peak, self.off)
        if dtype != BF16:
            ap = ap.bitcast(dtype)
        return ap


def build_program(nbA=NB, do_cc=True, do_C=True, nblkC=NOWN, do_attn=True, do_front=True, limit=None, n_phys=2560, do_B=True):
    nc = bass.Bass("TRN2", target_bir_lowering=False)

    def din(name, shape, dt=F32):
        return nc.dram_tensor(name, list(shape), dt, kind="ExternalInput").ap()

    def dout(name, shape, dt=F32):
        return nc.dram_tensor(name, list(shape), dt, kind="ExternalOutput").ap()

    xb = din("xb", [T, D])
    wA = din("wA", [D, 768])
    gmix = din("gmix", [128, 8])
    rope = din("rope", [NB, 128, 64])
    hconst = din("hconst", [128, 8])
    btin = din("bt", [128, 256])
    maskT_d = din("maskT", [128, 128])
    ident_d = din("ident", [128, 128])
    lamv_d = din("lamv", [1, 256])
    subln_d = din("subln", [1, 128])
    xown = din("xown", [NOWN * 128, D])
    idx_d = din("idx", [128, NOWN * 4], I32)
    wout_d = din("wout", [D, D])
    wmq_d = din("wmq", [D, D])
    wmk_d = din("wmk", [D, D])
    wmv_d = din("wmv", [D, D])
    wmo_d = din("wmo", [D, D])
    wup_d = din("wup", [D, 4 * D])
    wdn_d = din("wdn", [4 * D, D])
    gains_d = din("gains", [128, 24])
    nfin_d = din("nfin", [1, D])
    memp_d = din("memp", [256, D])
    xd_d = din("xd", [16, D])
    win_d = din("win", [D, 3072])
    ck_d = din("ck", [n_phys * 128, 512])
    cv_d = din("cv", [n_phys * 128, 512])
    pt_d = din("pt", [1, 256], I32)
    sst_d = din("sst", [16, 4, 64, 128])
    cmk_d = din("cmk", [16, 256, D])
    cmv_d = din("cmv", [16, 256, D])
    bconst_d = din("bconst", [128, 72])
    e16_d = din("e16", [64, 256])
    r84_d = din("r84", [8, 8])
    roped_d = din("roped", [16, 64])
    ys_o = dout("ys", [16, D])
    ks_o = dout("ks", [16, 512])
    vs_o = dout("vs", [16, 512])
    ss_o = dout("ss", [16, 4, 64, 128])
    yown = dout("yown", [NOWN * 128, D])
    kout = dout("kout", [T, 128])
    vout = dout("vout", [T, 128])
    sret = dout("sret", [64, 128])
    memk_o = dout("memk", [256, D])
    memv_o = dout("memv", [256, D])
    Ex = nc.dram_tensor("Ex", [T, 256], BF16)
    Gx = nc.dram_tensor("Gx", [4 * T, 256], BF16)

    es = ExitStack()
    ARENA_COLS = 106000
    arena_t = es.enter_context(nc.sbuf_tensor("arena", [128, ARENA_COLS], BF16))
    PS = es.enter_context(nc.psum_tensor("ps", [128, 8, 512], F32))
    A = Arena(arena_t, ARENA_COLS)
    P = Prog(nc)

    def psf(b, lo=0, hi=512, p0=0, p1=128):
        return PS[p0:p1, b, lo:hi]

    def psb(b):
        return PS[:, b, :].bitcast(BF16)

    ident = A.alloc(128, BF16)
    hc = A.alloc(8, F32)
    epsb = A.alloc(1, F32)
    barscr = A.alloc(8, F32)
    P._bar_fn = {
        "act": lambda e: e.activation(out=barscr[:, 0:1], in_=barscr[:, 1:2], func=AF.Copy),
        "dve": lambda e: e.memset(barscr[:, 2:3], 0.0),
        "pool": lambda e: e.memset(barscr[:, 4:5], 0.0),
    }
    identf = A.alloc(128, F32)
    P.add("sp", lambda e: e.dma_start(out=identf, in_=ident_d), writes=["identf"], slot="ident")
    P.add("dve", lambda e: e.tensor_copy(out=ident, in_=identf), reads=["identf"], writes=["ident"])
    P.add("sp", lambda e: e.dma_start(out=hc, in_=hconst), writes=["hc"], slot="hc")
    P.add("dve", lambda e: e.memset(epsb, EPS), writes=["epsb"])
    P.add("dve", lambda e: e.memset(barscr, 0.0), writes=["barscr"])
    XS = A.alloc(D, F32)
    MD = A.alloc(D, BF16)
    E16 = A.alloc(256, F32)
    r84 = A.alloc(8, F32)
    onesf = A.alloc(2, F32)
    selq = A.alloc(128, F32)
    hid_s = A.alloc(16, F32)
    OD4 = A.alloc(D, F32)
    Wsel4 = A.alloc(16, F32)
    P.add("sp", lambda e: e.dma_start(out=E16[0:64], in_=e16_d), writes=["E16"], slot="E16")
    P.add("sp", lambda e: e.dma_start(out=r84[0:8], in_=r84_d), writes=["r84"], slot="r84")
    P.add("pool", lambda e: e.memset(onesf, 1.0), writes=["onesf"])
    P.add("pool", lambda e: e.memset(XS, 0.0), writes=["XS"])
    P.add("pool", lambda e: e.memset(MD, 0.0), writes=["MD"])
    P.add("sp", lambda e: e.dma_start(out=XS[0:16], in_=xd_d), writes=["XS"], slot="XS")
    phase_base = A.off

    def rstd_ops(src_ap, src_res, n, junk, junk_res, ssq, lnv, rstd, tag):
        P.add("act", lambda e: e.activation(out=junk, in_=src_ap, func=AF.Square, accum_out=ssq),
              reads=[src_res], writes=[junk_res, tag + "ssq"])
        P.add("act", lambda e: e.activation(out=lnv, in_=ssq, func=AF.Ln, bias=epsb, scale=1.0 / n),
              reads=[tag + "ssq", "epsb"], writes=[tag + "lnv"])
        P.add("act", lambda e: e.activation(out=rstd, in_=lnv, func=AF.Exp, scale=-0.5),
              reads=[tag + "lnv"], writes=[tag + "rstd"])

    lw_state = {"n": 0}

    def load_weight(dst, dst_res, src, K, N, stage, gain=None, gain_res=None, eng="pool"):
        CH = stage[2]
        srcv = src.rearrange("(k p) n -> p k n", p=128)
        for k0 in range(0, K, 8):
            for n0 in range(0, N, CH):
                nn = min(CH, N - n0)
                kk = min(8, K - k0)
                si = lw_state["n"] % 2
                ce = "act" if (lw_state["n"] % 2 == 0) else "pool"
                lw_state["n"] += 1
                sres = "wstage%d" % si
                stv = stage[si][:, 0:kk * nn].rearrange("p (k n) -> p k n", k=kk)
                P.add("sp", lambda e, stv=stv, k0=k0, kk=kk, n0=n0, nn=nn: e.dma_start(
                    out=stv, in_=srcv[:, k0:k0 + kk, n0:n0 + nn]), writes=[sres], slot=sres)
                if gain is None:
                    if ce == "act":
                        P.add("act", lambda e, stv=stv, k0=k0, kk=kk, n0=n0, nn=nn: e.activation(
                            out=dst[:, k0:k0 + kk, n0:n0 + nn], in_=stv, func=AF.Copy), reads=[sres], writes=[dst_res])
                    else:
                        P.add("pool", lambda e, stv=stv, k0=k0, kk=kk, n0=n0, nn=nn: e.tensor_copy(
                            out=dst[:, k0:k0 + kk, n0:n0 + nn], in_=stv), reads=[sres], writes=[dst_res])
                else:
                    for k in range(kk):
                        if ce == "act":
                            P.add("act", lambda e, stv=stv, k=k, k0=k0, n0=n0, nn=nn: e.activation(
                                out=dst[:, k0 + k, n0:n0 + nn], in_=stv[:, k, :], func=AF.Copy, scale=gain[:, k0 + k:k0 + k + 1]),
                                reads=[sres, gain_res], writes=[dst_res])
                        else:
                            P.add("pool", lambda e, stv=stv, k=k, k0=k0, n0=n0, nn=nn: e.tensor_scalar(
                                out=dst[:, k0 + k, n0:n0 + nn], in0=stv[:, k, :],
                                scalar1=gain[:, k0 + k:k0 + k + 1], scalar2=1.0, op0=ALU.mult, op1=ALU.mult),
                                reads=[sres, gain_res], writes=[dst_res])

    wstage = [A.alloc(8 * 256, F32), A.alloc(8 * 256, F32), 256]
    WA = A.alloc(8 * 768, BF16).rearrange("p (k n) -> p k n", k=8)
    gm = A.alloc(8, F32)
    QK = A.alloc(2 * T, BF16).rearrange("p (m t) -> p m t", m=2)
    VX = A.alloc(NB * 130, BF16).rearrange("p (t c) -> p t c", t=NB)
    XB = [A.alloc(D, F32) for _ in range(2)]
    RP = [A.alloc(64, F32) for _ in range(2)]
    sqj = A.alloc(D, BF16)
    xn = A.alloc(D, BF16)
    hpT = A.alloc(D, BF16).rearrange("p (k t) -> p k t", k=8)
    KV = [A.alloc(256, F32) for _ in range(2)]
    qkbf = A.alloc(256, BF16)
    vrbf = A.alloc(128, BF16)
    sg1 = A.alloc(128, F32)
    sg = A.alloc(128, F32)
    qkr = A.alloc(128, F32).rearrange("p (m d) -> p m d", m=2)
    rt = [A.alloc(64, F32).rearrange("p (m d) -> p m d", m=2) for _ in range(4)]
    rot = A.alloc(128, F32).rearrange("p (m d) -> p m d", m=2)
    qkp = A.alloc(128, BF16).rearrange("p (m d) -> p m d", m=2)
    qkT = A.alloc(256, BF16).rearrange("p (m t) -> p m t", m=2)
    ptr = A.alloc(128, BF16)
    Sst = A.alloc(128, F32)
    Stt = A.alloc(128, F32)
    Sbf = A.alloc(128, BF16)
    sm = A.alloc(16, F32)
    bt = A.alloc(256, F32).rearrange("p (m t) -> p m t", m=2)
    maskT = A.alloc(128, F32)
    lamb = A.alloc(256, F32)
    lamt = A.alloc(128, F32)
    lam8 = A.alloc(8, F32)
    sub8 = A.alloc(128, F32)
    PT = [[A.alloc(512, BF16) for _ in range(2)] for _ in range(2)]
    tmpn = [[A.alloc(128, F32) for _ in range(2)] for _ in range(2)]
    oa = A.alloc(128, F32)
    oa2 = A.alloc(128, F32)
    MRG = [A.alloc(256, BF16) for _ in range(2)]

    P.add("sp", lambda e: e.dma_start(out=gm, in_=gmix), writes=["gm"], slot="gm")
    load_weight(WA, "WA", wA, 8, 768, wstage, gain=gm, gain_res="gm", eng="dve")
    P.add("sp", lambda e: e.dma_start(out=bt.rearrange("p m t -> p (m t)"), in_=btin), writes=["bt"], slot="bt")
    P.add("sp", lambda e: e.dma_start(out=maskT, in_=maskT_d), writes=["maskT"], slot="maskT")
    P.add("sp", lambda e: e.dma_start(out=lamb, in_=lamv_d.partition_broadcast(128)), writes=["lamb"], slot="lamb")
    P.add("sp", lambda e: e.dma_start(out=sub8, in_=subln_d.partition_broadcast(128)), writes=["sub8"], slot="sub8")
    P.add("dve", lambda e: e.tensor_scalar(out=sub8, in0=sub8, scalar1=0.8, scalar2=None, op0=ALU.mult),
          reads=["sub8"], writes=["sub8"])
    P.add("dve", lambda e: e.tensor_tensor(out=lamt[:, 0:64], in0=lamb[:, 0:64], in1=lamb[:, 64:128], op=ALU.mult),
          reads=["lamb"], writes=["lamt"])
    P.add("dve", lambda e: e.tensor_tensor(out=lamt[:, 64:128], in0=lamb[:, 128:192], in1=lamb[:, 192:256], op=ALU.mult),
          reads=["lamb"], writes=["lamt"])
    P.add("dve", lambda e: e.tensor_reduce(out=lam8[:, 0:2], in_=lamt.rearrange("p (a b) -> p a b", a=2),
                                           axis=AX.X, op=ALU.add), reads=["lamt"], writes=["lam8"])
    P.add("act", lambda e: e.activation(out=lam8[:, 2:4], in_=lam8[:, 0:2], func=AF.Exp), reads=["lam8"], writes=["lam8"])
    P.add("dve", lambda e: e.tensor_tensor(out=lam8[:, 4:5], in0=lam8[:, 3:4], in1=lam8[:, 2:3], op=ALU.subtract),
          reads=["lam8"], writes=["lam8"])
    P.add("dve", lambda e: e.tensor_scalar(out=lam8[:, 5:6], in0=lam8[:, 4:5], scalar1=-0.2, scalar2=None, op0=ALU.add),
          reads=["lam8"], writes=["lam8"])
    neglam = lam8[:, 5:6]
    P.add("pool", lambda e: e.memset(VX[:, :, 128:130], 1.0), writes=["VXones"])
    P.add("dve", lambda e: e.memset(Sst, 0.0), writes=["S"])
    P.add("dve", lambda e: e.memset(Sbf, 0.0), writes=["Sbf"])

    qsc, ksc, gC, cfar = hc[:, 0:1], hc[:, 1:2], hc[:, 2:3], hc[:, 3:4]

    def attention(t, mi, inject=None):
        nk = t + 1
        groups = [list(range(g0, min(g0 + 4, nk))) for g0 in range(0, nk, 4)]
        o1 = psf(7, 0, 129)
        o2 = psf(4, 256, 385)
        sbanks = [(5, 6), (0, 1)]

        def qk(gi):
            b1, b2 = sbanks[gi % 2]
            for j, kb in enumerate(groups[gi]):
                P.add("pe", lambda e, j=j, kb=kb, b1=b1: e.matmul(
                    psf(b1, j * 128, (j + 1) * 128), lhsT=QK[0:64, 1, kb * 128:(kb + 1) * 128],
                    rhs=QK[0:64, 0, t * 128:(t + 1) * 128], start=True, stop=True),
                    reads=[("QK", kb), ("QK", t)], writes=["ps%d" % b1])
                P.add("pe", lambda e, j=j, kb=kb, b2=b2: e.matmul(
                    psf(b2, j * 128, (j + 1) * 128), lhsT=QK[64:128, 1, kb * 128:(kb + 1) * 128],
                    rhs=QK[64:128, 0, t * 128:(t + 1) * 128], start=True, stop=True),
                    reads=[("QK", kb), ("QK", t)], writes=["ps%d" % b2])

        def soft(gi):
            bb = sbanks[gi % 2]
            buf = gi % 2
            kbs = groups[gi]
            nf = sum(1 for kb in kbs if kb <= t - 2)
            for m in range(2):
                b = bb[m]
                pt = PT[m][buf]
                ptres = "pt%d%d" % (m, buf)
                if nf > 0:
                    P.add("act", lambda e, b=b, pt=pt, nf=nf: e.activation(
                        out=pt[:, 0:nf * 128], in_=psf(b, 0, nf * 128), func=AF.Exp, bias=cfar, scale=0.125),
                        reads=["ps%d" % b, "hc"], writes=[ptres])
                for j, kb in enumerate(kbs):
                    if kb <= t - 2:
                        continue
                    w = kb - (t - 1)
                    tm = tmpn[m][w]
                    tres = "tmpn%d%d" % (m, w)
                    P.add("dve", lambda e, b=b, j=j, w=w, tm=tm: e.scalar_tensor_tensor(
                        out=tm, in0=psf(b, j * 128, (j + 1) * 128), scalar=0.125, in1=bt[:, w, :],
                        op0=ALU.mult, op1=ALU.add), reads=["ps%d" % b, "bt"], writes=[tres])
                    P.add("act", lambda e, j=j, tm=tm, pt=pt: e.activation(
                        out=pt[:, j * 128:(j + 1) * 128], in_=tm, func=AF.Exp), reads=[tres], writes=[ptres])

        def pv(gi):
            buf = gi % 2
            for j, kb in enumerate(groups[gi]):
                P.add("pe", lambda e, j=j, kb=kb, buf=buf: e.matmul(
                    o1, lhsT=PT[0][buf][:, j * 128:(j + 1) * 128], rhs=VX[:, kb, 0:129],
                    start=(kb == 0), stop=(kb == t)),
                    reads=["pt0%d" % buf, ("V", kb), "VXones"], writes=["ps7"])
                P.add("pe", lambda e, j=j, kb=kb, buf=buf: e.matmul(
                    o2, lhsT=PT[1][buf][:, j * 128:(j + 1) * 128], rhs=VX[:, kb, 0:129],
                    start=(kb == 0), stop=(kb == t)),
                    reads=["pt1%d" % buf, ("V", kb), "VXones"], writes=["ps4b"])

        ng = len(groups)
        for gi in range(ng):
            qk(gi)
            soft(gi)
            if gi > 0:
                pv(gi - 1)
            if inject is not None and gi == ng // 2:
                inject()
        pv(ng - 1)
        r1, r2 = sm[:, 4:5], sm[:, 5:6]
        P.add("dve", lambda e: e.reciprocal(out=r1, in_=psf(7, 128, 129)), reads=["ps7"], writes=["r1"])
        P.add("dve", lambda e: e.reciprocal(out=r2, in_=psf(4, 384, 385)), reads=["ps4b"], writes=["r2"])
        P.add("dve", lambda e: e.tensor_tensor(out=r2, in0=r2, in1=neglam, op=ALU.mult), reads=["r2", "lam8"], writes=["r2"])
        P.add("dve", lambda e: e.tensor_scalar(out=oa, in0=psf(7, 0, 128), scalar1=r1, scalar2=None, op0=ALU.mult),
              reads=["ps7", "r1"], writes=["oa"])
        P.add("dve", lambda e: e.scalar_tensor_tensor(out=oa2, in0=psf(4, 256, 384), scalar=r2, in1=oa,
                                                      op0=ALU.mult, op1=ALU.add),
              reads=["ps4b", "r2", "oa"], writes=["oa2"])
        rstd_ops(oa2, "oa2", 128, sqj[:, 0:128], "sqj", sm[:, 6:7], sm[:, 7:8], sm[:, 8:9], "a_")
        P.add("dve", lambda e: e.scalar_tensor_tensor(out=MRG[mi][:, 0:128], in0=oa2, scalar=sm[:, 8:9], in1=sub8,
                                                      op0=ALU.mult, op1=ALU.mult),
              reads=["oa2", "a_rstd", "sub8"], writes=["mrg%d" % mi])

    def block_front(t):
        i = t % 2
        x = XB[i]
        xr = "x%d" % i
        P.add("sp", lambda e: e.dma_start(out=x, in_=xb[t * 128:(t + 1) * 128, :]), writes=[xr], slot=xr)
        P.add("sp", lambda e: e.dma_start(out=RP[i], in_=rope[t]), writes=["rp%d" % i], slot="rp%d" % i)
        rstd_ops(x, xr, D, sqj, "sqj", sm[:, 0:1], sm[:, 1:2], sm[:, 2:3], "x_")
        P.add("dve", lambda e: e.tensor_scalar(out=xn, in0=x, scalar1=sm[:, 2:3], scalar2=None, op0=ALU.mult),
              reads=[xr, "x_rstd"], writes=["xn"])
        for k in range(8):
            P.add("pe", lambda e, k=k: e.transpose(out=psb(0)[:, k * 128:(k + 1) * 128], in_=xn[:, k * 128:(k + 1) * 128],
                                                   identity=ident), reads=["xn", "ident"], writes=["ps0"])
        P.add("act", lambda e: e.activation(out=hpT.rearrange("p k t -> p (k t)"), in_=psb(0), func=AF.Copy),
              reads=["ps0"], writes=["hpT"])
        for k in range(8):
            P.add("pe", lambda e, k=k: e.matmul(psf(1), lhsT=hpT[:, k, :], rhs=WA[:, k, 0:512], start=(k == 0), stop=(k == 7)),
                  reads=["hpT", "WA"], writes=["ps1"])
        for k in range(8):
            P.add("pe", lambda e, k=k: e.matmul(psf(2, 0, 256), lhsT=hpT[:, k, :], rhs=WA[:, k, 512:768], start=(k == 0), stop=(k == 7)),
                  reads=["hpT", "WA"], writes=["ps2"])
        kv = KV[i]
        P.add("dve", lambda e: e.tensor_copy(out=kv, in_=psf(1, 128, 384)), reads=["ps1"], writes=["kv%d" % i])
        P.add("sp", lambda e: e.dma_start(out=kout[t * 128:(t + 1) * 128, :], in_=kv[:, 0:128]),
              reads=["kv%d" % i], writes=[("kout", t)], slot="kv%d" % i)
        P.add("sp", lambda e: e.dma_start(out=vout[t * 128:(t + 1) * 128, :], in_=kv[:, 128:256]),
              reads=["kv%d" % i], writes=[("vout", t)], slot="kv%d" % i)
        P.add("dve", lambda e: e.tensor_copy(out=qkbf, in_=psf(1, 0, 256)), reads=["ps1"], writes=["qkbf"])
        P.add("act", lambda e: e.activation(out=vrbf, in_=psf(1, 384, 512), func=AF.Copy), reads=["ps1"], writes=["vrbf"])
        P.add("act", lambda e: e.activation(out=VX[:, t, 0:128], in_=psf(1, 256, 384), func=AF.Copy),
              reads=["ps1"], writes=[("V", t)])
        for m in range(2):
            P.add("pe", lambda e, m=m: e.transpose(out=psb(3)[:, m * 128:(m + 1) * 128], in_=qkbf[:, m * 128:(m + 1) * 128],
                                                   identity=ident), reads=["qkbf", "ident"], writes=["ps3A"])
        P.add("dve", lambda e: e.tensor_copy(out=QK[:, :, t * 128:(t + 1) * 128],
                                             in_=psb(3)[:, 0:256].rearrange("p (m t) -> p m t", m=2)),
              reads=["ps3A"], writes=[("QK", t)])
        P.add("act", lambda e: e.activation(out=sg1, in_=psf(2, 0, 128), func=AF.Exp, scale=-1.0), reads=["ps2"], writes=["sg1"])
        P.add("dve", lambda e: e.tensor_scalar(out=sg1, in0=sg1, scalar1=1.0, scalar2=None, op0=ALU.add), reads=["sg1"], writes=["sg1"])
        P.add("dve", lambda e: e.reciprocal(out=sg1, in_=sg1), reads=["sg1"], writes=["sg1"])
        P.add("dve", lambda e: e.tensor_tensor(out=sg, in0=psf(2, 0, 128), in1=sg1, op=ALU.mult), reads=["ps2", "sg1"], writes=["sg"])
        P.add("dve", lambda e: e.tensor_copy(out=qkr.rearrange("p m d -> p (m d)"), in_=psf(2, 128, 256)), reads=["ps2"], writes=["qkr"])
        cosb = RP[i][:, 0:32].unsqueeze(1).to_broadcast([128, 2, 32])
        sinb = RP[i][:, 32:64].unsqueeze(1).to_broadcast([128, 2, 32])
        x1, x2 = qkr[:, :, 0:32], qkr[:, :, 32:64]
        rpr = "rp%d" % i
        P.add("pool", lambda e: e.tensor_tensor(out=rt[0], in0=x1, in1=cosb, op=ALU.mult), reads=["qkr", rpr], writes=["rt0"])
        P.add("pool", lambda e: e.tensor_tensor(out=rt[1], in0=x2, in1=sinb, op=ALU.mult), reads=["qkr", rpr], writes=["rt1"])
        P.add("pool", lambda e: e.tensor_tensor(out=rt[2], in0=x2, in1=cosb, op=ALU.mult), reads=["qkr", rpr], writes=["rt2"])
        P.add("pool", lambda e: e.tensor_tensor(out=rt[3], in0=x1, in1=sinb, op=ALU.mult), reads=["qkr", rpr], writes=["rt3"])
        P.add("pool", lambda e: e.tensor_tensor(out=rot[:, :, 0:32], in0=rt[0], in1=rt[1], op=ALU.subtract),
              reads=["rt0", "rt1"], writes=["rot"])
        P.add("pool", lambda e: e.tensor_tensor(out=rot[:, :, 32:64], in0=rt[2], in1=rt[3], op=ALU.add),
              reads=["rt2", "rt3"], writes=["rot"])
        P.add("dve", lambda e: e.tensor_scalar(out=qkp[:, 0, :], in0=rot[:, 0, :], scalar1=qsc, scalar2=None, op0=ALU.mult),
              reads=["rot", "hc"], writes=["qkp"])
        P.add("dve", lambda e: e.tensor_scalar(out=qkp[:, 1, :], in0=rot[:, 1, :], scalar1=ksc, scalar2=None, op0=ALU.mult),
              reads=["rot", "hc"], writes=["qkp"])
        for m in range(2):
            P.add("pe", lambda e, m=m: e.transpose(out=psb(3)[0:64, 256 + m * 128:256 + (m + 1) * 128], in_=qkp[:, m, :],
                                                   identity=ident), reads=["qkp", "ident"], writes=["ps3B"])
        P.add("act", lambda e: e.activation(out=qkT[0:64].rearrange("p m t -> p (m t)"), in_=psb(3)[0:64, 256:512], func=AF.Copy),
              reads=["ps3B"], writes=["qkT"])
        P.add("pe", lambda e: e.matmul(psf(3, 256, 384), lhsT=qkT[0:64, 1, :], rhs=qkT[0:64, 0, :], start=True, stop=True),
              reads=["qkT"], writes=["ps3C"])
        P.add("dve", lambda e: e.tensor_tensor(out=ptr, in0=psf(3, 256, 384), in1=maskT, op=ALU.mult),
              reads=["ps3C", "maskT"], writes=["ptr"])
        P.add("pe", lambda e: e.matmul(psf(2, 256, 384), lhsT=ptr, rhs=vrbf, start=True, stop=False),
              reads=["ptr", "vrbf"], writes=["ps2"])
        P.add("pe", lambda e: e.matmul(psf(2, 256, 384), lhsT=qkT[0:64, 0, :], rhs=Sbf[0:64, :], start=False, stop=True),
              reads=["qkT", "Sbf"], writes=["ps2"])
        P.add("pe", lambda e: e.matmul(psf(3, 384, 512, 0, 64), lhsT=qkp[:, 1, :], rhs=vrbf, start=True, stop=True),
              reads=["qkp", "vrbf"], writes=["ps3D"])
        P.add("dve", lambda e: e.tensor_tensor(out=Stt[0:64], in0=Sst[0:64], in1=psf(3, 384, 512, 0, 64), op=ALU.add),
              reads=["S", "ps3D"], writes=["Stt"])
        P.add("dve", lambda e: e.tensor_scalar(out=Sst[0:64], in0=Stt[0:64], scalar1=gC[0:64], scalar2=None, op0=ALU.mult),
              reads=["Stt", "hc"], writes=["S"])
        P.add("act", lambda e: e.activation(out=Sbf[0:64], in_=Sst[0:64], func=AF.Copy), reads=["S"], writes=["Sbf"])
        rstd_ops(psf(2, 256, 384), "ps2", 128, sqj[:, 128:256], "sqj", sm[:, 9:10], sm[:, 10:11], sm[:, 11:12], "r_")
        P.add("dve", lambda e: e.scalar_tensor_tensor(out=MRG[i][:, 128:256], in0=psf(2, 256, 384), scalar=sm[:, 11:12], in1=sg,
                                                      op0=ALU.mult, op1=ALU.mult),
              reads=["ps2", "r_rstd", "sg"], writes=["mrg%d" % i])

    bsetup = []
    if do_B:
        zd = A.alloc(3072, F32)
        smb = A.alloc(32, F32)
        sq16 = A.alloc(512, F32)
        a16 = A.alloc(512, F32)
        roped = A.alloc(64, F32)
        b1_base = A.off
        wtB = A.alloc(8 * 512, BF16).rearrange("p (k n) -> p k n", k=8)
        xdb = A.alloc(D, BF16)
        hdT = A.alloc(D, BF16).rearrange("p (k t) -> p k t", k=8)
        qb = A.alloc(512, F32)
        NKB = 8
        Kt = [A.alloc(512, F32) for _ in range(NKB)]
        Vt = [A.alloc(512, F32) for _ in range(NKB)]
        KN = A.alloc(512, F32)
        VN = A.alloc(512, F32)
        prod = A.alloc(512, F32)
        Sal = A.alloc(17 * 8, F32)
        Pal = A.alloc(17 * 8, F32)
        bcs = A.alloc(72, F32)
        ptb = A.alloc(256, I32)
        idxp = A.alloc(256, I32)
        selt = A.alloc(128, F32)
        OD = A.alloc(512, F32)
        Wsel = A.alloc(16, F32)
        peakB = A.off

        def _bs_head():
            P.add("sp", lambda e: e.dma_start(out=bcs, in_=bconst_d), writes=["bcs"], slot="bcs")
            P.add("sp", lambda e: e.dma_start(out=ptb, in_=pt_d.partition_broadcast(128)), writes=["ptb"], slot="ptb")
            P.add("sp", lambda e: e.dma_start(out=roped[0:16], in_=roped_d), writes=["roped"], slot="roped")
            P.add("dve", lambda e: e.tensor_scalar(out=idxp, in0=ptb, scalar1=128.0, scalar2=bcs[:, 0:1], op0=ALU.mult, op1=ALU.add),
                  reads=["ptb", "bcs"], writes=["idxp"])
            P.add("pool", lambda e: e.memset(KN, 0.0), writes=["KN"])
            P.add("pool", lambda e: e.memset(VN, 0.0), writes=["VN"])
            rstd_ops(XS, "XS", D, sqj, "sqj", smb[:, 0:1], smb[:, 1:2], smb[:, 2:3], "d_")
            P.add("dve", lambda e: e.tensor_scalar(out=xdb, in0=XS, scalar1=smb[:, 2:3], scalar2=None, op0=ALU.mult),
                  reads=["XS", "d_rstd"], writes=["xdb"])
            for k in range(8):
                P.add("pe", lambda e, k=k: e.transpose(out=psb(0)[:, k * 128:(k + 1) * 128], in_=xdb[:, k * 128:(k + 1) * 128], identity=ident),
                      reads=["xdb", "ident"], writes=["ps0"])
            P.add("act", lambda e: e.activation(out=hdT.rearrange("p k t -> p (k t)"), in_=psb(0), func=AF.Copy), reads=["ps0"], writes=["hdT"])

        def _bs_chunk(ci):
            load_weight(wtB, "wtB", win_d[:, ci * 512:(ci + 1) * 512], 8, 512, wstage, gain=gm, gain_res="gm", eng="pool")
            for k in range(8):
                P.add("pe", lambda e, k=k: e.matmul(psf(1, 0, 512, 0, 16), lhsT=hdT[:, k, 0:16], rhs=wtB[:, k, :], start=(k == 0), stop=(k == 7)),
                      reads=["hdT", "wtB"], writes=["ps1"])
            P.add("dve", lambda e, ci=ci: e.tensor_copy(out=zd[0:16, ci * 512:(ci + 1) * 512], in_=psf(1, 0, 512, 0, 16)),
                  reads=["ps1"], writes=["zd"])

        def _bs_tail():
            P.add("sp", lambda e: e.dma_start(out=ks_o, in_=zd[0:16, 512:1024]), reads=["zd"], writes=["ks_o"], slot="zdo")
            P.add("sp", lambda e: e.dma_start(out=vs_o, in_=zd[0:16, 1024:1536]), reads=["zd"], writes=["vs_o"], slot="zdo")

        bsetup = [_bs_head] + [(lambda ci=ci: _bs_chunk(ci)) for ci in range(6)] + [_bs_tail]

    BS_T0 = 50
    if do_front:
        block_front(0)
    for t in range(nbA):
        bs_piece = bsetup[t - BS_T0] if (nbA == NB and 0 <= t - BS_T0 < len(bsetup)) else None

        def nxt(t=t, bs_piece=bs_piece):
            if do_front and t + 1 < nbA:
                block_front(t + 1)
            if bs_piece is not None:
                bs_piece()
        if do_attn:
            attention(t, t % 2, inject=nxt)
        elif nxt is not None:
            nxt()
        mi = t % 2
        P.add("sp", lambda e, t=t, mi=mi: e.dma_start(out=Ex.ap()[t * 128:(t + 1) * 128, :], in_=MRG[mi]),
              reads=["mrg%d" % mi], writes=[("Ex", t)], slot="mrg%d" % mi)
        if do_cc and t % 16 == 15:
            q = t // 16
            P.add("pool", lambda e, q=q: e.collective_compute(
                "AllGather", ALU.bypass, replica_groups=[[0, 1, 2, 3], [4, 5, 6, 7]],
                ins=[Ex.ap()[q * 2048:(q + 1) * 2048, :].opt()], outs=[Gx.ap()[q * 8192:(q + 1) * 8192, :].opt()]),
                reads=[("Ex", tt) for tt in range(q * 16, q * 16 + 16)], writes=[("Gx", q)], slot="cc", inc=1)
    P.add("sp", lambda e: e.dma_start(out=sret, in_=Sst[0:64]), reads=["S"], writes=["sret"], slot="sret")
    if nbA != NB:
        for piece in bsetup:
            piece()
    peakA = A.peak

    if do_B:
        pass
        coef8 = smb[:, 3:4]
        P.add("dve", lambda e: e.scalar_tensor_tensor(out=coef8[0:8], in0=r84[0:8, 5:6], scalar=neglam[0:8], in1=r84[0:8, 4:5],
                                                      op0=ALU.mult, op1=ALU.add), reads=["r84", "lam8"], writes=["coef8"])
        dm8 = r84[0:8, 0:4].unsqueeze(2).to_broadcast([8, 4, 128])
        biasd = bcs[:, 1:69].rearrange("p (g h) -> p g h", g=17).unsqueeze(3).to_broadcast([128, 17, 4, 2])
        for n in range(16):
            P.add("dve", lambda e, n=n: e.tensor_copy(out=selt[0:16], in_=identf[0:16, n:n + 1].to_broadcast([16, 128])),
                  reads=["identf"], writes=["selt"])
            P.add("pe", lambda e: e.matmul(psf(0), lhsT=selt[0:16], rhs=zd[0:16, 0:512], start=True, stop=True),
                  reads=["selt", "zd"], writes=["ps0"])
            P.add("act", lambda e: e.activation(out=qb, in_=psf(0), func=AF.Copy, scale=0.125), reads=["ps0"], writes=["qb"])
            P.add("sp", lambda e, n=n: e.dma_start(out=KN[0:1, :], in_=zd[n:n + 1, 512:1024]), reads=["zd"], writes=["KN"], slot="KN")
            P.add("sp", lambda e, n=n: e.dma_start(out=VN[0:1, :], in_=zd[n:n + 1, 1024:1536]), reads=["zd"], writes=["VN"], slot="VN")
            for g in range(17):
                if g < 16:
                    kb, kres = Kt[g % NKB], "kt%d" % (g % NKB)
                    P.add("pool", lambda e, kb=kb, n=n, g=g: e.indirect_dma_start(
                        out=kb, out_offset=None, in_=ck_d,
                        in_offset=bass.IndirectOffsetOnAxis(ap=idxp[:, n * 16 + g:n * 16 + g + 1], axis=0)),
                        reads=["idxp"], writes=[kres], slot=kres)
                else:
                    kb, kres = KN, "KN"
                P.add("dve", lambda e, kb=kb: e.tensor_tensor(out=prod, in0=kb, in1=qb, op=ALU.mult), reads=[kres, "qb"], writes=["prod"])
                P.add("dve", lambda e, g=g: e.tensor_reduce(out=Sal[:, g * 8:(g + 1) * 8], in_=prod.rearrange("p (a d) -> p a d", a=8),
                                                            axis=AX.X, op=ALU.add), reads=["prod"], writes=["Sal"])
            Sal4 = Sal.rearrange("p (g h m) -> p g h m", g=17, h=4)
            P.add("dve", lambda e: e.tensor_tensor(out=Sal4, in0=Sal4, in1=biasd, op=ALU.add), reads=["Sal", "bcs"], writes=["Sal"])
            P.add("act", lambda e: e.activation(out=Pal, in_=Sal, func=AF.Exp), reads=["Sal"], writes=["Pal"])
            for g in range(17):
                if g < 16:
                    vb, vres = Vt[g % NKB], "vt%d" % (g % NKB)
                    P.add("pool", lambda e, vb=vb, n=n, g=g: e.indirect_dma_start(
                        out=vb, out_offset=None, in_=cv_d,
                        in_offset=bass.IndirectOffsetOnAxis(ap=idxp[:, n * 16 + g:n * 16 + g + 1], axis=0)),
                        reads=["idxp"], writes=[vres], slot=vres)
                else:
                    vb, vres = VN, "VN"
                P.add("pe", lambda e, g=g, vb=vb: e.matmul(psf(1, 0, 512, 0, 8), lhsT=Pal[:, g * 8:(g + 1) * 8], rhs=vb, start=(g == 0), stop=(g == 16)),
                      reads=["Pal", vres], writes=["ps1"])
                P.add("pe", lambda e, g=g: e.matmul(psf(2, 0, 1, 0, 8), lhsT=Pal[:, g * 8:(g + 1) * 8], rhs=onesf[:, 0:1], start=(g == 0), stop=(g == 16)),
                      reads=["Pal", "onesf"], writes=["ps2"])
            w8 = smb[:, 4:5]
            P.add("dve", lambda e: e.reciprocal(out=w8[0:8], in_=psf(2, 0, 1, 0, 8)), reads=["ps2"], writes=["w8"])
            P.add("dve", lambda e: e.tensor_tensor(out=w8[0:8], in0=w8[0:8], in1=coef8[0:8], op=ALU.mult), reads=["w8", "coef8"], writes=["w8"])
            P.add("dve", lambda e: e.tensor_tensor(out=OD[0:8].rearrange("p (h d) -> p h d", h=4),
                                                   in0=psf(1, 0, 512, 0, 8).rearrange("p (h d) -> p h d", h=4), in1=dm8, op=ALU.mult),
                  reads=["ps1", "r84"], writes=["OD"])
            P.add("dve", lambda e, n=n: e.tensor_scalar(out=Wsel[0:8], in0=E16[0:8, n * 16:(n + 1) * 16], scalar1=w8[0:8], scalar2=None, op0=ALU.mult),
                  reads=["E16", "w8"], writes=["Wsel"])
            P.add("pe", lambda e, n=n: e.matmul(psf(3, 0, 512, 0, 16), lhsT=Wsel[0:8], rhs=OD[0:8], start=(n == 0), stop=(n == 15)),
                  reads=["Wsel", "OD"], writes=["ps3"])
        MD4 = MD[0:16].rearrange("p (h c) -> p h c", h=4)
        P.add("act", lambda e: e.activation(out=sq16[0:16], in_=psf(3, 0, 512, 0, 16), func=AF.Square), reads=["ps3"], writes=["sq16"])
        P.add("dve", lambda e: e.tensor_reduce(out=smb[0:16, 8:12], in_=sq16[0:16].rearrange("p (h d) -> p h d", h=4), axis=AX.X, op=ALU.add),
              reads=["sq16"], writes=["ms4"])
        P.add("act", lambda e: e.activation(out=smb[0:16, 12:16], in_=smb[0:16, 8:12], func=AF.Ln, bias=epsb[0:16], scale=1.0 / 128),
              reads=["ms4", "epsb"], writes=["l4"])
        P.add("act", lambda e: e.activation(out=smb[0:16, 16:20], in_=smb[0:16, 12:16], func=AF.Exp, scale=-0.5), reads=["l4"], writes=["r4"])
        P.add("dve", lambda e: e.tensor_tensor(out=a16[0:16].rearrange("p (h d) -> p h d", h=4),
                                               in0=psf(3, 0, 512, 0, 16).rearrange("p (h d) -> p h d", h=4),
                                               in1=smb[0:16, 16:20].unsqueeze(2).to_broadcast([16, 4, 128]), op=ALU.mult),
              reads=["ps3", "r4"], writes=["a16"])
        P.add("dve", lambda e: e.tensor_tensor(out=MD4[:, :, 0:128], in0=a16[0:16].rearrange("p (h d) -> p h d", h=4),
                                               in1=sub8[0:16].unsqueeze(1).to_broadcast([16, 4, 128]), op=ALU.mult),
              reads=["a16", "sub8"], writes=["MD"])
        P.barrier()
        A.off = b1_base
        rtd = [A.alloc(8 * 32, F32).rearrange("p (a d) -> p a d", a=8) for _ in range(4)]
        rotd = A.alloc(512, F32).rearrange("p (a d) -> p a d", a=8)
        prd = A.alloc(256, F32)
        ord16 = A.alloc(512, F32).rearrange("p (h d) -> p h d", h=4)
        qTd = A.alloc(64, F32)
        QZ = A.alloc(4 * 16 * 16, F32).rearrange("p (h n c) -> p h n c", h=4, n=16)
        Sn = [A.alloc(512, F32).rearrange("p (h d) -> p h d", h=4) for _ in range(2)]
        Snw = [A.alloc(512, F32).rearrange("p (h d) -> p h d", h=4) for _ in range(2)]
        VZn = A.alloc(512, F32)
        sgd = A.alloc(512, F32)
        qkd = zd[0:16, 1536:2048].rearrange("p (a d) -> p a d", a=8)
        cosd = roped[0:16, 0:32].unsqueeze(1).to_broadcast([16, 8, 32])
        sind = roped[0:16, 32:64].unsqueeze(1).to_broadcast([16, 8, 32])
        d1, d2 = qkd[:, :, 0:32], qkd[:, :, 32:64]
        P.add("dve", lambda e: e.tensor_tensor(out=rtd[0][0:16], in0=d1, in1=cosd, op=ALU.mult), reads=["zd", "roped"], writes=["rtd0"])
        P.add("dve", lambda e: e.tensor_tensor(out=rtd[1][0:16], in0=d2, in1=sind, op=ALU.mult), reads=["zd", "roped"], writes=["rtd1"])
        P.add("dve", lambda e: e.tensor_tensor(out=rtd[2][0:16], in0=d2, in1=cosd, op=ALU.mult), reads=["zd", "roped"], writes=["rtd2"])
        P.add("dve", lambda e: e.tensor_tensor(out=rtd[3][0:16], in0=d1, in1=sind, op=ALU.mult), reads=["zd", "roped"], writes=["rtd3"])
        P.add("dve", lambda e: e.tensor_tensor(out=rotd[0:16, :, 0:32], in0=rtd[0][0:16], in1=rtd[1][0:16], op=ALU.subtract),
              reads=["rtd0", "rtd1"], writes=["rotd"])
        P.add("dve", lambda e: e.tensor_tensor(out=rotd[0:16, :, 32:64], in0=rtd[2][0:16], in1=rtd[3][0:16], op=ALU.add),
              reads=["rtd2", "rtd3"], writes=["rotd"])
        P.add("dve", lambda e: e.tensor_scalar(out=rotd[0:16, 4:8, :], in0=rotd[0:16, 4:8, :], scalar1=0.125, scalar2=None, op0=ALU.mult),
              reads=["rotd"], writes=["rotd"])
        P.add("dve", lambda e: e.tensor_tensor(out=prd[0:16].rearrange("p (h d) -> p h d", h=4), in0=rotd[0:16, 0:4, :], in1=rotd[0:16, 4:8, :], op=ALU.mult),
              reads=["rotd"], writes=["prd"])
        P.add("dve", lambda e: e.tensor_reduce(out=smb[0:16, 20:24], in_=prd[0:16].rearrange("p (h d) -> p h d", h=4), axis=AX.X, op=ALU.add),
              reads=["prd"], writes=["qk4"])
        vr4 = zd[0:16, 2048:2560].rearrange("p (h d) -> p h d", h=4)
        P.add("dve", lambda e: e.tensor_tensor(out=ord16[0:16], in0=vr4, in1=smb[0:16, 20:24].unsqueeze(2).to_broadcast([16, 4, 128]), op=ALU.mult),
              reads=["zd", "qk4"], writes=["ord16"])
        for h in range(4):
            P.add("pe", lambda e, h=h: e.transpose(out=psf(0, h * 16, (h + 1) * 16, 0, 64), in_=rotd[0:16, h, :], identity=identf[0:16, 0:16]),
                  reads=["rotd", "identf"], writes=["ps0"])
        P.add("dve", lambda e: e.tensor_copy(out=qTd[0:64], in_=psf(0, 0, 64, 0, 64)), reads=["ps0"], writes=["qTd"])
        qT3 = qTd[0:64].rearrange("p (h n) -> p h n", h=4)
        P.add("dve", lambda e: e.tensor_tensor(out=QZ[0:64], in0=qT3.unsqueeze(3).to_broadcast([64, 4, 16, 16]),
                                               in1=E16[0:64].rearrange("p (n c) -> p n c", n=16).unsqueeze(1).to_broadcast([64, 4, 16, 16]), op=ALU.mult),
              reads=["qTd", "E16"], writes=["QZ"])
        gams = [1.0 - 2.0 ** (-5.0 - h) for h in range(4)]
        for n in range(16):
            i = n % 2
            P.add("sp", lambda e, n=n, i=i: e.dma_start(out=Sn[i][0:64], in_=sst_d[n].rearrange("h d e -> d h e")), writes=["sn%d" % i], slot="sn%d" % i)
            for h in range(4):
                P.add("pe", lambda e, n=n, h=h, i=i: e.matmul(psf(4 + h, 0, 128, 0, 16), lhsT=QZ[0:64, h, n, :], rhs=Sn[i][0:64, h, :],
                                                              start=(n == 0), stop=(n == 15)),
                      reads=["QZ", "sn%d" % i], writes=["ps%d" % (4 + h)])
            P.add("dve", lambda e, n=n: e.tensor_scalar(out=VZn[0:16], in0=zd[0:16, 2048:2560], scalar1=identf[0:16, n:n + 1], scalar2=None, op0=ALU.mult),
                  reads=["zd", "identf"], writes=["VZn"])
            for h in range(4):
                P.add("pe", lambda e, h=h: e.matmul(psf(1, h * 128, (h + 1) * 128, 0, 64), lhsT=rotd[0:16, 4 + h, :], rhs=VZn[0:16, h * 128:(h + 1) * 128],
                                                    start=True, stop=True), reads=["rotd", "VZn"], writes=["ps1"])
            for h in range(4):
                P.add("dve", lambda e, h=h, i=i: e.scalar_tensor_tensor(out=Snw[i][0:64, h, :], in0=Sn[i][0:64, h, :], scalar=gams[h],
                                                                        in1=psf(1, h * 128, (h + 1) * 128, 0, 64), op0=ALU.mult, op1=ALU.add),
                      reads=["sn%d" % i, "ps1"], writes=["snw%d" % i])
            P.add("sp", lambda e, n=n, i=i: e.dma_start(out=ss_o[n].rearrange("h d e -> d h e"), in_=Snw[i][0:64]),
                  reads=["snw%d" % i], writes=[("ss_o", n)], slot="snw%d" % i)
        for h in range(4):
            P.add("dve", lambda e, h=h: e.scalar_tensor_tensor(out=ord16[0:16, h, :], in0=psf(4 + h, 0, 128, 0, 16), scalar=gams[h], in1=ord16[0:16, h, :],
                                                               op0=ALU.mult, op1=ALU.add), reads=["ps%d" % (4 + h), "ord16"], writes=["ord16"])
        o2d = ord16[0:16].rearrange("p h d -> p (h d)")
        P.add("act", lambda e: e.activation(out=sq16[0:16], in_=o2d, func=AF.Square), reads=["ord16"], writes=["sq16"])
        P.add("dve", lambda e: e.tensor_reduce(out=smb[0:16, 8:12], in_=sq16[0:16].rearrange("p (h d) -> p h d", h=4), axis=AX.X, op=ALU.add),
              reads=["sq16"], writes=["ms4"])
        P.add("act", lambda e: e.activation(out=smb[0:16, 12:16], in_=smb[0:16, 8:12], func=AF.Ln, bias=epsb[0:16], scale=1.0 / 128),
              reads=["ms4", "epsb"], writes=["l4"])
        P.add("act", lambda e: e.activation(out=smb[0:16, 16:20], in_=smb[0:16, 12:16], func=AF.Exp, scale=-0.5), reads=["l4"], writes=["r4"])
        grd = zd[0:16, 2560:3072]
        P.add("act", lambda e: e.activation(out=sgd[0:16], in_=grd, func=AF.Exp, scale=-1.0), reads=["zd"], writes=["sgd"])
        P.add("dve", lambda e: e.tensor_scalar(out=sgd[0:16], in0=sgd[0:16], scalar1=1.0, scalar2=None, op0=ALU.add), reads=["sgd"], writes=["sgd"])
        P.add("dve", lambda e: e.reciprocal(out=sgd[0:16], in_=sgd[0:16]), reads=["sgd"], writes=["sgd"])
        P.add("dve", lambda e: e.tensor_tensor(out=sgd[0:16], in0=sgd[0:16], in1=grd, op=ALU.mult), reads=["sgd", "zd"], writes=["sgd"])
        P.add("dve", lambda e: e.tensor_tensor(out=a16[0:16].rearrange("p (h d) -> p h d", h=4), in0=ord16[0:16],
                                               in1=smb[0:16, 16:20].unsqueeze(2).to_broadcast([16, 4, 128]), op=ALU.mult),
              reads=["ord16", "r4"], writes=["a16"])
        P.add("dve", lambda e: e.tensor_tensor(out=MD4[:, :, 128:256], in0=a16[0:16].rearrange("p (h d) -> p h d", h=4),
                                               in1=sgd[0:16].rearrange("p (h d) -> p h d", h=4), op=ALU.mult),
              reads=["a16", "sgd"], writes=["MD"])

    if not do_C:
        P.add("sp", None, reads=[("kout", t) for t in range(nbA)] + [("vout", t) for t in range(nbA)] + ["sret"] + [("Ex", t) for t in range(nbA)] + (["ks_o", "vs_o"] + [("ss_o", n) for n in range(16)] if do_B else []))
        P.emit(es, limit)
        es.close()
        return nc, dict(P.stats)
    P.barrier()
    A.off = phase_base
    NBLK = nblkC
    wstage_all = A.alloc(8 * 256, F32)
    wstage = [wstage_all[:, 0:1024], wstage_all[:, 1024:2048], 128]
    gains = A.alloc(24, F32)
    idxt = A.alloc(NOWN * 4, I32)
    X2 = A.alloc(NBLK * D, F32).rearrange("p (b n) -> p b n", b=NBLK)
    H2T = A.alloc(8 * (NBLK + 1) * 128, BF16).rearrange("p (k t) -> p k t", k=8)
    cT = A.alloc(D, BF16).rearrange("p (k t) -> p k t", k=8)
    hbf = A.alloc(D, BF16)
    smc = A.alloc(16, F32)
    sqc = hbf
    qmb = A.alloc(D, F32)
    sweep_base = A.off
    Wout = A.alloc(8 * D, BF16).rearrange("p (k n) -> p k n", k=8)
    Wmq = A.alloc(8 * D, BF16).rearrange("p (k n) -> p k n", k=8)
    Wmo = A.alloc(8 * D, BF16).rearrange("p (k n) -> p k n", k=8)
    mkT = A.alloc(8 * 256, BF16).rearrange("p (c m) -> p c m", c=8)
    mvx = A.alloc(2 * 4 * 258, BF16).rearrange("p (a h c) -> p a h c", a=2, h=4)
    memT = A.alloc(8 * 256, BF16).rearrange("p (k t) -> p k t", k=8)
    regR = A.off
    memx = A.alloc(2 * D, F32).rearrange("p (a n) -> p a n", a=2)
    memo = A.alloc(2 * 512, F32).rearrange("p (a n) -> p a n", a=2)
    wtmp = A.alloc(8 * 512, BF16).rearrange("p (k n) -> p k n", k=8)
    peakC1 = A.off

    P.add("sp", lambda e: e.dma_start(out=gains, in_=gains_d), writes=["gains"], slot="gains")
    P.add("sp", lambda e: e.dma_start(out=idxt, in_=idx_d), writes=["idxt"], slot="idxt")
    gmq, gkv, gmlp = gains[:, 0:8], gains[:, 8:16], gains[:, 16:24]
    load_weight(Wout, "Wout", wout_d, 8, D, wstage)
    load_weight(Wmq, "Wmq", wmq_d, 8, D, wstage, gain=gmq, gain_res="gains")
    load_weight(Wmo, "Wmo", wmo_d, 8, D, wstage)
    P.add("pool", lambda e: e.memset(mvx[:, :, :, 256:258], 1.0), writes=["mvxones"])

    def transposes_to(dst, dst_res, src, src_res, bank, nk=8, evac="act"):
        for k in range(nk):
            P.add("pe", lambda e, k=k: e.transpose(out=psb(bank)[:, k * 128:(k + 1) * 128], in_=src[:, k * 128:(k + 1) * 128],
                                                   identity=ident), reads=(src_res if isinstance(src_res, list) else [src_res]) + ["ident"], writes=["ps%d" % bank])
        if evac == "act":
            P.add("act", lambda e: e.activation(out=dst, in_=psb(bank)[:, 0:nk * 128].rearrange("p (k t) -> p k t", k=nk), func=AF.Copy),
                  reads=["ps%d" % bank], writes=[dst_res])
        else:
            P.add("dve", lambda e: e.tensor_copy(out=dst, in_=psb(bank)[:, 0:nk * 128].rearrange("p (k t) -> p k t", k=nk)),
                  reads=["ps%d" % bank], writes=[dst_res])

    P.add("sp", lambda e: e.dma_start(out=memx, in_=memp_d.rearrange("(a p) n -> p a n", p=128)), writes=["memx"], slot="memx")
    for a in range(2):
        rstd_ops(memx[:, a, :], "memx", D, sqc, "hbf", smc[:, 0:1], smc[:, 1:2], smc[:, 2:3], "m_")
        P.add("dve", lambda e, a=a: e.tensor_scalar(out=hbf, in0=memx[:, a, :], scalar1=smc[:, 2:3], scalar2=None, op0=ALU.mult),
              reads=["memx", "m_rstd"], writes=["hbf"])
        transposes_to(memT[:, :, a * 128:(a + 1) * 128], "memT", hbf, "hbf", 0)
    for which, wsrc, outd in (("k", wmk_d, memk_o), ("v", wmv_d, memv_o)):
        for half in range(2):
            load_weight(wtmp, "wtmp", wsrc[:, half * 512:(half + 1) * 512], 8, 512, wstage, gain=gkv, gain_res="gains")
            for a in range(2):
                for k in range(8):
                    P.add("pe", lambda e, a=a, k=k: e.matmul(psf(1 + a), lhsT=memT[:, k, a * 128:(a + 1) * 128], rhs=wtmp[:, k, :],
                                                             start=(k == 0), stop=(k == 7)),
                          reads=["memT", "wtmp"], writes=["ps%d" % (1 + a)])
                P.add("dve", lambda e, a=a: e.tensor_copy(out=memo[:, a, :], in_=psf(1 + a)),
                      reads=["ps%d" % (1 + a)], writes=["memo"])
                if which == "v":
                    P.add("act", lambda e, a=a, half=half: e.activation(
                        out=mvx[:, a, half * 2:half * 2 + 2, 0:256], in_=psf(1 + a).rearrange("p (h c) -> p h c", h=2), func=AF.Copy),
                        reads=["ps%d" % (1 + a)], writes=["mvx"])
            if which == "k":
                for ct in range(4):
                    for k in range(8):
                        P.add("pe", lambda e, ct=ct, k=k: e.matmul(psf(3, 0, 256), lhsT=wtmp[:, k, ct * 128:(ct + 1) * 128], rhs=memT[:, k, :],
                                                                   start=(k == 0), stop=(k == 7)),
                              reads=["memT", "wtmp"], writes=["ps3"])
                    P.add("act", lambda e, ct=ct, half=half: e.activation(out=mkT[:, half * 4 + ct, :], in_=psf(3, 0, 256), func=AF.Copy),
                          reads=["ps3"], writes=["mkT"])
            P.add("sp", lambda e, outd=outd, half=half: e.dma_start(
                out=outd.rearrange("(a p) n -> p a n", p=128)[:, :, half * 512:(half + 1) * 512], in_=memo),
                reads=["memo"], writes=[("memout", which, half)], slot="memo")

    def rms_to_T(x_ap, x_res, dst, dst_res, tag):
        rstd_ops(x_ap, x_res, D, sqc, "hbf", smc[:, 3:4], smc[:, 4:5], smc[:, 5:6], tag)
        P.add("dve", lambda e: e.tensor_scalar(out=hbf, in0=x_ap, scalar1=smc[:, 5:6], scalar2=None, op0=ALU.mult),
              reads=[x_res, tag + "rstd"], writes=["hbf"])
        transposes_to(dst, dst_res, hbf, "hbf", 0)

    def proj_add(x_ap, x_res, lT, lT_res, W, W_res):
        for nt in range(2):
            for k in range(8):
                P.add("pe", lambda e, nt=nt, k=k: e.matmul(psf(1 + nt), lhsT=lT[:, k, :], rhs=W[:, k, nt * 512:(nt + 1) * 512],
                                                           start=(k == 0), stop=(k == 7)),
                      reads=[lT_res, W_res], writes=["ps%d" % (1 + nt)])
            P.add("dve", lambda e, nt=nt: e.tensor_tensor(out=x_ap[:, nt * 512:(nt + 1) * 512], in0=x_ap[:, nt * 512:(nt + 1) * 512],
                                                          in1=psf(1 + nt), op=ALU.add),
                  reads=[x_res, "ps%d" % (1 + nt)], writes=[x_res])

    P.barrier()
    A.off = regR
    MGall = A.alloc(8 * 256, BF16)
    MG = [MGall[:, 0:1024], MGall[:, 1024:2048]]
    qmd = MGall.bitcast(F32)
    qmT = A.alloc(D, BF16).rearrange("p (k t) -> p k t", k=8)
    PTm = A.alloc(256, BF16)
    om = A.alloc(D, BF16)
    peakC1 = max(peakC1, A.off)
    mkt = wstage_all[:, 0:1024]
    mvt = wstage_all[:, 1024:2048]

    def mem_attend_prompt():
        for ct in range(8):
            bank = 5 + ct // 4
            for k in range(8):
                P.add("pe", lambda e, ct=ct, k=k, bank=bank: e.matmul(psf(bank, (ct % 4) * 128, (ct % 4 + 1) * 128),
                                                                     lhsT=Wmq[:, k, ct * 128:(ct + 1) * 128], rhs=cT[:, k, :],
                                                                     start=(k == 0), stop=(k == 7)),
                      reads=["cT", "Wmq"], writes=["ps%d" % bank])
        for hb in range(2):
            P.add("act", lambda e, hb=hb: e.activation(out=qmT[:, hb * 4:hb * 4 + 4, :], in_=psf(5 + hb).rearrange("p (c t) -> p c t", c=4), func=AF.Copy),
                  reads=["ps%d" % (5 + hb)], writes=["qmT"])
        for hm in range(4):
            for mc in range(2):
                for c in range(2):
                    P.add("pe", lambda e, hm=hm, mc=mc, c=c: e.matmul(psf(3, mc * 128, (mc + 1) * 128),
                                                                     lhsT=mkT[:, hm * 2 + c, mc * 128:(mc + 1) * 128], rhs=qmT[:, hm * 2 + c, :],
                                                                     start=(c == 0), stop=(c == 1)),
                          reads=["mkT", "qmT"], writes=["ps3"])
            P.add("act", lambda e: e.activation(out=PTm, in_=psf(3, 0, 256), func=AF.Exp, scale=1.0 / 16.0), reads=["ps3"], writes=["PTm"])
            for mc in range(2):
                P.add("pe", lambda e, hm=hm, mc=mc: e.matmul(psf(4, 0, 257), lhsT=PTm[:, mc * 128:(mc + 1) * 128], rhs=mvx[:, mc, hm, 0:257],
                                                             start=(mc == 0), stop=(mc == 1)),
                      reads=["PTm", "mvx", "mvxones"], writes=["ps4"])
            P.add("dve", lambda e: e.reciprocal(out=smc[:, 6:7], in_=psf(4, 256, 257)), reads=["ps4"], writes=["rm"])
            P.add("dve", lambda e, hm=hm: e.tensor_scalar(out=om[:, hm * 256:(hm + 1) * 256], in0=psf(4, 0, 256), scalar1=smc[:, 6:7],
                                                          scalar2=None, op0=ALU.mult), reads=["ps4", "rm"], writes=["om"])

    def mem_attend_sample():
        mgres = [("mg%d" % i_, r_) for i_ in range(2) for r_ in range(4)]
        for nt in range(2):
            for k in range(8):
                P.add("pe", lambda e, nt=nt, k=k: e.matmul(psf(4 + nt, 0, 512, 0, 16), lhsT=cT[:, k, 0:16], rhs=Wmq[:, k, nt * 512:(nt + 1) * 512],
                                                           start=(k == 0), stop=(k == 7)), reads=["cT", "Wmq"], writes=["ps%d" % (4 + nt)])
            P.add("act", lambda e, nt=nt: e.activation(out=qmd[0:16, nt * 512:(nt + 1) * 512], in_=psf(4 + nt, 0, 512, 0, 16), func=AF.Copy, scale=1.0 / 16.0),
                  reads=["ps%d" % (4 + nt)], writes=mgres)
        P.add("pool", lambda e: e.memset(om, 0.0), writes=["om"])
        Sm = hid_s
        wf = Wout.rearrange("p k n -> p (k n)").bitcast(F32)
        mkts = [wf[:, 0:1024], wf[:, 1024:2048]]
        mvts = [wf[:, 2048:3072], wf[:, 3072:4096]]
        P.add("sp", None, writes=["Wout", "mkt0", "mkt1", "mvt0", "mvt1"])
        for n in range(16):
            P.add("dve", lambda e, n=n: e.tensor_copy(out=selq[0:16], in_=identf[0:16, n:n + 1].to_broadcast([16, 128])),
                  reads=["identf"], writes=["selq"])
            for nt in range(2):
                P.add("pe", lambda e, nt=nt: e.matmul(psf(4 + nt), lhsT=selq[0:16], rhs=qmd[0:16, nt * 512:(nt + 1) * 512], start=True, stop=True),
                      reads=["selq"] + mgres, writes=["ps%d" % (4 + nt)])
                P.add("act", lambda e, nt=nt: e.activation(out=qmb[:, nt * 512:(nt + 1) * 512], in_=psf(4 + nt), func=AF.Copy),
                      reads=["ps%d" % (4 + nt)], writes=["qmb"])
            for mc in range(2):
                mkt = mkts[mc]
                mkr = "mkt%d" % mc
                P.add("sp", lambda e, n=n, mc=mc, mkt=mkt: e.dma_start(out=mkt, in_=cmk_d[n, mc * 128:(mc + 1) * 128, :]), writes=[mkr], slot=mkr)
                P.add("sp", lambda e, n=n, mc=mc: e.dma_start(out=mvts[mc], in_=cmv_d[n, mc * 128:(mc + 1) * 128, :]), writes=["mvt%d" % mc], slot="mvt%d" % mc)
                P.add("pool", lambda e, mkt=mkt: e.tensor_tensor(out=mkt, in0=mkt, in1=qmb, op=ALU.mult), reads=["qmb"], writes=[mkr])
                P.add("dve", lambda e, mc=mc, mkt=mkt: e.tensor_reduce(out=Sm[:, mc * 4:(mc + 1) * 4], in_=mkt.rearrange("p (h d) -> p h d", h=4), axis=AX.X, op=ALU.add),
                      reads=[mkr], writes=["Sm"])
            P.add("act", lambda e: e.activation(out=Sm[:, 8:16], in_=Sm[:, 0:8], func=AF.Exp), reads=["Sm"], writes=["Pm"])
            for mc in range(2):
                for nt in range(2):
                    P.add("pe", lambda e, mc=mc, nt=nt: e.matmul(psf(1 + nt, 0, 512, 0, 4), lhsT=Sm[:, 8 + mc * 4:8 + (mc + 1) * 4], rhs=mvts[mc][:, nt * 512:(nt + 1) * 512],
                                                                start=(mc == 0), stop=(mc == 1)), reads=["Pm", "mvt%d" % mc], writes=["ps%d" % (1 + nt)])
                P.add("pe", lambda e, mc=mc: e.matmul(psf(3, 0, 1, 0, 4), lhsT=Sm[:, 8 + mc * 4:8 + (mc + 1) * 4], rhs=onesf[:, 0:1], start=(mc == 0), stop=(mc == 1)),
                      reads=["Pm", "onesf"], writes=["ps3"])
            w4 = smc[:, 13:14]
            P.add("dve", lambda e: e.reciprocal(out=w4[0:4], in_=psf(3, 0, 1, 0, 4)), reads=["ps3"], writes=["w4"])
            dm4 = identf[0:4, 0:4].unsqueeze(2).to_broadcast([4, 4, 256])
            for nt in range(2):
                P.add("dve", lambda e, nt=nt: e.tensor_tensor(out=OD4[0:4, nt * 512:(nt + 1) * 512].rearrange("p (h d) -> p h d", h=2),
                                                              in0=psf(1 + nt, 0, 512, 0, 4).rearrange("p (h d) -> p h d", h=2),
                                                              in1=identf[0:4, 2 * nt:2 * nt + 2].unsqueeze(2).to_broadcast([4, 2, 256]), op=ALU.mult),
                      reads=["ps%d" % (1 + nt), "identf"], writes=["OD4"])
            P.add("dve", lambda e, n=n: e.tensor_scalar(out=Wsel4[0:4], in0=E16[0:4, n * 16:(n + 1) * 16], scalar1=w4[0:4], scalar2=None, op0=ALU.mult),
                  reads=["E16", "w4"], writes=["Wsel4"])
            for nt in range(2):
                P.add("pe", lambda e, n=n, nt=nt: e.matmul(psf(6 + nt, 0, 512, 0, 16), lhsT=Wsel4[0:4], rhs=OD4[0:4, nt * 512:(nt + 1) * 512],
                                                           start=(n == 0), stop=(n == 15)), reads=["Wsel4", "OD4"], writes=["ps%d" % (6 + nt)])
        for nt in range(2):
            P.add("dve", lambda e, nt=nt: e.tensor_copy(out=om[0:16, nt * 512:(nt + 1) * 512], in_=psf(6 + nt, 0, 512, 0, 16)),
                  reads=["ps%d" % (6 + nt)], writes=["om"])

    def sweep1(blk, x, xr, mg, mgres, sample):
        transposes_to(cT, "cT", mg, mgres, 0)
        proj_add(x, xr, cT, "cT", Wout, "Wout")
        rms_to_T(x, xr, cT, "cT", "q_")
        if sample:
            mem_attend_sample()
        else:
            mem_attend_prompt()
        transposes_to(cT, "cT", om, "om", 0)
        proj_add(x, xr, cT, "cT", Wmo, "Wmo")
        rms_to_T(x, xr, H2T[:, :, blk * 128:(blk + 1) * 128], ("H2T", blk), "h_")

    for blk in range(NBLK):
        i = blk % 2
        x = X2[:, blk, :]
        xr = ("X2", blk)
        mg = MG[i]
        mgr = "mg%d" % i
        P.add("sp", lambda e, blk=blk, x=x: e.dma_start(out=x, in_=xown[blk * 128:(blk + 1) * 128, :]), writes=[xr], slot="x2ld%d" % i)
        for r in range(4):
            P.add("pool", lambda e, r=r, blk=blk, mg=mg: e.indirect_dma_start(
                out=mg[:, r * 256:(r + 1) * 256], out_offset=None, in_=Gx.ap(),
                in_offset=bass.IndirectOffsetOnAxis(ap=idxt[:, blk * 4 + r:blk * 4 + r + 1], axis=0)),
                reads=[("Gx", q_) for q_ in range(4)] + ["idxt"], writes=[(mgr, r)], slot="%s_%d" % (mgr, r))
        sweep1(blk, x, xr, mg, [(mgr, r) for r in range(4)], False)
    if do_B:
        sweep1(NBLK, XS, "XS", MD, ["MD"], True)

    P.barrier()
    A.off = sweep_base
    Wu = [A.alloc(8 * 512, BF16).rearrange("p (k n) -> p k n", k=8) for _ in range(2)]
    Wd = [A.alloc(4 * 1024, BF16).rearrange("p (k n) -> p k n", k=4) for _ in range(2)]
    hid = A.alloc(4 * 512, BF16).rearrange("p (f t) -> p f t", f=4)
    sqh = [A.alloc(512, F32) for _ in range(2)]
    yo = [A.alloc(D, F32) for _ in range(2)]
    nfin = A.alloc(D, F32)
    P.add("sp", lambda e: e.dma_start(out=nfin, in_=nfin_d.partition_broadcast(128)), writes=["nfin"], slot="nfin")
    NT4 = NBLK // 4

    def load_eighth(q8):
        i8 = q8 % 2
        load_weight(Wu[i8], "Wup%d" % i8, wup_d[:, q8 * 512:(q8 + 1) * 512], 8, 512, wstage, gain=gmlp, gain_res="gains")
        load_weight(Wd[i8], "Wdn%d" % i8, wdn_d[q8 * 512:(q8 + 1) * 512, :], 4, 1024, wstage)

    def mlp_tile(i8, ncols, rhs_of_k, h2res, xs):
        wur, wdr = "Wup%d" % i8, "Wdn%d" % i8
        for fc in range(4):
            bank = (5, 6, 7, 0)[fc]
            for k in range(8):
                P.add("pe", lambda e, fc=fc, k=k, bank=bank: e.matmul(psf(bank, 0, ncols), lhsT=Wu[i8][:, k, fc * 128:(fc + 1) * 128],
                                                                  rhs=rhs_of_k(k), start=(k == 0), stop=(k == 7)),
                      reads=h2res + [wur], writes=["ps%d" % bank])
            sq = sqh[fc % 2]
            sqr = "sqh%d" % (fc % 2)
            P.add("act", lambda e, bank=bank, sq=sq: e.activation(out=sq[:, 0:ncols], in_=psf(bank, 0, ncols), func=AF.Square),
                  reads=["ps%d" % bank], writes=[sqr])
            P.add("dve", lambda e, bank=bank, sq=sq, fc=fc: e.scalar_tensor_tensor(out=hid[:, fc, 0:ncols], in0=psf(bank, 0, ncols), scalar=0.0,
                                                                               in1=sq[:, 0:ncols], op0=ALU.is_gt, op1=ALU.mult),
                  reads=["ps%d" % bank, sqr], writes=["hid"])
        for bb, (x, xr) in enumerate(xs):
            for nt in range(2):
                ob = 1 + nt + 2 * (bb % 2)
                for fc in range(4):
                    P.add("pe", lambda e, nt=nt, fc=fc, bb=bb, ob=ob: e.matmul(psf(ob), lhsT=hid[:, fc, bb * 128:(bb + 1) * 128],
                                                                            rhs=Wd[i8][:, fc, nt * 512:(nt + 1) * 512], start=(fc == 0), stop=(fc == 3)),
                          reads=["hid", wdr], writes=["ps%d" % ob])
                P.add("dve", lambda e, nt=nt, x=x, ob=ob: e.tensor_tensor(out=x[:, nt * 512:(nt + 1) * 512], in0=x[:, nt * 512:(nt + 1) * 512],
                                                                         in1=psf(ob), op=ALU.add),
                      reads=[xr, "ps%d" % ob], writes=[xr])

    load_eighth(0)
    for q8 in range(8):
        if q8 + 1 < 8:
            load_eighth(q8 + 1)
        i8 = q8 % 2
        for tt in range(NT4):
            mlp_tile(i8, 512, (lambda k, tt=tt: H2T[:, k, tt * 512:(tt + 1) * 512]), [("H2T", tt * 4 + bb) for bb in range(4)],
                     [(X2[:, tt * 4 + bb, :], ("X2", tt * 4 + bb)) for bb in range(4)])
        if do_B:
            mlp_tile(i8, 128, (lambda k: H2T[:, k, NBLK * 128:(NBLK + 1) * 128]), [("H2T", NBLK)], [(XS, "XS")])
    for blk in range(NBLK + (1 if do_B else 0)):
        i = blk % 2
        x = X2[:, blk, :] if blk < NBLK else XS
        xr = ("X2", blk) if blk < NBLK else "XS"
        rstd_ops(x, xr, D, sqc, "hbf", smc[:, 7:8], smc[:, 8:9], smc[:, 9:10], "f_")
        P.add("dve", lambda e, x=x, i=i: e.scalar_tensor_tensor(out=yo[i], in0=x, scalar=smc[:, 9:10], in1=nfin, op0=ALU.mult, op1=ALU.mult),
              reads=[xr, "f_rstd", "nfin"], writes=["yo%d" % i])
        if blk < NBLK:
            P.add("sp", lambda e, blk=blk, i=i: e.dma_start(out=yown[blk * 128:(blk + 1) * 128, :], in_=yo[i]),
                  reads=["yo%d" % i], writes=[("yown", blk)], slot="yo%d" % i)
        else:
            P.add("sp", lambda e, i=i: e.dma_start(out=ys_o, in_=yo[i][0:16, :]), reads=["yo%d" % i], writes=["ys_o"], slot="yo%d" % i)

    P.add("sp", None, reads=[("yown", blk) for blk in range(NBLK)] + [("kout", t) for t in range(NB)]
          + [("vout", t) for t in range(NB)] + ["sret"] + [("memout", w, hf) for w in "kv" for hf in range(2)]
          + (["ys_o", "ks_o", "vs_o"] + [("ss_o", n) for n in range(16)] if do_B else []))
    P.emit(es, limit)
    es.close()
    return nc, dict(P.stats, peakA=peakA, peakC1=peakC1, peakC2=A.off)


def _t5_bucket_np(n):
    n = np.maximum(n, 0)
    nf = np.maximum(n, 1).astype(np.float32)
    large = 16 + (np.log(nf / np.float32(16)) / np.float32(math.log(128 / 16)) * np.float32(16)).astype(np.int32)
    large = np.minimum(large, 31)
    return np.where(n < 16, n, large)


def _rope_table():
    inv = (np.float32(10000.0) ** (-np.linspace(0.0, 1.0, 32, dtype=np.float32))).astype(np.float32)
    pos = np.arange(T, dtype=np.float32)
    ang = (pos[:, None] * inv[None, :]).astype(np.float32)
    tab = np.concatenate([np.cos(ang), np.sin(ang)], axis=1).astype(np.float32)
    return tab.reshape(NB, 128, 64)


_PROG = {}


def _get_prog(n_phys):
    if n_phys not in _PROG:
        _PROG[n_phys] = build_program(n_phys=n_phys)
    return _PROG[n_phys]


def kernel(x_prompt, x_sample, mem_prompt, cache_k, cache_v, page_table, state_ret,
           cache_mem_k, cache_mem_v, rel_bias, norm_mix, w_in, lambda_q1, lambda_k1,
           lambda_q2, lambda_k2, subln_a, w_out, norm_mem_q, norm_mem_kv, w_mq, w_mk,
           w_mv, w_mo, norm_mlp, w_up, w_down, norm_final, _compact_pools=False):
    f32 = np.float32
    x_prompt = np.asarray(x_prompt, f32)
    w_in0 = np.asarray(w_in, f32)[0]
    rel_bias = np.asarray(rel_bias, f32)
    cache_k = np.asarray(cache_k, f32)
    cache_v = np.asarray(cache_v, f32)
    page_table = np.asarray(page_table, np.int32)
    n_phys = cache_k.shape[1]
    if _compact_pools:
        n_phys = 256
    nc, stats = _get_prog(n_phys)
    ck_full = cache_k[0].reshape(-1, 512)
    cv_full = cache_v[0].reshape(-1, 512)
    x_sample = np.asarray(x_sample, f32)
    state_ret = np.asarray(state_ret, f32)
    cache_mem_k = np.asarray(cache_mem_k, f32)
    cache_mem_v = np.asarray(cache_mem_v, f32)
    e16 = np.ascontiguousarray(np.tile(np.eye(16, dtype=f32).reshape(1, 256), (64, 1)))
    r84 = np.zeros((8, 8), f32)
    for hh in range(4):
        for mm in range(2):
            r84[hh * 2 + mm, hh] = 1.0
            r84[hh * 2 + mm, 4 + mm] = 1.0
    inv = (np.float32(10000.0) ** (-np.linspace(0.0, 1.0, 32, dtype=np.float32))).astype(f32)
    angd = (np.float32(PAST) * inv).astype(f32)
    roped = np.ascontiguousarray(np.tile(np.concatenate([np.cos(angd), np.sin(angd)]).astype(f32)[None, :], (16, 1)))
    pp = np.arange(128)
    bconst_base = np.zeros((128, 72), f32)
    bconst_base[:, 0] = pp
    reld = PAST - (np.arange(16)[None, :] * 128 + pp[:, None])
    bkd = _t5_bucket_np(reld)

    rope = _rope_table()
    p = np.arange(128)
    maskT = (p[None, :] >= p[:, None]).astype(f32)
    ident = np.eye(128, dtype=f32)
    lamv = np.concatenate([np.asarray(a, f32)[0] for a in (lambda_q1, lambda_k1, lambda_q2, lambda_k2)])[None, :]
    subln = np.asarray(subln_a, f32)[0][None, :]
    wout_full = np.asarray(w_out, f32)[0]
    perm = np.concatenate([np.concatenate([np.arange(r * 128, (r + 1) * 128), 512 + np.arange(r * 128, (r + 1) * 128)]) for r in range(4)])
    wout_p = np.ascontiguousarray(wout_full[perm])

    def g8(v):
        return np.ascontiguousarray(np.asarray(v, f32).reshape(8, 128).T)

    gains = np.concatenate([g8(norm_mem_q[0]), g8(norm_mem_kv[0]), g8(norm_mlp[0])], axis=1)
    rel_d = p[None, :] - p[:, None]
    bk_sub = _t5_bucket_np(128 + rel_d)
    bk_dia = _t5_bucket_np(rel_d)

    in_maps = []
    for c in range(NCORES):
        b, h = c // 4, c % 4
        j = h
        cols = np.concatenate([
            np.arange(h * 128, (h + 1) * 128),
            512 + np.arange(h * 128, (h + 1) * 128),
            1024 + np.arange(h * 128, (h + 1) * 128),
            2048 + np.arange(h * 128, (h + 1) * 128),
            2560 + np.arange(h * 128, (h + 1) * 128),
            1536 + np.arange(h * 64, (h + 1) * 64),
            1792 + np.arange(h * 64, (h + 1) * 64),
        ])
        gam = 1.0 - 2.0 ** (-5.0 - h)
        hconst = np.zeros((128, 8), f32)
        hconst[:, 0] = gam ** (p + 1.0)
        hconst[:, 1] = (64.0 ** -0.5) * gam ** (-(p + 1.0))
        hconst[:, 2] = gam ** 128.0
        hconst[:, 3] = rel_bias[31, h]
        bt = np.empty((128, 2, 128), f32)
        bt[:, 0, :] = rel_bias[bk_sub, h]
        bt[:, 1, :] = np.where(rel_d >= 0, rel_bias[bk_dia, h], f32(NEG))
        idx = np.empty((128, NOWN * 4), np.int32)
        for blk in range(NOWN):
            for r in range(4):
                idx[:, blk * 4 + r] = j * 8192 + r * 2048 + blk * 128 + p
        bconst = bconst_base.copy()
        bd = np.full((128, 17, 4), f32(NEG), f32)
        bd[:, 0:16, :] = rel_bias[bkd, :]
        bd[0, 16, :] = rel_bias[0, :]
        bconst[:, 1:69] = bd.reshape(128, 68)
        ptc = page_table[c * 16:(c + 1) * 16]
        if _compact_pools:
            uniq, invp = np.unique(ptc.reshape(-1), return_inverse=True)
            ck_c = np.zeros((256 * 128, 512), f32)
            cv_c = np.zeros((256 * 128, 512), f32)
            ck_c[:len(uniq) * 128] = cache_k[0][uniq].reshape(-1, 512)
            cv_c[:len(uniq) * 128] = cache_v[0][uniq].reshape(-1, 512)
            ptc = invp.reshape(16, 16).astype(np.int32)
        else:
            ck_c, cv_c = ck_full, cv_full
        in_maps.append({
            "xd": np.ascontiguousarray(x_sample[c * 16:(c + 1) * 16, 0, :]),
            "win": w_in0,
            "ck": ck_c,
            "cv": cv_c,
            "pt": np.ascontiguousarray(ptc.reshape(1, 256)),
            "sst": np.ascontiguousarray(state_ret[0, c * 16:(c + 1) * 16]),
            "cmk": np.ascontiguousarray(cache_mem_k[0, c * 16:(c + 1) * 16].reshape(16, 256, D)),
            "cmv": np.ascontiguousarray(cache_mem_v[0, c * 16:(c + 1) * 16].reshape(16, 256, D)),
            "bconst": bconst,
            "e16": e16,
            "r84": r84,
            "roped": roped,
            "xb": x_prompt[b],
            "wA": np.ascontiguousarray(w_in0[:, cols]),
            "gmix": g8(norm_mix[0]),
            "rope": rope,
            "hconst": hconst,
            "bt": bt.reshape(128, 256),
            "maskT": maskT,
            "ident": ident,
            "lamv": lamv,
            "subln": subln,
            "xown": np.ascontiguousarray(x_prompt[b, j * 2048:(j + 1) * 2048]),
            "idx": idx,
            "wout": wout_p,
            "wmq": np.asarray(w_mq, f32)[0],
            "wmk": np.asarray(w_mk, f32)[0],
            "wmv": np.asarray(w_mv, f32)[0],
            "wmo": np.asarray(w_mo, f32)[0],
            "wup": np.asarray(w_up, f32)[0],
            "wdn": np.asarray(w_down, f32)[0],
            "gains": gains,
            "nfin": np.asarray(norm_final, f32)[None, :],
            "memp": np.asarray(mem_prompt, f32)[b],
        })
    res = run_bass_kernel_spmd(nc, in_maps, core_ids=list(range(NCORES)))
    R = res.results

    y_prompt = np.empty((2, T, D), f32)
    k_prompt = np.empty((1, 2, NB, 128, 4, 128), f32)
    v_prompt = np.empty((1, 2, NB, 128, 4, 128), f32)
    state_ret_prompt = np.empty((1, 2, 4, 64, 128), f32)
    mem_k_prompt = np.empty((1, 2, 256, 4, 256), f32)
    mem_v_prompt = np.empty((1, 2, 256, 4, 256), f32)
    for c in range(NCORES):
        b, h = c // 4, c % 4
        y_prompt[b, h * 2048:(h + 1) * 2048] = R[c]["yown"]
        k_prompt[0, b, :, :, h, :] = R[c]["kout"].reshape(NB, 128, 128)
        v_prompt[0, b, :, :, h, :] = R[c]["vout"].reshape(NB, 128, 128)
        state_ret_prompt[0, b, h] = R[c]["sret"]
        if h == 0:
            mem_k_prompt[0, b] = R[c]["memk"].reshape(256, 4, 256)
            mem_v_prompt[0, b] = R[c]["memv"].reshape(256, 4, 256)
    y_sample = np.empty((DEC, 1, D), f32)
    k_sample = np.empty((1, DEC, 1, 4, 128), f32)
    v_sample = np.empty((1, DEC, 1, 4, 128), f32)
    state_ret_sample = np.empty((1, DEC, 4, 64, 128), f32)
    for c in range(NCORES):
        y_sample[c * 16:(c + 1) * 16, 0] = R[c]["ys"]
        k_sample[0, c * 16:(c + 1) * 16, 0] = R[c]["ks"].reshape(16, 4, 128)
        v_sample[0, c * 16:(c + 1) * 16, 0] = R[c]["vs"].reshape(16, 4, 128)
        state_ret_sample[0, c * 16:(c + 1) * 16] = R[c]["ss"]
    return (y_prompt, y_sample, k_prompt, v_prompt, state_ret_prompt, mem_k_prompt, mem_v_prompt,
            k_sample, v_sample, state_ret_sample)
```
